# Optimizing a Trainium2 kernel written in Bass

```python
import jax, jax.numpy as jnp
from jax import lax
import numpy as np

D_MODEL = 1024
BATCH = 8
SEQ = 4096
DEPTH = 1

MEM_LEN = 256
MIX_WIDTH = D_MODEL
A_WIDTH = MIX_WIDTH // 2
B_WIDTH = MIX_WIDTH - A_WIDTH
A_HEADS = 4
A_HEAD_DIM = A_WIDTH // A_HEADS
B_HEADS = 4
CHUNK = 128
CONV_W = 3
IN_A = 2 * A_WIDTH
IN_B = 3 * B_WIDTH
IN_TOTAL = IN_A + IN_B
X_HEADS = 4
X_HEAD_DIM = D_MODEL // X_HEADS
D_FF = ((8 * D_MODEL // 3 + 255) // 256) * 256
EPS = 1e-6

kernel_name = "hybrid_sgu_shortconv_xattn_layer"


def rms_norm(x, g):
    xf = x.astype(jnp.float32)
    y = xf * lax.rsqrt(jnp.mean(xf * xf, axis=-1, keepdims=True) + EPS)
    return (y * g.astype(jnp.float32)).astype(x.dtype)


def layer_norm(x, g, b):
    xf = x.astype(jnp.float32)
    mu = jnp.mean(xf, axis=-1, keepdims=True)
    var = jnp.mean(jnp.square(xf - mu), axis=-1, keepdims=True)
    y = (xf - mu) * lax.rsqrt(var + EPS)
    return (y * g.astype(jnp.float32) + b.astype(jnp.float32)).astype(x.dtype)


def spatial_gating(a, sgu_ln_g, sgu_ln_b, w_spatial, b_spatial):
    bsz, seq, _ = a.shape
    a = jax.nn.gelu(a)
    u, v = jnp.split(a, 2, axis=-1)
    v = layer_norm(v, sgu_ln_g, sgu_ln_b)
    n_chunks = seq // CHUNK
    v = v.reshape(bsz, n_chunks, CHUNK, A_HEADS, A_HEAD_DIM)
    mask = jnp.tril(jnp.ones((CHUNK, CHUNK), dtype=w_spatial.dtype))
    w = w_spatial * mask[None]
    mixed = jnp.einsum("hts,bnshd->bnthd", w, v)
    mixed = mixed + jnp.transpose(b_spatial)[None, None, :, :, None]
    mixed = mixed.reshape(bsz, seq, A_WIDTH)
    return u * mixed


def short_gated_conv(h, conv_w):
    gate_b, gate_c, val = jnp.split(h, 3, axis=-1)
    z = gate_c * val
    zp = jnp.pad(z, ((0, 0), (CONV_W - 1, 0), (0, 0)))
    seq = z.shape[1]
    conv = (conv_w[0] * zp[:, 0:seq] + conv_w[1] * zp[:, 1:seq + 1]
            + conv_w[2] * zp[:, 2:seq + 2])
    return gate_b * conv


def cross_attention(h, memn, w_q, w_kv, w_o):
    bsz, seq, _ = h.shape
    q = (h @ w_q).reshape(bsz, seq, X_HEADS, X_HEAD_DIM)
    k, v = jnp.split(memn @ w_kv, 2, axis=-1)
    k = k.reshape(bsz, MEM_LEN, X_HEADS, X_HEAD_DIM)
    v = v.reshape(bsz, MEM_LEN, X_HEADS, X_HEAD_DIM)
    scale = X_HEAD_DIM ** -0.5
    s = jnp.einsum("bshd,bmhd->bhsm", q, k).astype(jnp.float32) * scale
    p = jax.nn.softmax(s, axis=-1).astype(v.dtype)
    o = jnp.einsum("bhsm,bmhd->bshd", p, v).reshape(bsz, seq, D_MODEL)
    return o @ w_o


def setup_inputs(seed: int = 0) -> dict:
    key = jax.random.key(seed)
    ks = jax.random.split(key, 24)
    f32 = jnp.float32
    nrm = lambda k, shape, scale: jax.random.normal(k, shape, f32) * scale
    gain = lambda k, n: 1.0 + 0.02 * jax.random.normal(k, (n,), f32)
    return {
        "x": jax.random.normal(ks[0], (BATCH, SEQ, D_MODEL), f32),
        "mem": jax.random.normal(ks[1], (BATCH, MEM_LEN, D_MODEL), f32),
        "ln_mix_g": gain(ks[2], D_MODEL),
        "w_in": nrm(ks[3], (D_MODEL, IN_TOTAL), D_MODEL ** -0.5),
        "sgu_ln_g": gain(ks[4], A_WIDTH),
        "sgu_ln_b": nrm(ks[5], (A_WIDTH,), 0.02),
        "w_spatial": nrm(ks[6], (A_HEADS, CHUNK, CHUNK), CHUNK ** -0.5),
        "b_spatial": 1.0 + nrm(ks[7], (A_HEADS, CHUNK), 0.02),
        "conv_w": nrm(ks[8], (CONV_W, B_WIDTH), CONV_W ** -0.5),
        "grp_norm_a": gain(ks[9], A_WIDTH),
        "grp_norm_b": gain(ks[10], B_WIDTH),
        "w_out": nrm(ks[11], (MIX_WIDTH, D_MODEL), MIX_WIDTH ** -0.5),
        "ln_attn_g": gain(ks[12], D_MODEL),
        "ln_mem_g": gain(ks[13], D_MODEL),
        "w_q": nrm(ks[14], (D_MODEL, D_MODEL), D_MODEL ** -0.5),
        "w_kv": nrm(ks[15], (D_MODEL, 2 * D_MODEL), D_MODEL ** -0.5),
        "w_o": nrm(ks[16], (D_MODEL, D_MODEL), D_MODEL ** -0.5),
        "ln_ffn_g": gain(ks[17], D_MODEL),
        "w_gate_up": nrm(ks[18], (D_MODEL, 2 * D_FF), D_MODEL ** -0.5),
        "w_down": nrm(ks[19], (D_FF, D_MODEL), D_FF ** -0.5),
        "ln_final_g": gain(ks[20], D_MODEL),
    }


def reference(x, mem, ln_mix_g, w_in, sgu_ln_g, sgu_ln_b, w_spatial, b_spatial,
              conv_w, grp_norm_a, grp_norm_b, w_out, ln_attn_g, ln_mem_g,
              w_q, w_kv, w_o, ln_ffn_g, w_gate_up, w_down, ln_final_g):
    memn = rms_norm(mem, ln_mem_g)
    for _ in range(DEPTH):
        h = rms_norm(x, ln_mix_g) @ w_in
        h_a = h[..., :IN_A]
        h_b = h[..., IN_A:]
        y_a = rms_norm(spatial_gating(h_a, sgu_ln_g, sgu_ln_b, w_spatial, b_spatial), grp_norm_a)
        y_b = rms_norm(short_gated_conv(h_b, conv_w), grp_norm_b)
        x = x + jnp.concatenate([y_a, y_b], axis=-1) @ w_out
        x = x + cross_attention(rms_norm(x, ln_attn_g), memn, w_q, w_kv, w_o)
        g, u = jnp.split(rms_norm(x, ln_ffn_g) @ w_gate_up, 2, axis=-1)
        x = x + (jax.nn.silu(g) * u) @ w_down
    return rms_norm(x, ln_final_g)
```

```python
import numpy as np
import concourse.bass as bass
import concourse.mybir as mybir
from concourse.bass_utils import run_bass_kernel_spmd

F32 = mybir.dt.float32
BF16 = mybir.dt.bfloat16
U8 = mybir.dt.uint8
AF = mybir.ActivationFunctionType
ALU = mybir.AluOpType

D = 1024
KD = 8
SEQ = 4096
TS = 1024
T = 512
NSUB = TS // T
NST = SEQ // TS
MEM = 256
DFF = 2816
KF = DFF // 128
EPS = 1e-6
NSLOT = 6
SLOT_B = 8192

C_MIX, C_ATT, C_FFN, C_FIN, C_MEM = 0, 8, 16, 24, 32
C_GA, C_GB, C_GLN, C_BLN, C_CW = 40, 44, 48, 52, 56
NCV = 68

_ES = {F32: 4, BF16: 2, U8: 1}


def _esize(dt):
    return _ES[dt]


def _hull(ap):
    es = _esize(ap.dtype)
    pairs = [tuple(p) for p in ap.ap]
    pstride = pairs[0][0]
    off = ap.offset % pstride if pstride else ap.offset
    ext = 1
    for st, cnt in pairs[1:]:
        ext += (cnt - 1) * abs(st)
    return ap.tensor.name, off * es, (off + ext) * es


class Op:
    __slots__ = ("eng", "fn", "idx", "waits", "signal", "sigval", "dma_key", "dma_cnt", "know", "gid")


class Sched:
    CENG = ("pe", "act", "dve", "pool")
    GRAN = 128

    def __init__(self):
        self.ops = []
        self.eng_ops = {e: [] for e in ("pe", "act", "dve", "pool", "sp")}
        self.nidx = {e: 0 for e in self.CENG}
        self.blocks = {}
        self.know = {e: {} for e in ("pe", "act", "dve", "pool", "sp")}
        self.dma_cnt = {}

    def _blocks(self, ap):
        name, a, b = _hull(ap)
        g = 2048 if name.startswith("ps") else self.GRAN
        return [(name, i) for i in range(a // g, (b - 1) // g + 1)]

    def add(self, eng, fn, reads=(), writes=(), dma_key=None, extra_deps=()):
        op = Op()
        op.eng = eng
        op.fn = fn
        op.dma_key = dma_key
        op.signal = False
        op.sigval = None
        op.waits = []
        op.gid = len(self.ops)
        is_dma = dma_key is not None
        if is_dma:
            self.dma_cnt[dma_key] = self.dma_cnt.get(dma_key, 0) + 16
            op.dma_cnt = self.dma_cnt[dma_key]
            op.idx = None
        else:
            op.dma_cnt = None
            if eng in self.CENG and fn is not None:
                op.idx = self.nidx[eng]
                self.nidx[eng] += 1
            else:
                op.idx = None
        deps = {}
        for d in extra_deps:
            deps[d] = True
        for ap in reads:
            for blk in self._blocks(ap):
                ent = self.blocks.get(blk)
                if ent is None:
                    ent = [None, {}, []]
                    self.blocks[blk] = ent
                if ent[0] is not None:
                    deps[ent[0]] = True
        for ap in writes:
            for blk in self._blocks(ap):
                ent = self.blocks.get(blk)
                if ent is None:
                    ent = [None, {}, []]
                    self.blocks[blk] = ent
                if ent[0] is not None and ent[0] not in deps:
                    deps[ent[0]] = False
                for r in ent[1].values():
                    if r not in deps:
                        deps[r] = False
                for r in ent[2]:
                    if r not in deps:
                        deps[r] = False
        deps.pop(op, None)
        kn = self.know[eng]
        for a in sorted(deps, key=lambda o: -o.gid):
            raw = deps[a]
            if a.dma_key is None:
                if a.idx is None:
                    continue
                if a.eng == eng and not is_dma:
                    if eng == "pe":
                        continue
                    if not raw:
                        continue
                key, val = a.eng, a.idx + 1
            else:
                key, val = ("d", a.dma_key), a.dma_cnt
            if kn.get(key, 0) >= val:
                continue
            op.waits.append(a)
            a.signal = True
            for k, v in a.know.items():
                if kn.get(k, 0) < v:
                    kn[k] = v
        op.know = dict(kn)
        if is_dma:
            op.know[("d", dma_key)] = op.dma_cnt
        elif op.idx is not None:
            op.know[eng] = op.idx + 1
        for ap in reads:
            for blk in self._blocks(ap):
                ent = self.blocks[blk]
                if is_dma or op.idx is None:
                    ent[2].append(op)
                else:
                    ent[1][eng] = op
        for ap in writes:
            for blk in self._blocks(ap):
                ent = self.blocks[blk]
                ent[0] = op
                ent[1] = {}
                ent[2] = []
        self.ops.append(op)
        self.eng_ops[eng].append(op)
        return op

    def finalize(self):
        for e in self.CENG:
            n = 0
            for op in self.eng_ops[e]:
                if op.dma_key is None and op.idx is not None and op.signal:
                    n += 1
                    op.sigval = n

    def emit(self, eng, handle, sems, dsems):
        for op in self.eng_ops[eng]:
            for a in op.waits:
                if a.dma_key is None:
                    handle.wait_ge(sems[a.eng], a.sigval)
                else:
                    handle.wait_ge(dsems[a.dma_key], a.dma_cnt)
            if op.fn is None:
                continue
            ins = op.fn(handle)
            if op.dma_key is not None:
                ins.then_inc(dsems[op.dma_key], 16)
            elif op.signal:
                ins.then_inc(sems[op.eng], 1)


def build_program():
    nc = bass.Bass("TRN2", target_bir_lowering=False)
    x_d = nc.dram_tensor("x", [SEQ, D], F32, kind="ExternalInput").ap()
    mem_d = nc.dram_tensor("mem", [MEM, D], F32, kind="ExternalInput").ap()
    w_in_d = nc.dram_tensor("w_in", [D, 2560], F32, kind="ExternalInput").ap()
    w_out_d = nc.dram_tensor("w_out", [D, D], F32, kind="ExternalInput").ap()
    w_q_d = nc.dram_tensor("w_q", [D, D], F32, kind="ExternalInput").ap()
    w_kv_d = nc.dram_tensor("w_kv", [D, 2 * D], F32, kind="ExternalInput").ap()
    w_o_d = nc.dram_tensor("w_o", [D, D], F32, kind="ExternalInput").ap()
    w_gu_d = nc.dram_tensor("w_gate_up", [D, 2 * DFF], F32, kind="ExternalInput").ap()
    w_dn_d = nc.dram_tensor("w_down", [DFF, D], F32, kind="ExternalInput").ap()
    cvec_d = nc.dram_tensor("cvec", [128, NCV], F32, kind="ExternalInput").ap()
    wspT_d = nc.dram_tensor("wspT", [128, 4, 128], F32, kind="ExternalInput").ap()
    bspb_d = nc.dram_tensor("bspb", [128, 4, 128], F32, kind="ExternalInput").ap()
    y_d = nc.dram_tensor("y", [SEQ, D], F32, kind="ExternalOutput").ap()

    top = [0]

    def alloc(nbytes, align=128):
        off = (top[0] + align - 1) // align * align
        top[0] = off + nbytes
        return off

    o_ident = alloc(512)
    o_ones = alloc(256)
    o_cvec = alloc(NCV * 4)
    o_wTm = alloc(1024)
    o_E = alloc(2048)
    o_KT = alloc(4096)
    o_Vt = alloc(4096)
    o_zc = alloc(32)
    o_small = alloc(512)
    o_sq = alloc(8192)
    o_std = alloc(4096)
    o_xT = alloc(32768)
    o_xn = alloc(16384)
    o_ring = alloc(NSLOT * SLOT_B)
    o_scr = alloc(81920)
    total = top[0]
    big = nc.alloc_sbuf_tensor("big", [128, total], U8)

    def view(off, dt, *shape):
        n = 1
        for s_ in shape:
            n *= s_
        ap = big[:, off:off + n * _esize(dt)].bitcast(dt)
        if len(shape) == 2:
            ap = ap.rearrange("p (a b) -> p a b", a=shape[0])
        elif len(shape) == 3:
            ap = ap.rearrange("p (a b c) -> p a b c", a=shape[0], b=shape[1])
        return ap

    ident = view(o_ident, F32, 128)
    ones = view(o_ones, BF16, 128)
    cvec = view(o_cvec, F32, NCV)
    wTm = view(o_wTm, BF16, 4, 128)
    E = view(o_E, F32, 4, 128)
    KT = view(o_KT, BF16, 8, 256)
    Vt = view(o_Vt, BF16, 2, 1024)
    zc = view(o_zc, F32, 4, 2)
    st6 = view(o_small, F32, 4, 6)
    mv = view(o_small + 128, F32, 4, 2)
    sdv = view(o_small + 256, F32, 4)
    nmr = view(o_small + 384, F32, 4)
    sq = view(o_sq, BF16, 8, 512)
    stdb = view(o_std, F32, 2, 512)
    xT = view(o_xT, F32, 8, TS)
    xn = view(o_xn, BF16, 8, TS)
    u_b = view(o_scr + 0, F32, 4, TS)
    gb_b = view(o_scr + 16384, BF16, 4, TS)
    gc_b = view(o_scr + 24576, BF16, 4, TS)
    vn_b = view(o_scr + 32768, BF16, 8, 512)
    yn_b = view(o_scr + 40960, BF16, 8, TS)
    ya_b = view(o_scr + 57344, F32, 4, 512)
    vg_b = view(o_scr + 57344, F32, 4, 512)
    z_b = view(o_scr + 65536, F32, 4, 514)
    q_b = view(o_scr + 0, BF16, 8, TS)
    o_b = view(o_scr + 16384, BF16, 8, TS)
    p_b = view(o_scr + 32768, BF16, 8, 512)
    rden_b = view(o_scr + 40960, F32, 4, 512)
    h_b = view(o_scr + 0, BF16, KF, TS)
    sg_b = view(o_scr + 45056, F32, 2, 512)
    outT_b = view(o_scr + 49152, F32, 8, 512)
    xin = view(o_scr + 65536, F32, 4, D)
    memT = view(o_scr + 0, F32, 8, MEM)
    memn = view(o_scr + 8192, BF16, 8, MEM)
    wTf = view(o_scr + 16384, F32, 4, 128)
    bspb = view(o_scr + 18432, F32, 4, 128)

    ps = nc.alloc_psum_tensor("ps", [128, 4096], F32)
    bank_ctr = [0]

    def bank():
        b = bank_ctr[0] % 8
        bank_ctr[0] += 1
        return ps[:, b * 512:(b + 1) * 512]

    S = Sched()

    groups = []

    def slot_view(si, dt, *shape):
        return view(o_ring + si * SLOT_B, dt, *shape)

    def a_group(w_d, c0):
        return ("A", [(lambda si: slot_view(si, BF16, 8, 512),
                       w_d[:, c0:c0 + 512].rearrange("(k p) n -> p k n", p=128))])

    def gu_group(i):
        def dv(half):
            return lambda si: view(o_ring + si * SLOT_B + half * 4096, BF16, 8, 256)
        return ("GU", [(dv(0), w_gu_d[:, 256 * i:256 * i + 256].rearrange("(k p) n -> p k n", p=128)),
                       (dv(1), w_gu_d[:, DFF + 256 * i:DFF + 256 * i + 256].rearrange("(k p) n -> p k n", p=128))])

    def dn_group(m):
        def dv(half):
            return lambda si: view(o_ring + si * SLOT_B + half * 11 * 256, BF16, 11, 128)
        return ("DN", [(dv(0), w_dn_d[0:11 * 128, m * 128:(m + 1) * 128].rearrange("(k p) n -> p k n", p=128)),
                       (dv(1), w_dn_d[11 * 128:22 * 128, m * 128:(m + 1) * 128].rearrange("(k p) n -> p k n", p=128))])

    for i in range(4):
        groups.append(a_group(w_kv_d, 512 * i))
    for st in range(NST):
        for i in range(5):
            groups.append(a_group(w_in_d, 512 * i))
        for i in range(2):
            groups.append(a_group(w_out_d, 512 * i))
        for i in range(2):
            groups.append(a_group(w_q_d, 512 * i))
        for i in range(2):
            groups.append(a_group(w_o_d, 512 * i))
        for i in range(11):
            groups.append(gu_group(i))
        for m in range(8):
            groups.append(dn_group(m))

    ring_state = {"next_dma": 0, "next_acq": 0}

    def ring_issue(gi):
        kind, parts = groups[gi]
        si = gi % NSLOT
        for dvf, src in parts:
            dst = dvf(si)
            S.add("pool", (lambda e, dst=dst, src=src: e.dma_start(out=dst, in_=src)),
                  writes=[dst], dma_key=("ring", si))

    def ring_acquire():
        gi = ring_state["next_acq"]
        ring_state["next_acq"] += 1
        want = min(gi + NSLOT - 1, len(groups) - 1)
        while ring_state["next_dma"] <= want:
            ring_issue(ring_state["next_dma"])
            ring_state["next_dma"] += 1
        kind, _ = groups[gi]
        si = gi % NSLOT
        if kind == "A":
            return slot_view(si, BF16, 8, 512)
        if kind == "GU":
            return slot_view(si, BF16, 2, 8, 256)
        return slot_view(si, BF16, KF, 128)

    std_ctr = [0]

    def norm(src, dst, gbase, nch, N, Tn):
        for c in range(nch):
            S.add("act", (lambda e, c=c: e.activation(out=sq[:, c, 0:Tn], in_=src[:, c, :], func=AF.Square)),
                  reads=[src[:, c, :]], writes=[sq[:, c, 0:Tn]])
        b = bank()

        def f(e):
            for c in range(nch):
                ins = e.matmul(b[:, 0:Tn], lhsT=ones, rhs=sq[:, c, 0:Tn], start=(c == 0), stop=(c == nch - 1))
            return ins
        S.add("pe", f, reads=[ones, sq[:, 0:nch, 0:Tn]], writes=[b])
        sd = stdb[:, std_ctr[0] % 2, 0:Tn]
        std_ctr[0] += 1
        S.add("act", (lambda e: e.activation(out=sd, in_=b[:, 0:Tn], func=AF.Sqrt, bias=EPS, scale=1.0 / N)),
              reads=[b], writes=[sd])
        S.add("dve", (lambda e: e.reciprocal(out=sd, in_=sd)), reads=[sd], writes=[sd])
        for c in range(nch):
            S.add("dve", (lambda e, c=c: e.scalar_tensor_tensor(
                out=dst[:, c, :], in0=src[:, c, :], scalar=cvec[:, gbase + c:gbase + c + 1], in1=sd,
                op0=ALU.mult, op1=ALU.mult)),
                reads=[src[:, c, :], cvec, sd], writes=[dst[:, c, :]])

    def mm_w(b, wfn, rhsfn, nk, ncols=T, extra_reads=()):
        def f(e):
            for k in range(nk):
                ins = e.matmul(b[:, 0:ncols], lhsT=wfn(k), rhs=rhsfn(k), start=(k == 0), stop=(k == nk - 1))
            return ins
        return f

    S.add("sp", lambda e: e.dma_start(out=cvec, in_=cvec_d), writes=[cvec], dma_key="cvec")
    S.add("sp", lambda e: e.dma_start(out=wTf, in_=wspT_d), writes=[wTf], dma_key="wTf")
    S.add("sp", lambda e: e.dma_start(out=bspb, in_=bspb_d), writes=[bspb], dma_key="bspb")
    S.add("sp", lambda e: e.dma_start(out=xin[:, 0:2, :], in_=mem_d.rearrange("(j p) d -> p j d", p=128)),
          writes=[xin[:, 0:2, :]], dma_key="xin")
    S.add("pool", lambda e: e.memset(ident, 0.0), writes=[ident])
    S.add("pool", lambda e: e.affine_select(out=ident, in_=ident, pattern=[[-1, 128]], compare_op=ALU.not_equal,
                                            fill=1.0, base=0, channel_multiplier=1),
          reads=[ident], writes=[ident])
    S.add("pool", lambda e: e.memset(ones, 1.0), writes=[ones])
    S.add("pool", lambda e: e.memset(zc, 0.0), writes=[zc])
    for h in range(4):
        S.add("pool", (lambda e, h=h: e.affine_select(out=wTf[:, h, :], in_=wTf[:, h, :], pattern=[[1, 128]],
                                                      compare_op=ALU.is_ge, fill=0.0, base=0, channel_multiplier=-1)),
              reads=[wTf[:, h, :]], writes=[wTf[:, h, :]])
    S.add("dve", lambda e: e.tensor_copy(out=wTm, in_=wTf), reads=[wTf], writes=[wTm])
    b = bank()

    def f_rw(e, b=b):
        for h in range(4):
            ins = e.matmul(b[:, h * 128:(h + 1) * 128], lhsT=ones, rhs=wTm[:, h, :], start=True, stop=True)
        return ins
    S.add("pe", f_rw, reads=[ones, wTm], writes=[b])
    for h in range(4):
        S.add("dve", (lambda e, h=h, b=b: e.scalar_tensor_tensor(
            out=E[:, h, :], in0=b[:, h * 128:(h + 1) * 128], scalar=cvec[:, C_BLN + h:C_BLN + h + 1],
            in1=bspb[:, h, :], op0=ALU.mult, op1=ALU.add)),
            reads=[b, cvec, bspb[:, h, :]], writes=[E[:, h, :]])

    for c in range(8):
        b = bank()

        def f(e, c=c, b=b):
            for j in range(2):
                ins = e.transpose(b[:, j * 128:(j + 1) * 128], xin[:, j, c * 128:(c + 1) * 128], ident)
            return ins
        S.add("pe", f, reads=[xin[:, 0:2, c * 128:(c + 1) * 128], ident], writes=[b])
        if c % 2 == 0:
            S.add("act", (lambda e, c=c, b=b: e.copy(out=memT[:, c, :], in_=b[:, 0:MEM])),
                  reads=[b], writes=[memT[:, c, :]])
        else:
            S.add("dve", (lambda e, c=c, b=b: e.tensor_copy(out=memT[:, c, :], in_=b[:, 0:MEM])),
                  reads=[b], writes=[memT[:, c, :]])
    norm(memT, memn, C_MEM, 8, D, MEM)
    for gi in range(2):
        g = ring_acquire()
        for mm in range(4):
            c = gi * 4 + mm
            b = bank()
            S.add("pe", mm_w(b, (lambda k, g=g, mm=mm: g[:, k, mm * 128:(mm + 1) * 128]),
                             (lambda k: memn[:, k, :]), 8, ncols=MEM),
                  reads=[g, memn], writes=[b])
            S.add("act", (lambda e, c=c, b=b: e.copy(out=KT[:, c, :], in_=b[:, 0:MEM])),
                  reads=[b], writes=[KT[:, c, :]])
    for gi in range(2):
        g = ring_acquire()
        for mc in range(2):
            b = bank()
            S.add("pe", mm_w(b, (lambda k, mc=mc: memn[:, k, mc * 128:(mc + 1) * 128]),
                             (lambda k, g=g: g[:, k, :]), 8, ncols=512),
                  reads=[g, memn], writes=[b])
            S.add("act", (lambda e, mc=mc, gi=gi, b=b: e.copy(out=Vt[:, mc, gi * 512:(gi + 1) * 512], in_=b)),
                  reads=[b], writes=[Vt[:, mc, gi * 512:(gi + 1) * 512]])

    out_dmas = []
    for st in range(NST):
        for s in range(NSUB):
            t0 = st * TS + s * T
            tsl = slice(s * T, (s + 1) * T)
            S.add("sp", (lambda e, t0=t0: e.dma_start(
                out=xin, in_=x_d[t0:t0 + T, :].rearrange("(j p) d -> p j d", p=128))),
                writes=[xin], dma_key="xin")
            for c in range(8):
                b = bank()

                def f(e, c=c, b=b):
                    for j in range(4):
                        ins = e.transpose(b[:, j * 128:(j + 1) * 128], xin[:, j, c * 128:(c + 1) * 128], ident)
                    return ins
                S.add("pe", f, reads=[xin[:, :, c * 128:(c + 1) * 128], ident], writes=[b])
                if c % 2 == 0:
                    S.add("act", (lambda e, c=c, b=b, tsl=tsl: e.copy(out=xT[:, c, tsl], in_=b)),
                          reads=[b], writes=[xT[:, c, tsl]])
                else:
                    S.add("dve", (lambda e, c=c, b=b, tsl=tsl: e.tensor_copy(out=xT[:, c, tsl], in_=b)),
                          reads=[b], writes=[xT[:, c, tsl]])
        for s in range(NSUB):
            tsl = slice(s * T, (s + 1) * T)
            norm(xT[:, :, tsl], xn[:, :, tsl], C_MIX, 8, D, T)
        g = ring_acquire()
        for s in range(NSUB):
            tsl = slice(s * T, (s + 1) * T)
            for m in range(4):
                b = bank()
                S.add("pe", mm_w(b, (lambda k, g=g, m=m: g[:, k, m * 128:(m + 1) * 128]),
                                 (lambda k, tsl=tsl: xn[:, k, tsl]), 8),
                      reads=[g, xn[:, :, tsl]], writes=[b])
                S.add("act", (lambda e, m=m, b=b, tsl=tsl: e.activation(out=u_b[:, m, tsl], in_=b,
                                                                         func=AF.Gelu_apprx_tanh)),
                      reads=[b], writes=[u_b[:, m, tsl]])
        g = ring_acquire()
        for s in range(NSUB):
            tsl = slice(s * T, (s + 1) * T)
            for j in range(4):
                b = bank()
                S.add("pe", mm_w(b, (lambda k, s=s, j=j: xn[:, k, s * T + j * 128:s * T + (j + 1) * 128]),
                                 (lambda k, g=g: g[:, k, :]), 8),
                      reads=[g, xn[:, :, s * T + j * 128:s * T + (j + 1) * 128]], writes=[b])
                S.add("act", (lambda e, j=j, b=b: e.activation(out=vg_b[:, j, :], in_=b, func=AF.Gelu_apprx_tanh)),
                      reads=[b], writes=[vg_b[:, j, :]])
                S.add("dve", (lambda e, j=j: e.bn_stats(out=st6[:, j, :], in_=vg_b[:, j, :])),
                      reads=[vg_b[:, j, :]], writes=[st6[:, j, :]])
                S.add("dve", (lambda e, j=j: e.bn_aggr(out=mv[:, j, :], in_=st6[:, j, :])),
                      reads=[st6[:, j, :]], writes=[mv[:, j, :]])
            S.add("act", (lambda e: e.activation(out=sdv, in_=mv[:, :, 1], func=AF.Sqrt, bias=EPS, scale=1.0)),
                  reads=[mv], writes=[sdv])
            S.add("dve", (lambda e: e.reciprocal(out=sdv, in_=sdv)), reads=[sdv], writes=[sdv])
            S.add("dve", (lambda e: e.scalar_tensor_tensor(out=nmr, in0=mv[:, :, 0], scalar=-1.0, in1=sdv,
                                                           op0=ALU.mult, op1=ALU.mult)),
                  reads=[mv, sdv], writes=[nmr])
            for j in range(4):
                S.add("act", (lambda e, s=s, j=j: e.activation(out=vn_b[:, s * 4 + j, :], in_=vg_b[:, j, :],
                                                               func=AF.Identity, bias=nmr[:, j:j + 1],
                                                               scale=sdv[:, j:j + 1])),
                      reads=[vg_b[:, j, :], nmr, sdv], writes=[vn_b[:, s * 4 + j, :]])
            for h in range(4):
                b = bank()

                def f(e, s=s, h=h, b=b):
                    for j in range(4):
                        ins = e.matmul(b[:, j * 128:(j + 1) * 128], lhsT=vn_b[:, s * 4 + j, h * 128:(h + 1) * 128],
                                       rhs=wTm[:, h, :], start=True, stop=True)
                    return ins
                S.add("pe", f, reads=[vn_b[:, s * 4:(s + 1) * 4, h * 128:(h + 1) * 128], wTm[:, h, :]], writes=[b])
                S.add("dve", (lambda e, h=h, b=b: e.scalar_tensor_tensor(
                    out=ya_b[:, h, :].rearrange("p (j t) -> p j t", j=4),
                    in0=b.rearrange("p (j t) -> p j t", j=4),
                    scalar=cvec[:, C_GLN + h:C_GLN + h + 1],
                    in1=E[:, h, :].unsqueeze(1).to_broadcast([128, 4, 128]),
                    op0=ALU.mult, op1=ALU.add)),
                    reads=[b, cvec, E[:, h, :]], writes=[ya_b[:, h, :]])
                S.add("dve", (lambda e, h=h, tsl=tsl: e.tensor_tensor(out=ya_b[:, h, :], in0=ya_b[:, h, :],
                                                                       in1=u_b[:, h, tsl], op=ALU.mult)),
                      reads=[ya_b[:, h, :], u_b[:, h, tsl]], writes=[ya_b[:, h, :]])
            norm(ya_b, yn_b[:, 0:4, tsl], C_GA, 4, 512, T)
        for dstb in (gb_b, gc_b):
            g = ring_acquire()
            for s in range(NSUB):
                tsl = slice(s * T, (s + 1) * T)
                for m in range(4):
                    b = bank()
                    S.add("pe", mm_w(b, (lambda k, g=g, m=m: g[:, k, m * 128:(m + 1) * 128]),
                                     (lambda k, tsl=tsl: xn[:, k, tsl]), 8),
                          reads=[g, xn[:, :, tsl]], writes=[b])
                    S.add("act", (lambda e, m=m, b=b, tsl=tsl, dstb=dstb: e.copy(out=dstb[:, m, tsl], in_=b)),
                          reads=[b], writes=[dstb[:, m, tsl]])
        g = ring_acquire()
        for s in range(NSUB):
            tsl = slice(s * T, (s + 1) * T)
            S.add("dve", (lambda e: e.tensor_copy(out=z_b[:, :, 0:2], in_=zc)), reads=[zc], writes=[z_b[:, :, 0:2]])
            for m in range(4):
                b = bank()
                S.add("pe", mm_w(b, (lambda k, g=g, m=m: g[:, k, m * 128:(m + 1) * 128]),
                                 (lambda k, tsl=tsl: xn[:, k, tsl]), 8),
                      reads=[g, xn[:, :, tsl]], writes=[b])
                S.add("dve", (lambda e, m=m, b=b, tsl=tsl: e.tensor_tensor(out=z_b[:, m, 2:514], in0=b,
                                                                            in1=gc_b[:, m, tsl], op=ALU.mult)),
                      reads=[b, gc_b[:, m, tsl]], writes=[z_b[:, m, 2:514]])
            S.add("dve", (lambda e: e.tensor_copy(out=zc, in_=z_b[:, :, 512:514])),
                  reads=[z_b[:, :, 512:514]], writes=[zc])
            for m in range(4):
                S.add("dve", (lambda e, m=m: e.tensor_scalar(out=ya_b[:, m, :], in0=z_b[:, m, 0:512],
                                                             scalar1=cvec[:, C_CW + m:C_CW + m + 1], scalar2=None,
                                                             op0=ALU.mult)),
                      reads=[z_b[:, m, 0:512], cvec], writes=[ya_b[:, m, :]])
                for jj in (1, 2):
                    S.add("dve", (lambda e, m=m, jj=jj: e.scalar_tensor_tensor(
                        out=ya_b[:, m, :], in0=z_b[:, m, jj:jj + 512],
                        scalar=cvec[:, C_CW + jj * 4 + m:C_CW + jj * 4 + m + 1], in1=ya_b[:, m, :],
                        op0=ALU.mult, op1=ALU.add)),
                        reads=[z_b[:, m, jj:jj + 512], cvec, ya_b[:, m, :]], writes=[ya_b[:, m, :]])
                S.add("dve", (lambda e, m=m, tsl=tsl: e.tensor_tensor(out=ya_b[:, m, :], in0=ya_b[:, m, :],
                                                                       in1=gb_b[:, m, tsl], op=ALU.mult)),
                      reads=[ya_b[:, m, :], gb_b[:, m, tsl]], writes=[ya_b[:, m, :]])
            norm(ya_b, yn_b[:, 4:8, tsl], C_GB, 4, 512, T)

        def proj_residual(src_b):
            for gi in range(2):
                g = ring_acquire()
                for s in range(NSUB):
                    tsl = slice(s * T, (s + 1) * T)
                    for mm in range(4):
                        m = gi * 4 + mm
                        b = bank()
                        S.add("pe", mm_w(b, (lambda k, g=g, mm=mm: g[:, k, mm * 128:(mm + 1) * 128]),
                                         (lambda k, tsl=tsl: src_b[:, k, tsl]), 8),
                              reads=[g, src_b[:, :, tsl]], writes=[b])
                        S.add("dve", (lambda e, m=m, b=b, tsl=tsl: e.tensor_tensor(
                            out=xT[:, m, tsl], in0=b, in1=xT[:, m, tsl], op=ALU.add)),
                            reads=[b, xT[:, m, tsl]], writes=[xT[:, m, tsl]])
        proj_residual(yn_b)

        for s in range(NSUB):
            tsl = slice(s * T, (s + 1) * T)
            norm(xT[:, :, tsl], xn[:, :, tsl], C_ATT, 8, D, T)
        for gi in range(2):
            g = ring_acquire()
            for s in range(NSUB):
                tsl = slice(s * T, (s + 1) * T)
                for mm in range(4):
                    m = gi * 4 + mm
                    b = bank()
                    S.add("pe", mm_w(b, (lambda k, g=g, mm=mm: g[:, k, mm * 128:(mm + 1) * 128]),
                                     (lambda k, tsl=tsl: xn[:, k, tsl]), 8),
                          reads=[g, xn[:, :, tsl]], writes=[b])
                    S.add("act", (lambda e, m=m, b=b, tsl=tsl: e.activation(out=q_b[:, m, tsl], in_=b, func=AF.Copy,
                                                                             scale=0.0625)),
                          reads=[b], writes=[q_b[:, m, tsl]])
        for s in range(NSUB):
            tsl = slice(s * T, (s + 1) * T)
            for h in range(4):
                for mc in range(2):
                    b = bank()

                    def f(e, h=h, mc=mc, b=b, tsl=tsl):
                        for half in range(2):
                            ins = e.matmul(b, lhsT=KT[:, h * 2 + half, mc * 128:(mc + 1) * 128],
                                           rhs=q_b[:, h * 2 + half, tsl], start=(half == 0), stop=(half == 1))
                        return ins
                    S.add("pe", f, reads=[KT[:, h * 2:h * 2 + 2, :], q_b[:, h * 2:h * 2 + 2, tsl]], writes=[b])
                    S.add("act", (lambda e, h=h, mc=mc, b=b: e.activation(out=p_b[:, h * 2 + mc, :], in_=b,
                                                                          func=AF.Exp)),
                          reads=[b], writes=[p_b[:, h * 2 + mc, :]])
            for h in range(4):
                b = bank()

                def f(e, h=h, b=b):
                    for mc in range(2):
                        ins = e.matmul(b, lhsT=ones, rhs=p_b[:, h * 2 + mc, :], start=(mc == 0), stop=(mc == 1))
                    return ins
                S.add("pe", f, reads=[ones, p_b[:, h * 2:h * 2 + 2, :]], writes=[b])
                S.add("dve", (lambda e, h=h, b=b: e.reciprocal(out=rden_b[:, h, :], in_=b)),
                      reads=[b], writes=[rden_b[:, h, :]])
            for c in range(8):
                h = c // 2
                b = bank()

                def f(e, c=c, h=h, b=b):
                    for mc in range(2):
                        ins = e.matmul(b, lhsT=Vt[:, mc, c * 128:(c + 1) * 128], rhs=p_b[:, h * 2 + mc, :],
                                       start=(mc == 0), stop=(mc == 1))
                    return ins
                S.add("pe", f, reads=[Vt[:, :, c * 128:(c + 1) * 128], p_b[:, h * 2:h * 2 + 2, :]], writes=[b])
                S.add("dve", (lambda e, c=c, h=h, b=b, tsl=tsl: e.tensor_tensor(
                    out=o_b[:, c, tsl], in0=b, in1=rden_b[:, h, :], op=ALU.mult)),
                    reads=[b, rden_b[:, h, :]], writes=[o_b[:, c, tsl]])
        proj_residual(o_b)

        for s in range(NSUB):
            tsl = slice(s * T, (s + 1) * T)
            norm(xT[:, :, tsl], xn[:, :, tsl], C_FFN, 8, D, T)
        sg_ctr = 0
        for gi in range(11):
            g = ring_acquire()
            for s in range(NSUB):
                tsl = slice(s * T, (s + 1) * T)
                for i in range(2):
                    bg = bank()
                    S.add("pe", mm_w(bg, (lambda k, g=g, i=i: g[:, 0, k, i * 128:(i + 1) * 128]),
                                     (lambda k, tsl=tsl: xn[:, k, tsl]), 8),
                          reads=[g, xn[:, :, tsl]], writes=[bg])
                    bu = bank()
                    S.add("pe", mm_w(bu, (lambda k, g=g, i=i: g[:, 1, k, i * 128:(i + 1) * 128]),
                                     (lambda k, tsl=tsl: xn[:, k, tsl]), 8),
                          reads=[g, xn[:, :, tsl]], writes=[bu])
                    sg = sg_b[:, sg_ctr % 2, :]
                    sg_ctr += 1
                    S.add("act", (lambda e, sg=sg, bg=bg: e.activation(out=sg, in_=bg, func=AF.Silu)),
                          reads=[bg], writes=[sg])
                    S.add("dve", (lambda e, sg=sg, bu=bu, j=gi * 2 + i, tsl=tsl: e.tensor_tensor(
                        out=h_b[:, j, tsl], in0=bu, in1=sg, op=ALU.mult)),
                        reads=[bu, sg], writes=[h_b[:, gi * 2 + i, tsl]])
        for m in range(8):
            g = ring_acquire()
            for s in range(NSUB):
                tsl = slice(s * T, (s + 1) * T)
                b = bank()
                S.add("pe", mm_w(b, (lambda k, g=g: g[:, k, :]), (lambda k, tsl=tsl: h_b[:, k, tsl]), KF),
                      reads=[g, h_b[:, :, tsl]], writes=[b])
                S.add("dve", (lambda e, m=m, b=b, tsl=tsl: e.tensor_tensor(
                    out=xT[:, m, tsl], in0=b, in1=xT[:, m, tsl], op=ALU.add)),
                    reads=[b, xT[:, m, tsl]], writes=[xT[:, m, tsl]])

        for s in range(NSUB):
            t0 = st * TS + s * T
            tsl = slice(s * T, (s + 1) * T)
            norm(xT[:, :, tsl], outT_b, C_FIN, 8, D, T)
            for j in range(4):
                for half in range(2):
                    b = bank()

                    def f(e, j=j, half=half, b=b):
                        for cc in range(4):
                            c = half * 4 + cc
                            ins = e.transpose(b[:, cc * 128:(cc + 1) * 128], outT_b[:, c, j * 128:(j + 1) * 128], ident)
                        return ins
                    S.add("pe", f, reads=[outT_b[:, half * 4:half * 4 + 4, j * 128:(j + 1) * 128], ident], writes=[b])
                    dsts = xin[:, j, half * 512:(half + 1) * 512]
                    if (j + half) % 2 == 0:
                        S.add("act", (lambda e, dsts=dsts, b=b: e.copy(out=dsts, in_=b)), reads=[b], writes=[dsts])
                    else:
                        S.add("dve", (lambda e, dsts=dsts, b=b: e.tensor_copy(out=dsts, in_=b)), reads=[b], writes=[dsts])
            od = S.add("sp", (lambda e, t0=t0: e.dma_start(
                out=y_d[t0:t0 + T, :].rearrange("(j p) d -> p j d", p=128), in_=xin)),
                reads=[xin], dma_key="yout")
            out_dmas.append(od)

    S.add("sp", None, extra_deps=[out_dmas[-1]])
    assert ring_state["next_acq"] == len(groups), (ring_state, len(groups))

    S.finalize()
    sems = {e: nc.alloc_semaphore("sem_" + e) for e in Sched.CENG}
    dsems = {}
    for key in S.dma_cnt:
        dsems[key] = nc.alloc_semaphore("dsem_" + "_".join(str(k) for k in (key if isinstance(key, tuple) else (key,))))

    with nc.Block() as block:
        @block.tensor
        def _(e):
            S.emit("pe", e, sems, dsems)

        @block.scalar
        def _(e):
            S.emit("act", e, sems, dsems)

        @block.vector
        def _(e):
            S.emit("dve", e, sems, dsems)

        @block.gpsimd
        def _(e):
            S.emit("pool", e, sems, dsems)

        @block.sync
        def _(e):
            S.emit("sp", e, sems, dsems)
    return nc


def _host_layout(inputs):
    f = lambda a: np.ascontiguousarray(np.asarray(a, dtype=np.float32))
    col = lambda v, n: f(v).reshape(n, 128).T
    cw = f(inputs["conv_w"])
    cvec = np.concatenate([
        col(inputs["ln_mix_g"], 8), col(inputs["ln_attn_g"], 8), col(inputs["ln_ffn_g"], 8),
        col(inputs["ln_final_g"], 8), col(inputs["ln_mem_g"], 8),
        col(inputs["grp_norm_a"], 4), col(inputs["grp_norm_b"], 4),
        col(inputs["sgu_ln_g"], 4), col(inputs["sgu_ln_b"], 4),
        col(cw[0], 4), col(cw[1], 4), col(cw[2], 4),
    ], axis=1)
    assert cvec.shape == (128, NCV)
    wspT = f(np.transpose(f(inputs["w_spatial"]), (2, 0, 1)))
    bspb = f(np.broadcast_to(f(inputs["b_spatial"])[None, :, :], (128, 4, 128)))
    shared = {
        "w_in": f(inputs["w_in"]), "w_out": f(inputs["w_out"]), "w_q": f(inputs["w_q"]),
        "w_kv": f(inputs["w_kv"]), "w_o": f(inputs["w_o"]), "w_gate_up": f(inputs["w_gate_up"]),
        "w_down": f(inputs["w_down"]), "cvec": f(cvec), "wspT": wspT, "bspb": bspb,
    }
    x = f(inputs["x"])
    mem = f(inputs["mem"])
    in_maps = []
    for b in range(8):
        d = dict(shared)
        d["x"] = x[b]
        d["mem"] = mem[b]
        in_maps.append(d)
    return in_maps


def kernel(**inputs):
    in_maps = _host_layout(inputs)
    nc = build_program()
    res = run_bass_kernel_spmd(nc, in_maps, core_ids=list(range(8)))
    out = np.stack([np.asarray(r["y"], dtype=np.float32) for r in res.results], axis=0)
    return out
```

```python
import numpy as np
import concourse.bass as bass
import concourse.mybir as mybir
from concourse.bass_utils import run_bass_kernel_spmd

F32 = mybir.dt.float32
BF16 = mybir.dt.bfloat16
U8 = mybir.dt.uint8
AF = mybir.ActivationFunctionType
ALU = mybir.AluOpType

D = 1024
KD = 8
SEQ = 4096
TS = 1024
T = 512
NSUB = TS // T
NST = SEQ // TS
MEM = 256
DFF = 2816
KF = DFF // 128
EPS = 1e-6
NSLOT = 5
SLOT_B = 8192

C_MIX, C_ATT, C_FFN, C_FIN, C_MEM = 0, 8, 16, 24, 32
C_GA, C_GB, C_GLN, C_BLN, C_CW = 40, 44, 48, 52, 56
NCV = 68

_ES = {F32: 4, BF16: 2, U8: 1}


def _esize(dt):
    return _ES[dt]


def _hull(ap):
    es = _esize(ap.dtype)
    pairs = [tuple(p) for p in ap.ap]
    pstride = pairs[0][0]
    off = ap.offset % pstride if pstride else ap.offset
    ext = 1
    for st, cnt in pairs[1:]:
        ext += (cnt - 1) * abs(st)
    return ap.tensor.name, off * es, (off + ext) * es


class Op:
    __slots__ = ("eng", "fn", "idx", "waits", "signal", "sigval", "dma_key", "dma_cnt", "know", "gid")


class Sched:
    CENG = ("pe", "act", "dve", "pool")
    GRAN = 128

    def __init__(self):
        self.ops = []
        self.eng_ops = {e: [] for e in ("pe", "act", "dve", "pool", "sp")}
        self.nidx = {e: 0 for e in self.CENG}
        self.blocks = {}
        self.know = {e: {} for e in ("pe", "act", "dve", "pool", "sp")}
        self.dma_cnt = {}

    def _blocks(self, ap):
        name, a, b = _hull(ap)
        g = 2048 if name.startswith("ps") else self.GRAN
        return [(name, i) for i in range(a // g, (b - 1) // g + 1)]

    def add(self, eng, fn, reads=(), writes=(), dma_key=None, extra_deps=()):
        op = Op()
        op.eng = eng
        op.fn = fn
        op.dma_key = dma_key
        op.signal = False
        op.sigval = None
        op.waits = []
        op.gid = len(self.ops)
        is_dma = dma_key is not None
        if is_dma:
            self.dma_cnt[dma_key] = self.dma_cnt.get(dma_key, 0) + 16
            op.dma_cnt = self.dma_cnt[dma_key]
            op.idx = None
        else:
            op.dma_cnt = None
            if eng in self.CENG and fn is not None:
                op.idx = self.nidx[eng]
                self.nidx[eng] += 1
            else:
                op.idx = None
        deps = {}
        for d in extra_deps:
            deps[d] = True
        for ap in reads:
            for blk in self._blocks(ap):
                ent = self.blocks.get(blk)
                if ent is None:
                    ent = [None, {}, []]
                    self.blocks[blk] = ent
                if ent[0] is not None:
                    deps[ent[0]] = True
        for ap in writes:
            for blk in self._blocks(ap):
                ent = self.blocks.get(blk)
                if ent is None:
                    ent = [None, {}, []]
                    self.blocks[blk] = ent
                if ent[0] is not None and ent[0] not in deps:
                    deps[ent[0]] = False
                for r in ent[1].values():
                    if r not in deps:
                        deps[r] = False
                for r in ent[2]:
                    if r not in deps:
                        deps[r] = False
        deps.pop(op, None)
        kn = self.know[eng]
        for a in sorted(deps, key=lambda o: -o.gid):
            raw = deps[a]
            if a.dma_key is None:
                if a.idx is None:
                    continue
                if a.eng == eng and not is_dma:
                    if eng == "pe":
                        continue
                    if not raw:
                        continue
                key, val = a.eng, a.idx + 1
            else:
                key, val = ("d", a.dma_key), a.dma_cnt
            if kn.get(key, 0) >= val:
                continue
            op.waits.append(a)
            a.signal = True
            for k, v in a.know.items():
                if kn.get(k, 0) < v:
                    kn[k] = v
        op.know = dict(kn)
        if is_dma:
            op.know[("d", dma_key)] = op.dma_cnt
        elif op.idx is not None:
            op.know[eng] = op.idx + 1
        for ap in reads:
            for blk in self._blocks(ap):
                ent = self.blocks[blk]
                if is_dma or op.idx is None:
                    ent[2].append(op)
                else:
                    ent[1][eng] = op
        for ap in writes:
            for blk in self._blocks(ap):
                ent = self.blocks[blk]
                ent[0] = op
                ent[1] = {}
                ent[2] = []
        self.ops.append(op)
        self.eng_ops[eng].append(op)
        return op

    def finalize(self):
        for e in self.CENG:
            n = 0
            for op in self.eng_ops[e]:
                if op.dma_key is None and op.idx is not None and op.signal:
                    n += 1
                    op.sigval = n

    def emit(self, eng, handle, sems, dsems):
        for op in self.eng_ops[eng]:
            for a in op.waits:
                if a.dma_key is None:
                    handle.wait_ge(sems[a.eng], a.sigval)
                else:
                    handle.wait_ge(dsems[a.dma_key], a.dma_cnt)
            if op.fn is None:
                continue
            ins = op.fn(handle)
            if op.dma_key is not None:
                ins.then_inc(dsems[op.dma_key], 16)
            elif op.signal:
                ins.then_inc(sems[op.eng], 1)


def build_program():
    nc = bass.Bass("TRN2", target_bir_lowering=False)
    x_d = nc.dram_tensor("x", [SEQ, D], F32, kind="ExternalInput").ap()
    mem_d = nc.dram_tensor("mem", [MEM, D], F32, kind="ExternalInput").ap()
    w_in_d = nc.dram_tensor("w_in", [D, 2560], F32, kind="ExternalInput").ap()
    w_out_d = nc.dram_tensor("w_out", [D, D], F32, kind="ExternalInput").ap()
    w_q_d = nc.dram_tensor("w_q", [D, D], F32, kind="ExternalInput").ap()
    w_kv_d = nc.dram_tensor("w_kv", [D, 2 * D], F32, kind="ExternalInput").ap()
    w_o_d = nc.dram_tensor("w_o", [D, D], F32, kind="ExternalInput").ap()
    w_gu_d = nc.dram_tensor("w_gate_up", [D, 2 * DFF], F32, kind="ExternalInput").ap()
    w_dn_d = nc.dram_tensor("w_down", [DFF, D], F32, kind="ExternalInput").ap()
    cvec_d = nc.dram_tensor("cvec", [128, NCV], F32, kind="ExternalInput").ap()
    wspT_d = nc.dram_tensor("wspT", [128, 4, 128], F32, kind="ExternalInput").ap()
    bspb_d = nc.dram_tensor("bspb", [128, 4, 128], F32, kind="ExternalInput").ap()
    y_d = nc.dram_tensor("y", [SEQ, D], F32, kind="ExternalOutput").ap()

    top = [0]

    def alloc(nbytes, align=128):
        off = (top[0] + align - 1) // align * align
        top[0] = off + nbytes
        return off

    NSTD = 4
    SCR = 83968
    o_ident = alloc(512)
    o_ones = alloc(256)
    o_cvec = alloc(NCV * 4)
    o_wTm = alloc(1024)
    o_E = alloc(2048)
    o_KT = alloc(4096)
    o_Vt = alloc(4096)
    o_zc = alloc(32)
    o_small = alloc(1024)
    o_sq = alloc(16384)
    o_std = alloc(NSTD * 2048)
    o_xT = alloc(32768)
    o_xn = alloc(16384)
    o_ring = alloc(NSLOT * SLOT_B)
    o_scr = alloc(SCR)
    total = top[0]
    big = nc.alloc_sbuf_tensor("big", [128, total], U8)

    def view(off, dt, *shape):
        n = 1
        for s_ in shape:
            n *= s_
        ap = big[:, off:off + n * _esize(dt)].bitcast(dt)
        if len(shape) == 2:
            ap = ap.rearrange("p (a b) -> p a b", a=shape[0])
        elif len(shape) == 3:
            ap = ap.rearrange("p (a b c) -> p a b c", a=shape[0], b=shape[1])
        return ap

    ident = view(o_ident, F32, 128)
    ones = view(o_ones, BF16, 128)
    cvec = view(o_cvec, F32, NCV)
    wTm = view(o_wTm, BF16, 4, 128)
    E = view(o_E, F32, 4, 128)
    KT = view(o_KT, BF16, 8, 256)
    Vt = view(o_Vt, BF16, 2, 1024)
    zc = view(o_zc, F32, 4, 2)
    st6 = view(o_small, F32, 2, 4, 6)
    mv = view(o_small + 256, F32, 2, 4, 2)
    sdv = view(o_small + 384, F32, 2, 4)
    nmr = view(o_small + 448, F32, 2, 4)
    sq = view(o_sq, BF16, 2, 8, 512)
    stdb = view(o_std, F32, NSTD, 512)
    xT = view(o_xT, F32, 8, TS)
    xn = view(o_xn, BF16, 8, TS)
    u_b = view(o_scr + 0, F32, 4, TS)
    gc_b = view(o_scr + 16384, BF16, 4, TS)
    vn_b = view(o_scr + 24576, BF16, 8, 512)
    yn_b = view(o_scr + 32768, BF16, 8, TS)
    vg_b = view(o_scr + 49152, F32, 4, 512)
    z_b = view(o_scr + 57344, F32, 4, 514)
    yb_b = view(o_scr + 65664, F32, 4, TS)
    q_b = view(o_scr + 0, BF16, 8, TS)
    o_b = view(o_scr + 16384, BF16, 8, TS)
    p_b = view(o_scr + 32768, BF16, 2, 8, 512)
    rden_b = view(o_scr + 49152, F32, 2, 4, 512)
    h_b = view(o_scr + 0, BF16, KF, TS)
    sg_b = view(o_scr + 45056, F32, 2, 512)
    xin_b = view(o_scr + 0, F32, 2, 4, D)
    outT_b = view(o_scr + 32768, F32, 2, 8, 512)
    yst_b = view(o_scr + 65536, F32, 4, D)
    memT = view(o_scr + 32768, F32, 8, MEM)
    memn = view(o_scr + 40960, BF16, 8, MEM)
    wTf = view(o_scr + 49152, F32, 4, 128)
    bspb = view(o_scr + 51200, F32, 4, 128)

    ps = nc.alloc_psum_tensor("ps", [128, 4096], F32)
    bank_ctr = [0]

    def bank():
        b = bank_ctr[0] % 8
        bank_ctr[0] += 1
        return ps[:, b * 512:(b + 1) * 512]

    S = Sched()

    groups = []

    def slot_view(si, dt, *shape):
        return view(o_ring + si * SLOT_B, dt, *shape)

    def a_group(w_d, c0):
        return ("A", [(lambda si: slot_view(si, BF16, 8, 512),
                       w_d[:, c0:c0 + 512].rearrange("(k p) n -> p k n", p=128))])

    def gu_group(i):
        def dv(half):
            return lambda si: view(o_ring + si * SLOT_B + half * 4096, BF16, 8, 256)
        return ("GU", [(dv(0), w_gu_d[:, 256 * i:256 * i + 256].rearrange("(k p) n -> p k n", p=128)),
                       (dv(1), w_gu_d[:, DFF + 256 * i:DFF + 256 * i + 256].rearrange("(k p) n -> p k n", p=128))])

    def dn_group(m):
        def dv(half):
            return lambda si: view(o_ring + si * SLOT_B + half * 11 * 256, BF16, 11, 128)
        return ("DN", [(dv(0), w_dn_d[0:11 * 128, m * 128:(m + 1) * 128].rearrange("(k p) n -> p k n", p=128)),
                       (dv(1), w_dn_d[11 * 128:22 * 128, m * 128:(m + 1) * 128].rearrange("(k p) n -> p k n", p=128))])

    WIN_ORDER = (3, 4, 1, 0, 2)
    for i in range(4):
        groups.append(a_group(w_kv_d, 512 * i))
    for st in range(NST):
        for i in WIN_ORDER:
            groups.append(a_group(w_in_d, 512 * i))
        for i in range(2):
            groups.append(a_group(w_out_d, 512 * i))
        for i in range(2):
            groups.append(a_group(w_q_d, 512 * i))
        for i in range(2):
            groups.append(a_group(w_o_d, 512 * i))
        for i in range(11):
            groups.append(gu_group(i))
        for m in range(8):
            groups.append(dn_group(m))

    ring_state = {"next_dma": 0, "next_acq": 0}

    def ring_issue(gi):
        kind, parts = groups[gi]
        si = gi % NSLOT
        for dvf, src in parts:
            dst = dvf(si)
            S.add("pool", (lambda e, dst=dst, src=src: e.dma_start(out=dst, in_=src)),
                  writes=[dst], dma_key=("ring", si))

    held = []
    sticky = set()

    def ring_pump():
        base = held[0] if held else ring_state["next_acq"]
        want = min(base + NSLOT - 1, len(groups) - 1)
        while ring_state["next_dma"] <= want:
            ring_issue(ring_state["next_dma"])
            ring_state["next_dma"] += 1

    def ring_acquire(stick=False):
        gi = ring_state["next_acq"]
        ring_state["next_acq"] += 1
        held[:] = [h for h in held if h in sticky]
        held.append(gi)
        if stick:
            sticky.add(gi)
        ring_pump()
        kind, _ = groups[gi]
        si = gi % NSLOT
        if kind == "A":
            return slot_view(si, BF16, 8, 512)
        if kind == "GU":
            return slot_view(si, BF16, 2, 8, 256)
        return slot_view(si, BF16, KF, 128)

    def ring_unstick():
        sticky.clear()

    std_ctr = [0]

    def normA(src, nch, Tn, sqv):
        for c in range(nch):
            S.add("act", (lambda e, c=c: e.activation(out=sqv[:, c, 0:Tn], in_=src[:, c, :], func=AF.Square)),
                  reads=[src[:, c, :]], writes=[sqv[:, c, 0:Tn]])

    def normB(src, dst, gbase, nch, N, Tn, sqv):
        b = bank()

        def f(e):
            for c in range(nch):
                ins = e.matmul(b[:, 0:Tn], lhsT=ones, rhs=sqv[:, c, 0:Tn], start=(c == 0), stop=(c == nch - 1))
            return ins
        S.add("pe", f, reads=[ones, sqv[:, 0:nch, 0:Tn]], writes=[b])
        sd = stdb[:, std_ctr[0] % NSTD, 0:Tn]
        std_ctr[0] += 1
        S.add("act", (lambda e: e.activation(out=sd, in_=b[:, 0:Tn], func=AF.Sqrt, bias=EPS, scale=1.0 / N)),
              reads=[b], writes=[sd])
        S.add("dve", (lambda e: e.reciprocal(out=sd, in_=sd)), reads=[sd], writes=[sd])
        for c in range(nch):
            S.add("dve", (lambda e, c=c: e.scalar_tensor_tensor(
                out=dst[:, c, :], in0=src[:, c, :], scalar=cvec[:, gbase + c:gbase + c + 1], in1=sd,
                op0=ALU.mult, op1=ALU.mult)),
                reads=[src[:, c, :], cvec, sd], writes=[dst[:, c, :]])

    def mm_w(b, wfn, rhsfn, nk, ncols=T):
        def f(e):
            for k in range(nk):
                ins = e.matmul(b[:, 0:ncols], lhsT=wfn(k), rhs=rhsfn(k), start=(k == 0), stop=(k == nk - 1))
            return ins
        return f

    def tsl_of(s):
        return slice(s * T, (s + 1) * T)

    S.add("sp", lambda e: e.dma_start(out=cvec, in_=cvec_d), writes=[cvec], dma_key="cvec")
    S.add("sp", lambda e: e.dma_start(out=wTf, in_=wspT_d), writes=[wTf], dma_key="wTf")
    S.add("sp", lambda e: e.dma_start(out=bspb, in_=bspb_d), writes=[bspb], dma_key="bspb")
    S.add("sp", lambda e: e.dma_start(out=yst_b[:, 0:2, :], in_=mem_d.rearrange("(j p) d -> p j d", p=128)),
          writes=[yst_b[:, 0:2, :]], dma_key="memin")

    def x_load(st, s):
        t0 = st * TS + s * T
        S.add("sp", (lambda e: e.dma_start(
            out=xin_b[:, s, :, :], in_=x_d[t0:t0 + T, :].rearrange("(j p) d -> p j d", p=128))),
            writes=[xin_b[:, s, :, :]], dma_key=("xin", s))

    x_load(0, 0)
    x_load(0, 1)
    S.add("pool", lambda e: e.memset(ident, 0.0), writes=[ident])
    S.add("pool", lambda e: e.affine_select(out=ident, in_=ident, pattern=[[-1, 128]], compare_op=ALU.not_equal,
                                            fill=1.0, base=0, channel_multiplier=1),
          reads=[ident], writes=[ident])
    S.add("pool", lambda e: e.memset(ones, 1.0), writes=[ones])
    S.add("pool", lambda e: e.memset(zc, 0.0), writes=[zc])
    for h in range(4):
        S.add("pool", (lambda e, h=h: e.affine_select(out=wTf[:, h, :], in_=wTf[:, h, :], pattern=[[1, 128]],
                                                      compare_op=ALU.is_ge, fill=0.0, base=0, channel_multiplier=-1)),
              reads=[wTf[:, h, :]], writes=[wTf[:, h, :]])
    S.add("dve", lambda e: e.tensor_copy(out=wTm, in_=wTf), reads=[wTf], writes=[wTm])
    b = bank()

    def f_rw(e, b=b):
        for h in range(4):
            ins = e.matmul(b[:, h * 128:(h + 1) * 128], lhsT=ones, rhs=wTm[:, h, :], start=True, stop=True)
        return ins
    S.add("pe", f_rw, reads=[ones, wTm], writes=[b])
    for h in range(4):
        S.add("dve", (lambda e, h=h, b=b: e.scalar_tensor_tensor(
            out=E[:, h, :], in0=b[:, h * 128:(h + 1) * 128], scalar=cvec[:, C_BLN + h:C_BLN + h + 1],
            in1=bspb[:, h, :], op0=ALU.mult, op1=ALU.add)),
            reads=[b, cvec, bspb[:, h, :]], writes=[E[:, h, :]])

    for c in range(8):
        b = bank()

        def f(e, c=c, b=b):
            for j in range(2):
                ins = e.transpose(b[:, j * 128:(j + 1) * 128], yst_b[:, j, c * 128:(c + 1) * 128], ident)
            return ins
        S.add("pe", f, reads=[yst_b[:, 0:2, c * 128:(c + 1) * 128], ident], writes=[b])
        if c % 2 == 0:
            S.add("act", (lambda e, c=c, b=b: e.copy(out=memT[:, c, :], in_=b[:, 0:MEM])),
                  reads=[b], writes=[memT[:, c, :]])
        else:
            S.add("dve", (lambda e, c=c, b=b: e.tensor_copy(out=memT[:, c, :], in_=b[:, 0:MEM])),
                  reads=[b], writes=[memT[:, c, :]])
    normA(memT, 8, MEM, sq[:, 0])
    normB(memT, memn, C_MEM, 8, D, MEM, sq[:, 0])
    for gi in range(2):
        g = ring_acquire()
        for mm in range(4):
            c = gi * 4 + mm
            b = bank()
            S.add("pe", mm_w(b, (lambda k, g=g, mm=mm: g[:, k, mm * 128:(mm + 1) * 128]),
                             (lambda k: memn[:, k, :]), 8, ncols=MEM),
                  reads=[g, memn], writes=[b])
            S.add("act", (lambda e, c=c, b=b: e.copy(out=KT[:, c, :], in_=b[:, 0:MEM])),
                  reads=[b], writes=[KT[:, c, :]])
    for gi in range(2):
        g = ring_acquire()
        for mc in range(2):
            b = bank()
            S.add("pe", mm_w(b, (lambda k, mc=mc: memn[:, k, mc * 128:(mc + 1) * 128]),
                             (lambda k, g=g: g[:, k, :]), 8, ncols=512),
                  reads=[g, memn], writes=[b])
            S.add("act", (lambda e, mc=mc, gi=gi, b=b: e.copy(out=Vt[:, mc, gi * 512:(gi + 1) * 512], in_=b)),
                  reads=[b], writes=[Vt[:, mc, gi * 512:(gi + 1) * 512]])

    def wgroup(g, s, src, evac, nmm=4):
        tsl = tsl_of(s)
        for mm in range(nmm):
            b = bank()
            S.add("pe", mm_w(b, (lambda k, g=g, mm=mm: g[:, k, mm * 128:(mm + 1) * 128]),
                             (lambda k, tsl=tsl: src[:, k, tsl]), 8),
                  reads=[g, src[:, :, tsl]], writes=[b])
            evac(mm, b)

    def proj_residual(src_b, after_s):
        g0 = ring_acquire(stick=True)
        g1 = ring_acquire()
        for s in range(NSUB):
            tsl = tsl_of(s)
            for gi, g in enumerate((g0, g1)):
                def ev(mm, b, gi=gi, tsl=tsl):
                    m = gi * 4 + mm
                    S.add("dve", (lambda e: e.tensor_tensor(out=xT[:, m, tsl], in0=b, in1=xT[:, m, tsl], op=ALU.add)),
                          reads=[b, xT[:, m, tsl]], writes=[xT[:, m, tsl]])
                wgroup(g, s, src_b, ev)
            after_s(s)
        ring_unstick()

    out_dmas = []
    for st in range(NST):
        for s in range(NSUB):
            tsl = tsl_of(s)
            for c in range(8):
                b = bank()

                def f(e, c=c, b=b, s=s):
                    for j in range(4):
                        ins = e.transpose(b[:, j * 128:(j + 1) * 128], xin_b[:, s, j, c * 128:(c + 1) * 128], ident)
                    return ins
                S.add("pe", f, reads=[xin_b[:, s, :, c * 128:(c + 1) * 128], ident], writes=[b])
                if c % 2 == 0:
                    S.add("act", (lambda e, c=c, b=b, tsl=tsl: e.copy(out=xT[:, c, tsl], in_=b)),
                          reads=[b], writes=[xT[:, c, tsl]])
                else:
                    S.add("dve", (lambda e, c=c, b=b, tsl=tsl: e.tensor_copy(out=xT[:, c, tsl], in_=b)),
                          reads=[b], writes=[xT[:, c, tsl]])
            normA(xT[:, :, tsl], 8, T, sq[:, s])

        g = ring_acquire()
        for s in range(NSUB):
            tsl = tsl_of(s)
            normB(xT[:, :, tsl], xn[:, :, tsl], C_MIX, 8, D, T, sq[:, s])

            def ev(mm, b, tsl=tsl):
                S.add("act", (lambda e: e.copy(out=gc_b[:, mm, tsl], in_=b)), reads=[b], writes=[gc_b[:, mm, tsl]])
            wgroup(g, s, xn, ev)

        g_val = ring_acquire(stick=True)

        def val_part(s):
            tsl = tsl_of(s)
            S.add("dve", (lambda e: e.tensor_copy(out=z_b[:, :, 0:2], in_=zc)), reads=[zc], writes=[z_b[:, :, 0:2]])

            def ev(mm, b):
                S.add("dve", (lambda e: e.tensor_tensor(out=z_b[:, mm, 2:514], in0=b, in1=gc_b[:, mm, tsl],
                                                        op=ALU.mult)),
                      reads=[b, gc_b[:, mm, tsl]], writes=[z_b[:, mm, 2:514]])
            wgroup(g_val, s, xn, ev)
            S.add("dve", (lambda e: e.tensor_copy(out=zc, in_=z_b[:, :, 512:514])),
                  reads=[z_b[:, :, 512:514]], writes=[zc])
            for m in range(4):
                S.add("dve", (lambda e, m=m: e.tensor_scalar(out=yb_b[:, m, tsl], in0=z_b[:, m, 0:512],
                                                             scalar1=cvec[:, C_CW + m:C_CW + m + 1], scalar2=None,
                                                             op0=ALU.mult)),
                      reads=[z_b[:, m, 0:512], cvec], writes=[yb_b[:, m, tsl]])
                for jj in (1, 2):
                    S.add("dve", (lambda e, m=m, jj=jj: e.scalar_tensor_tensor(
                        out=yb_b[:, m, tsl], in0=z_b[:, m, jj:jj + 512],
                        scalar=cvec[:, C_CW + jj * 4 + m:C_CW + jj * 4 + m + 1], in1=yb_b[:, m, tsl],
                        op0=ALU.mult, op1=ALU.add)),
                        reads=[z_b[:, m, jj:jj + 512], cvec, yb_b[:, m, tsl]], writes=[yb_b[:, m, tsl]])
        val_part(0)

        g = ring_acquire()
        for s in range(NSUB):
            for j in range(4):
                b = bank()
                S.add("pe", mm_w(b, (lambda k, s=s, j=j: xn[:, k, s * T + j * 128:s * T + (j + 1) * 128]),
                                 (lambda k, g=g: g[:, k, :]), 8),
                      reads=[g, xn[:, :, s * T + j * 128:s * T + (j + 1) * 128]], writes=[b])
                S.add("act", (lambda e, j=j, b=b: e.activation(out=vg_b[:, j, :], in_=b, func=AF.Gelu_apprx_tanh)),
                      reads=[b], writes=[vg_b[:, j, :]])
                S.add("dve", (lambda e, j=j, s=s: e.bn_stats(out=st6[:, s, j, :], in_=vg_b[:, j, :])),
                      reads=[vg_b[:, j, :]], writes=[st6[:, s, j, :]])
                S.add("dve", (lambda e, j=j, s=s: e.bn_aggr(out=mv[:, s, j, :], in_=st6[:, s, j, :])),
                      reads=[st6[:, s, j, :]], writes=[mv[:, s, j, :]])
            S.add("act", (lambda e, s=s: e.activation(out=sdv[:, s, :], in_=mv[:, s, :, 1], func=AF.Sqrt, bias=EPS,
                                                      scale=1.0)),
                  reads=[mv[:, s]], writes=[sdv[:, s, :]])
            S.add("dve", (lambda e, s=s: e.reciprocal(out=sdv[:, s, :], in_=sdv[:, s, :])),
                  reads=[sdv[:, s, :]], writes=[sdv[:, s, :]])
            S.add("dve", (lambda e, s=s: e.scalar_tensor_tensor(out=nmr[:, s, :], in0=mv[:, s, :, 0], scalar=-1.0,
                                                                in1=sdv[:, s, :], op0=ALU.mult, op1=ALU.mult)),
                  reads=[mv[:, s], sdv[:, s, :]], writes=[nmr[:, s, :]])
            for j in range(4):
                S.add("act", (lambda e, s=s, j=j: e.activation(out=vn_b[:, s * 4 + j, :], in_=vg_b[:, j, :],
                                                               func=AF.Identity, bias=nmr[:, s, j:j + 1],
                                                               scale=sdv[:, s, j:j + 1])),
                      reads=[vg_b[:, j, :], nmr[:, s, :], sdv[:, s, :]], writes=[vn_b[:, s * 4 + j, :]])
        val_part(1)
        ring_unstick()

        g = ring_acquire()
        for s in range(NSUB):
            tsl = tsl_of(s)

            def ev(mm, b, tsl=tsl):
                S.add("act", (lambda e: e.activation(out=u_b[:, mm, tsl], in_=b, func=AF.Gelu_apprx_tanh)),
                      reads=[b], writes=[u_b[:, mm, tsl]])
            wgroup(g, s, xn, ev)

        for s in range(NSUB):
            tsl = tsl_of(s)
            for h in range(4):
                b = bank()

                def f(e, s=s, h=h, b=b):
                    for j in range(4):
                        ins = e.matmul(b[:, j * 128:(j + 1) * 128], lhsT=vn_b[:, s * 4 + j, h * 128:(h + 1) * 128],
                                       rhs=wTm[:, h, :], start=True, stop=True)
                    return ins
                S.add("pe", f, reads=[vn_b[:, s * 4:(s + 1) * 4, h * 128:(h + 1) * 128], wTm[:, h, :]], writes=[b])
                tmp = vg_b[:, h, :]
                S.add("dve", (lambda e, h=h, b=b, tmp=tmp: e.scalar_tensor_tensor(
                    out=tmp.rearrange("p (j t) -> p j t", j=4),
                    in0=b.rearrange("p (j t) -> p j t", j=4),
                    scalar=cvec[:, C_GLN + h:C_GLN + h + 1],
                    in1=E[:, h, :].unsqueeze(1).to_broadcast([128, 4, 128]),
                    op0=ALU.mult, op1=ALU.add)),
                    reads=[b, cvec, E[:, h, :]], writes=[tmp])
                S.add("dve", (lambda e, h=h, tsl=tsl, tmp=tmp: e.tensor_tensor(out=u_b[:, h, tsl], in0=tmp,
                                                                                in1=u_b[:, h, tsl], op=ALU.mult)),
                      reads=[tmp, u_b[:, h, tsl]], writes=[u_b[:, h, tsl]])
            normA(u_b[:, :, tsl], 4, T, sq[:, s, 0:4])

        g = ring_acquire()
        for s in range(NSUB):
            tsl = tsl_of(s)
            normB(u_b[:, :, tsl], yn_b[:, 0:4, tsl], C_GA, 4, 512, T, sq[:, s, 0:4])

            def ev(mm, b, tsl=tsl):
                S.add("dve", (lambda e: e.tensor_tensor(out=yb_b[:, mm, tsl], in0=b, in1=yb_b[:, mm, tsl],
                                                        op=ALU.mult)),
                      reads=[b, yb_b[:, mm, tsl]], writes=[yb_b[:, mm, tsl]])
            wgroup(g, s, xn, ev)
            normA(yb_b[:, :, tsl], 4, T, sq[:, s, 4:8])

        nb_done = [False, False]

        def after_wout(s):
            normA(xT[:, :, tsl_of(s)], 8, T, sq[:, s])

        g0 = ring_acquire(stick=True)
        g1 = ring_acquire()
        for s in range(NSUB):
            tsl = tsl_of(s)
            normB(yb_b[:, :, tsl], yn_b[:, 4:8, tsl], C_GB, 4, 512, T, sq[:, s, 4:8])
            for gi, g in enumerate((g0, g1)):
                def ev(mm, b, gi=gi, tsl=tsl):
                    m = gi * 4 + mm
                    S.add("dve", (lambda e: e.tensor_tensor(out=xT[:, m, tsl], in0=b, in1=xT[:, m, tsl], op=ALU.add)),
                          reads=[b, xT[:, m, tsl]], writes=[xT[:, m, tsl]])
                wgroup(g, s, yn_b, ev)
            after_wout(s)
        ring_unstick()

        for gi in range(2):
            g = ring_acquire()
            for s in range(NSUB):
                tsl = tsl_of(s)
                if gi == 0:
                    normB(xT[:, :, tsl], xn[:, :, tsl], C_ATT, 8, D, T, sq[:, s])

                def ev(mm, b, gi=gi, tsl=tsl):
                    m = gi * 4 + mm
                    S.add("act", (lambda e: e.activation(out=q_b[:, m, tsl], in_=b, func=AF.Copy, scale=0.0625)),
                          reads=[b], writes=[q_b[:, m, tsl]])
                wgroup(g, s, xn, ev)
        for s in range(NSUB):
            tsl = tsl_of(s)
            for h in range(4):
                for mc in range(2):
                    b = bank()

                    def f(e, h=h, mc=mc, b=b, tsl=tsl):
                        for half in range(2):
                            ins = e.matmul(b, lhsT=KT[:, h * 2 + half, mc * 128:(mc + 1) * 128],
                                           rhs=q_b[:, h * 2 + half, tsl], start=(half == 0), stop=(half == 1))
                        return ins
                    S.add("pe", f, reads=[KT[:, h * 2:h * 2 + 2, :], q_b[:, h * 2:h * 2 + 2, tsl]], writes=[b])
                    S.add("act", (lambda e, h=h, mc=mc, b=b, s=s: e.activation(out=p_b[:, s, h * 2 + mc, :], in_=b,
                                                                               func=AF.Exp)),
                          reads=[b], writes=[p_b[:, s, h * 2 + mc, :]])
        for s in range(NSUB):
            tsl = tsl_of(s)
            for h in range(4):
                b = bank()

                def f(e, h=h, b=b, s=s):
                    for mc in range(2):
                        ins = e.matmul(b, lhsT=ones, rhs=p_b[:, s, h * 2 + mc, :], start=(mc == 0), stop=(mc == 1))
                    return ins
                S.add("pe", f, reads=[ones, p_b[:, s, h * 2:h * 2 + 2, :]], writes=[b])
                S.add("dve", (lambda e, h=h, b=b, s=s: e.reciprocal(out=rden_b[:, s, h, :], in_=b)),
                      reads=[b], writes=[rden_b[:, s, h, :]])
            for c in range(8):
                h = c // 2
                b = bank()

                def f(e, c=c, h=h, b=b, s=s):
                    for mc in range(2):
                        ins = e.matmul(b, lhsT=Vt[:, mc, c * 128:(c + 1) * 128], rhs=p_b[:, s, h * 2 + mc, :],
                                       start=(mc == 0), stop=(mc == 1))
                    return ins
                S.add("pe", f, reads=[Vt[:, :, c * 128:(c + 1) * 128], p_b[:, s, h * 2:h * 2 + 2, :]], writes=[b])
                S.add("dve", (lambda e, c=c, h=h, b=b, tsl=tsl, s=s: e.tensor_tensor(
                    out=o_b[:, c, tsl], in0=b, in1=rden_b[:, s, h, :], op=ALU.mult)),
                    reads=[b, rden_b[:, s, h, :]], writes=[o_b[:, c, tsl]])
        proj_residual(o_b, after_wout)

        sg_ctr = 0
        for gi in range(11):
            g = ring_acquire()
            for s in range(NSUB):
                tsl = tsl_of(s)
                if gi == 0:
                    normB(xT[:, :, tsl], xn[:, :, tsl], C_FFN, 8, D, T, sq[:, s])
                for i in range(2):
                    bg = bank()
                    S.add("pe", mm_w(bg, (lambda k, g=g, i=i: g[:, 0, k, i * 128:(i + 1) * 128]),
                                     (lambda k, tsl=tsl: xn[:, k, tsl]), 8),
                          reads=[g, xn[:, :, tsl]], writes=[bg])
                    bu = bank()
                    S.add("pe", mm_w(bu, (lambda k, g=g, i=i: g[:, 1, k, i * 128:(i + 1) * 128]),
                                     (lambda k, tsl=tsl: xn[:, k, tsl]), 8),
                          reads=[g, xn[:, :, tsl]], writes=[bu])
                    sg = sg_b[:, sg_ctr % 2, :]
                    sg_ctr += 1
                    S.add("act", (lambda e, sg=sg, bg=bg: e.activation(out=sg, in_=bg, func=AF.Silu)),
                          reads=[bg], writes=[sg])
                    S.add("dve", (lambda e, sg=sg, bu=bu, j=gi * 2 + i, tsl=tsl: e.tensor_tensor(
                        out=h_b[:, j, tsl], in0=bu, in1=sg, op=ALU.mult)),
                        reads=[bu, sg], writes=[h_b[:, gi * 2 + i, tsl]])
        for m in range(8):
            g = ring_acquire()
            for s in range(NSUB):
                tsl = tsl_of(s)
                b = bank()
                S.add("pe", mm_w(b, (lambda k, g=g: g[:, k, :]), (lambda k, tsl=tsl: h_b[:, k, tsl]), KF),
                      reads=[g, h_b[:, :, tsl]], writes=[b])
                S.add("dve", (lambda e, m=m, b=b, tsl=tsl: e.tensor_tensor(
                    out=xT[:, m, tsl], in0=b, in1=xT[:, m, tsl], op=ALU.add)),
                    reads=[b, xT[:, m, tsl]], writes=[xT[:, m, tsl]])

        if st + 1 < NST:
            x_load(st + 1, 0)
            x_load(st + 1, 1)

        for s in range(NSUB):
            normA(xT[:, :, tsl_of(s)], 8, T, sq[:, s])
        for s in range(NSUB):
            t0 = st * TS + s * T
            tsl = tsl_of(s)
            normB(xT[:, :, tsl], outT_b[:, s], C_FIN, 8, D, T, sq[:, s])
            for j in range(4):
                for half in range(2):
                    b = bank()

                    def f(e, j=j, half=half, b=b, s=s):
                        for cc in range(4):
                            c = half * 4 + cc
                            ins = e.transpose(b[:, cc * 128:(cc + 1) * 128], outT_b[:, s, c, j * 128:(j + 1) * 128],
                                              ident)
                        return ins
                    S.add("pe", f, reads=[outT_b[:, s, half * 4:half * 4 + 4, j * 128:(j + 1) * 128], ident],
                          writes=[b])
                    dsts = yst_b[:, j, half * 512:(half + 1) * 512]
                    if (j + half) % 2 == 0:
                        S.add("act", (lambda e, dsts=dsts, b=b: e.copy(out=dsts, in_=b)), reads=[b], writes=[dsts])
                    else:
                        S.add("dve", (lambda e, dsts=dsts, b=b: e.tensor_copy(out=dsts, in_=b)), reads=[b], writes=[dsts])
            od = S.add("sp", (lambda e, t0=t0: e.dma_start(
                out=y_d[t0:t0 + T, :].rearrange("(j p) d -> p j d", p=128), in_=yst_b)),
                reads=[yst_b], dma_key="yout")
            out_dmas.append(od)

    S.add("sp", None, extra_deps=[out_dmas[-1]])
    assert ring_state["next_acq"] == len(groups), (ring_state, len(groups))

    S.finalize()
    sems = {e: nc.alloc_semaphore("sem_" + e) for e in Sched.CENG}
    dsems = {}
    for key in S.dma_cnt:
        dsems[key] = nc.alloc_semaphore("dsem_" + "_".join(str(k) for k in (key if isinstance(key, tuple) else (key,))))

    with nc.Block() as block:
        @block.tensor
        def _(e):
            S.emit("pe", e, sems, dsems)

        @block.scalar
        def _(e):
            S.emit("act", e, sems, dsems)

        @block.vector
        def _(e):
            S.emit("dve", e, sems, dsems)

        @block.gpsimd
        def _(e):
            S.emit("pool", e, sems, dsems)

        @block.sync
        def _(e):
            S.emit("sp", e, sems, dsems)
    return nc


def _host_layout(inputs):
    f = lambda a: np.ascontiguousarray(np.asarray(a, dtype=np.float32))
    col = lambda v, n: f(v).reshape(n, 128).T
    cw = f(inputs["conv_w"])
    cvec = np.concatenate([
        col(inputs["ln_mix_g"], 8), col(inputs["ln_attn_g"], 8), col(inputs["ln_ffn_g"], 8),
        col(inputs["ln_final_g"], 8), col(inputs["ln_mem_g"], 8),
        col(inputs["grp_norm_a"], 4), col(inputs["grp_norm_b"], 4),
        col(inputs["sgu_ln_g"], 4), col(inputs["sgu_ln_b"], 4),
        col(cw[0], 4), col(cw[1], 4), col(cw[2], 4),
    ], axis=1)
    assert cvec.shape == (128, NCV)
    wspT = f(np.transpose(f(inputs["w_spatial"]), (2, 0, 1)))
    bspb = f(np.broadcast_to(f(inputs["b_spatial"])[None, :, :], (128, 4, 128)))
    shared = {
        "w_in": f(inputs["w_in"]), "w_out": f(inputs["w_out"]), "w_q": f(inputs["w_q"]),
        "w_kv": f(inputs["w_kv"]), "w_o": f(inputs["w_o"]), "w_gate_up": f(inputs["w_gate_up"]),
        "w_down": f(inputs["w_down"]), "cvec": f(cvec), "wspT": wspT, "bspb": bspb,
    }
    x = f(inputs["x"])
    mem = f(inputs["mem"])
    in_maps = []
    for b in range(8):
        d = dict(shared)
        d["x"] = x[b]
        d["mem"] = mem[b]
        in_maps.append(d)
    return in_maps


def kernel(**inputs):
    in_maps = _host_layout(inputs)
    nc = build_program()
    res = run_bass_kernel_spmd(nc, in_maps, core_ids=list(range(8)))
    out = np.stack([np.asarray(r["y"], dtype=np.float32) for r in res.results], axis=0)
    return out
```

```python
import numpy as np
import concourse.bass as bass
import concourse.mybir as mybir
from concourse.bass_utils import run_bass_kernel_spmd

F32 = mybir.dt.float32
BF16 = mybir.dt.bfloat16
U8 = mybir.dt.uint8
AF = mybir.ActivationFunctionType
ALU = mybir.AluOpType

D = 1024
KD = 8
SEQ = 4096
TS = 1024
T = 512
NSUB = TS // T
NST = SEQ // TS
MEM = 256
DFF = 2816
KF = DFF // 128
EPS = 1e-6
NSLOT = 5
SLOT_B = 8192

C_MIX, C_ATT, C_FFN, C_FIN, C_MEM = 0, 8, 16, 24, 32
C_GA, C_GB, C_GLN, C_BLN, C_CW = 40, 44, 48, 52, 56
NCV = 68

_ES = {F32: 4, BF16: 2, U8: 1}


def _esize(dt):
    return _ES[dt]


def _hull(ap):
    es = _esize(ap.dtype)
    pairs = [tuple(p) for p in ap.ap]
    pstride = pairs[0][0]
    off = ap.offset % pstride if pstride else ap.offset
    ext = 1
    for st, cnt in pairs[1:]:
        ext += (cnt - 1) * abs(st)
    return ap.tensor.name, off * es, (off + ext) * es


class Op:
    __slots__ = ("eng", "fn", "idx", "waits", "signal", "sigval", "dma_key", "dma_cnt", "know", "gid")


class Sched:
    CENG = ("pe", "act", "dve", "pool")
    GRAN = 128

    def __init__(self):
        self.ops = []
        self.eng_ops = {e: [] for e in ("pe", "act", "dve", "pool", "sp")}
        self.nidx = {e: 0 for e in self.CENG}
        self.blocks = {}
        self.know = {e: {} for e in ("pe", "act", "dve", "pool", "sp")}
        self.dma_cnt = {}

    def _blocks(self, ap):
        name, a, b = _hull(ap)
        g = 2048 if name.startswith("ps") else self.GRAN
        return [(name, i) for i in range(a // g, (b - 1) // g + 1)]

    def add(self, eng, fn, reads=(), writes=(), dma_key=None, extra_deps=()):
        op = Op()
        op.eng = eng
        op.fn = fn
        op.dma_key = dma_key
        op.signal = False
        op.sigval = None
        op.waits = []
        op.gid = len(self.ops)
        is_dma = dma_key is not None
        if is_dma:
            self.dma_cnt[dma_key] = self.dma_cnt.get(dma_key, 0) + 16
            op.dma_cnt = self.dma_cnt[dma_key]
            op.idx = None
        else:
            op.dma_cnt = None
            if eng in self.CENG and fn is not None:
                op.idx = self.nidx[eng]
                self.nidx[eng] += 1
            else:
                op.idx = None
        deps = {}
        for d in extra_deps:
            deps[d] = True
        for ap in reads:
            for blk in self._blocks(ap):
                ent = self.blocks.get(blk)
                if ent is None:
                    ent = [None, {}, []]
                    self.blocks[blk] = ent
                if ent[0] is not None:
                    deps[ent[0]] = True
        for ap in writes:
            for blk in self._blocks(ap):
                ent = self.blocks.get(blk)
                if ent is None:
                    ent = [None, {}, []]
                    self.blocks[blk] = ent
                if ent[0] is not None and ent[0] not in deps:
                    deps[ent[0]] = False
                for r in ent[1].values():
                    if r not in deps:
                        deps[r] = False
                for r in ent[2]:
                    if r not in deps:
                        deps[r] = False
        deps.pop(op, None)
        kn = self.know[eng]
        for a in sorted(deps, key=lambda o: -o.gid):
            raw = deps[a]
            if a.dma_key is None:
                if a.idx is None:
                    continue
                if a.eng == eng and not is_dma:
                    if eng == "pe":
                        continue
                    if not raw:
                        continue
                key, val = a.eng, a.idx + 1
            else:
                key, val = ("d", a.dma_key), a.dma_cnt
            if kn.get(key, 0) >= val:
                continue
            op.waits.append(a)
            a.signal = True
            for k, v in a.know.items():
                if kn.get(k, 0) < v:
                    kn[k] = v
        op.know = dict(kn)
        if is_dma:
            op.know[("d", dma_key)] = op.dma_cnt
        elif op.idx is not None:
            op.know[eng] = op.idx + 1
        for ap in reads:
            for blk in self._blocks(ap):
                ent = self.blocks[blk]
                if is_dma or op.idx is None:
                    ent[2].append(op)
                else:
                    ent[1][eng] = op
        for ap in writes:
            for blk in self._blocks(ap):
                ent = self.blocks[blk]
                ent[0] = op
                ent[1] = {}
                ent[2] = []
        self.ops.append(op)
        self.eng_ops[eng].append(op)
        return op

    def finalize(self):
        for e in self.CENG:
            n = 0
            for op in self.eng_ops[e]:
                if op.dma_key is None and op.idx is not None and op.signal:
                    n += 1
                    op.sigval = n

    def emit(self, eng, handle, sems, dsems):
        for op in self.eng_ops[eng]:
            for a in op.waits:
                if a.dma_key is None:
                    handle.wait_ge(sems[a.eng], a.sigval)
                else:
                    handle.wait_ge(dsems[a.dma_key], a.dma_cnt)
            if op.fn is None:
                continue
            ins = op.fn(handle)
            if op.dma_key is not None:
                ins.then_inc(dsems[op.dma_key], 16)
            elif op.signal:
                ins.then_inc(sems[op.eng], 1)


def build_program():
    nc = bass.Bass("TRN2", target_bir_lowering=False)
    x_d = nc.dram_tensor("x", [SEQ, D], F32, kind="ExternalInput").ap()
    mem_d = nc.dram_tensor("mem", [MEM, D], F32, kind="ExternalInput").ap()
    w_in_d = nc.dram_tensor("w_in", [D, 2560], F32, kind="ExternalInput").ap()
    w_out_d = nc.dram_tensor("w_out", [D, D], F32, kind="ExternalInput").ap()
    w_q_d = nc.dram_tensor("w_q", [D, D], F32, kind="ExternalInput").ap()
    w_kv_d = nc.dram_tensor("w_kv", [D, 2 * D], F32, kind="ExternalInput").ap()
    w_o_d = nc.dram_tensor("w_o", [D, D], F32, kind="ExternalInput").ap()
    w_gu_d = nc.dram_tensor("w_gate_up", [D, 2 * DFF], F32, kind="ExternalInput").ap()
    w_dn_d = nc.dram_tensor("w_down", [DFF, D], F32, kind="ExternalInput").ap()
    cvec_d = nc.dram_tensor("cvec", [128, NCV], F32, kind="ExternalInput").ap()
    wspT_d = nc.dram_tensor("wspT", [128, 4, 128], F32, kind="ExternalInput").ap()
    bspb_d = nc.dram_tensor("bspb", [128, 4, 128], F32, kind="ExternalInput").ap()
    y_d = nc.dram_tensor("y", [SEQ, D], F32, kind="ExternalOutput").ap()

    top = [0]

    def alloc(nbytes, align=128):
        off = (top[0] + align - 1) // align * align
        top[0] = off + nbytes
        return off

    NSTD = 4
    SCR = 83968
    o_ident = alloc(512)
    o_ones = alloc(256)
    o_cvec = alloc(NCV * 4)
    o_wTm = alloc(1024)
    o_E = alloc(2048)
    o_KT = alloc(4096)
    o_Vt = alloc(4096)
    o_zc = alloc(32)
    o_small = alloc(1024)
    o_sq = alloc(16384)
    o_std = alloc(NSTD * 2048)
    o_xT = alloc(32768)
    o_xn = alloc(16384)
    o_ring = alloc(NSLOT * SLOT_B)
    o_scr = alloc(SCR)
    total = top[0]
    big = nc.alloc_sbuf_tensor("big", [128, total], U8)

    def view(off, dt, *shape):
        n = 1
        for s_ in shape:
            n *= s_
        ap = big[:, off:off + n * _esize(dt)].bitcast(dt)
        if len(shape) == 2:
            ap = ap.rearrange("p (a b) -> p a b", a=shape[0])
        elif len(shape) == 3:
            ap = ap.rearrange("p (a b c) -> p a b c", a=shape[0], b=shape[1])
        return ap

    ident = view(o_ident, F32, 128)
    ones = view(o_ones, BF16, 128)
    cvec = view(o_cvec, F32, NCV)
    wTm = view(o_wTm, BF16, 4, 128)
    E = view(o_E, F32, 4, 128)
    KT = view(o_KT, BF16, 8, 256)
    Vt = view(o_Vt, BF16, 2, 1024)
    zc = view(o_zc, F32, 4, 2)
    st6 = view(o_small, F32, 2, 4, 6)
    mv = view(o_small + 256, F32, 2, 4, 2)
    sdv = view(o_small + 384, F32, 2, 4)
    nmr = view(o_small + 448, F32, 2, 4)
    sq = view(o_sq, BF16, 2, 8, 512)
    stdb = view(o_std, F32, NSTD, 512)
    xT = view(o_xT, F32, 8, TS)
    xn = view(o_xn, BF16, 8, TS)
    u_b = view(o_scr + 0, F32, 4, TS)
    gc_b = view(o_scr + 16384, BF16, 4, TS)
    vn_b = view(o_scr + 24576, BF16, 8, 512)
    yn_b = view(o_scr + 32768, BF16, 8, TS)
    vg_b = view(o_scr + 49152, F32, 4, 512)
    z_b = view(o_scr + 57344, F32, 4, 514)
    yb_b = view(o_scr + 65664, F32, 4, TS)
    q_b = view(o_scr + 0, BF16, 8, TS)
    o_b = view(o_scr + 16384, BF16, 8, TS)
    p_b = view(o_scr + 32768, BF16, 2, 8, 512)
    rden_b = view(o_scr + 49152, F32, 2, 4, 512)
    h_b = view(o_scr + 0, BF16, KF, TS)
    sg_b = view(o_scr + 45056, F32, 2, 512)
    xin_b = view(o_scr + 0, F32, 2, 4, D)
    outT_b = view(o_scr + 32768, F32, 2, 8, 512)
    yst_b = view(o_scr + 65536, F32, 4, D)
    memT = view(o_scr + 32768, F32, 8, MEM)
    memn = view(o_scr + 40960, BF16, 8, MEM)
    wTf = view(o_scr + 49152, F32, 4, 128)
    bspb = view(o_scr + 51200, F32, 4, 128)

    ps = nc.alloc_psum_tensor("ps", [128, 4096], F32)
    bank_ctr = [0]

    def bank():
        b = bank_ctr[0] % 8
        bank_ctr[0] += 1
        return ps[:, b * 512:(b + 1) * 512]

    S = Sched()

    groups = []

    def slot_view(si, dt, *shape):
        return view(o_ring + si * SLOT_B, dt, *shape)

    def a_group(w_d, c0):
        return ("A", [(lambda si: slot_view(si, BF16, 8, 512),
                       w_d[:, c0:c0 + 512].rearrange("(k p) n -> p k n", p=128))])

    def gu_group(i):
        def dv(half):
            return lambda si: view(o_ring + si * SLOT_B + half * 4096, BF16, 8, 256)
        return ("GU", [(dv(0), w_gu_d[:, 256 * i:256 * i + 256].rearrange("(k p) n -> p k n", p=128)),
                       (dv(1), w_gu_d[:, DFF + 256 * i:DFF + 256 * i + 256].rearrange("(k p) n -> p k n", p=128))])

    def dn_group(m):
        def dv(half):
            return lambda si: view(o_ring + si * SLOT_B + half * 11 * 256, BF16, 11, 128)
        return ("DN", [(dv(0), w_dn_d[0:11 * 128, m * 128:(m + 1) * 128].rearrange("(k p) n -> p k n", p=128)),
                       (dv(1), w_dn_d[11 * 128:22 * 128, m * 128:(m + 1) * 128].rearrange("(k p) n -> p k n", p=128))])

    WIN_ORDER = (3, 4, 1, 0, 2)
    for i in range(4):
        groups.append(a_group(w_kv_d, 512 * i))
    for st in range(NST):
        for i in WIN_ORDER:
            groups.append(a_group(w_in_d, 512 * i))
        for i in range(2):
            groups.append(a_group(w_out_d, 512 * i))
        for i in range(2):
            groups.append(a_group(w_q_d, 512 * i))
        for i in range(2):
            groups.append(a_group(w_o_d, 512 * i))
        for i in range(11):
            groups.append(gu_group(i))
        for m in range(8):
            groups.append(dn_group(m))

    ring_state = {"next_dma": 0, "next_acq": 0}

    def ring_issue(gi):
        kind, parts = groups[gi]
        si = gi % NSLOT
        for dvf, src in parts:
            dst = dvf(si)
            S.add("pool", (lambda e, dst=dst, src=src: e.dma_start(out=dst, in_=src)),
                  writes=[dst], dma_key=("ring", si))

    held = []
    sticky = set()

    def ring_pump():
        base = held[0] if held else ring_state["next_acq"]
        want = min(base + NSLOT - 1, len(groups) - 1)
        while ring_state["next_dma"] <= want:
            ring_issue(ring_state["next_dma"])
            ring_state["next_dma"] += 1

    def ring_acquire(stick=False):
        gi = ring_state["next_acq"]
        ring_state["next_acq"] += 1
        held[:] = [h for h in held if h in sticky]
        held.append(gi)
        if stick:
            sticky.add(gi)
        ring_pump()
        kind, _ = groups[gi]
        si = gi % NSLOT
        if kind == "A":
            return slot_view(si, BF16, 8, 512)
        if kind == "GU":
            return slot_view(si, BF16, 2, 8, 256)
        return slot_view(si, BF16, KF, 128)

    def ring_unstick():
        sticky.clear()

    std_ctr = [0]

    def normA(src, nch, Tn, sqv):
        for c in range(nch):
            S.add("act", (lambda e, c=c: e.activation(out=sqv[:, c, 0:Tn], in_=src[:, c, :], func=AF.Square)),
                  reads=[src[:, c, :]], writes=[sqv[:, c, 0:Tn]])

    def normB(src, dst, gbase, nch, N, Tn, sqv):
        b = bank()

        def f(e):
            for c in range(nch):
                ins = e.matmul(b[:, 0:Tn], lhsT=ones, rhs=sqv[:, c, 0:Tn], start=(c == 0), stop=(c == nch - 1))
            return ins
        S.add("pe", f, reads=[ones, sqv[:, 0:nch, 0:Tn]], writes=[b])
        sd = stdb[:, std_ctr[0] % NSTD, 0:Tn]
        std_ctr[0] += 1
        S.add("act", (lambda e: e.activation(out=sd, in_=b[:, 0:Tn], func=AF.Sqrt, bias=EPS, scale=1.0 / N)),
              reads=[b], writes=[sd])
        S.add("dve", (lambda e: e.reciprocal(out=sd, in_=sd)), reads=[sd], writes=[sd])
        for c in range(nch):
            S.add("dve", (lambda e, c=c: e.scalar_tensor_tensor(
                out=dst[:, c, :], in0=src[:, c, :], scalar=cvec[:, gbase + c:gbase + c + 1], in1=sd,
                op0=ALU.mult, op1=ALU.mult)),
                reads=[src[:, c, :], cvec, sd], writes=[dst[:, c, :]])

    def mm_w(b, wfn, rhsfn, nk, ncols=T):
        def f(e):
            for k in range(nk):
                ins = e.matmul(b[:, 0:ncols], lhsT=wfn(k), rhs=rhsfn(k), start=(k == 0), stop=(k == nk - 1))
            return ins
        return f

    def tsl_of(s):
        return slice(s * T, (s + 1) * T)

    S.add("sp", lambda e: e.dma_start(out=cvec, in_=cvec_d), writes=[cvec], dma_key="cvec")
    S.add("sp", lambda e: e.dma_start(out=wTf, in_=wspT_d), writes=[wTf], dma_key="wTf")
    S.add("sp", lambda e: e.dma_start(out=bspb, in_=bspb_d), writes=[bspb], dma_key="bspb")
    S.add("sp", lambda e: e.dma_start(out=yst_b[:, 0:2, :], in_=mem_d.rearrange("(j p) d -> p j d", p=128)),
          writes=[yst_b[:, 0:2, :]], dma_key="memin")

    def x_load(st, s):
        t0 = st * TS + s * T
        S.add("sp", (lambda e: e.dma_start(
            out=xin_b[:, s, :, :], in_=x_d[t0:t0 + T, :].rearrange("(j p) d -> p j d", p=128))),
            writes=[xin_b[:, s, :, :]], dma_key=("xin", s))

    x_load(0, 0)
    x_load(0, 1)
    S.add("pool", lambda e: e.memset(ident, 0.0), writes=[ident])
    S.add("pool", lambda e: e.affine_select(out=ident, in_=ident, pattern=[[-1, 128]], compare_op=ALU.not_equal,
                                            fill=1.0, base=0, channel_multiplier=1),
          reads=[ident], writes=[ident])
    S.add("pool", lambda e: e.memset(ones, 1.0), writes=[ones])
    S.add("pool", lambda e: e.memset(zc, 0.0), writes=[zc])
    for h in range(4):
        S.add("pool", (lambda e, h=h: e.affine_select(out=wTf[:, h, :], in_=wTf[:, h, :], pattern=[[1, 128]],
                                                      compare_op=ALU.is_ge, fill=0.0, base=0, channel_multiplier=-1)),
              reads=[wTf[:, h, :]], writes=[wTf[:, h, :]])
    S.add("dve", lambda e: e.tensor_copy(out=wTm, in_=wTf), reads=[wTf], writes=[wTm])
    b = bank()

    def f_rw(e, b=b):
        for h in range(4):
            ins = e.matmul(b[:, h * 128:(h + 1) * 128], lhsT=ones, rhs=wTm[:, h, :], start=True, stop=True)
        return ins
    S.add("pe", f_rw, reads=[ones, wTm], writes=[b])
    for h in range(4):
        S.add("dve", (lambda e, h=h, b=b: e.scalar_tensor_tensor(
            out=E[:, h, :], in0=b[:, h * 128:(h + 1) * 128], scalar=cvec[:, C_BLN + h:C_BLN + h + 1],
            in1=bspb[:, h, :], op0=ALU.mult, op1=ALU.add)),
            reads=[b, cvec, bspb[:, h, :]], writes=[E[:, h, :]])

    for c in range(8):
        b = bank()

        def f(e, c=c, b=b):
            for j in range(2):
                ins = e.transpose(b[:, j * 128:(j + 1) * 128], yst_b[:, j, c * 128:(c + 1) * 128], ident)
            return ins
        S.add("pe", f, reads=[yst_b[:, 0:2, c * 128:(c + 1) * 128], ident], writes=[b])
        if c % 2 == 0:
            S.add("act", (lambda e, c=c, b=b: e.copy(out=memT[:, c, :], in_=b[:, 0:MEM])),
                  reads=[b], writes=[memT[:, c, :]])
        else:
            S.add("dve", (lambda e, c=c, b=b: e.tensor_copy(out=memT[:, c, :], in_=b[:, 0:MEM])),
                  reads=[b], writes=[memT[:, c, :]])
    normA(memT, 8, MEM, sq[:, 0])
    normB(memT, memn, C_MEM, 8, D, MEM, sq[:, 0])
    for gi in range(2):
        g = ring_acquire()
        for mm in range(4):
            c = gi * 4 + mm
            b = bank()
            S.add("pe", mm_w(b, (lambda k, g=g, mm=mm: g[:, k, mm * 128:(mm + 1) * 128]),
                             (lambda k: memn[:, k, :]), 8, ncols=MEM),
                  reads=[g, memn], writes=[b])
            S.add("act", (lambda e, c=c, b=b: e.copy(out=KT[:, c, :], in_=b[:, 0:MEM])),
                  reads=[b], writes=[KT[:, c, :]])
    for gi in range(2):
        g = ring_acquire()
        for mc in range(2):
            b = bank()
            S.add("pe", mm_w(b, (lambda k, mc=mc: memn[:, k, mc * 128:(mc + 1) * 128]),
                             (lambda k, g=g: g[:, k, :]), 8, ncols=512),
                  reads=[g, memn], writes=[b])
            S.add("act", (lambda e, mc=mc, gi=gi, b=b: e.copy(out=Vt[:, mc, gi * 512:(gi + 1) * 512], in_=b)),
                  reads=[b], writes=[Vt[:, mc, gi * 512:(gi + 1) * 512]])

    def wgroup(g, s, src, evac, nmm=4):
        tsl = tsl_of(s)
        for mm in range(nmm):
            b = bank()
            S.add("pe", mm_w(b, (lambda k, g=g, mm=mm: g[:, k, mm * 128:(mm + 1) * 128]),
                             (lambda k, tsl=tsl: src[:, k, tsl]), 8),
                  reads=[g, src[:, :, tsl]], writes=[b])
            evac(mm, b)

    def wgroup_k(wfn, s, src, evac, nmm=4):
        tsl = tsl_of(s)
        banks = [bank() for _ in range(nmm)]
        for k in range(8):
            def f(e, k=k):
                for mm in range(nmm):
                    ins = e.matmul(banks[mm], lhsT=wfn(k, mm), rhs=src[:, k, tsl], start=(k == 0), stop=(k == 7))
                return ins
            S.add("pe", f, reads=[wfn(k, mm) for mm in range(nmm)] + [src[:, k, tsl]], writes=banks)
        for mm in range(nmm):
            evac(mm, banks[mm])

    def proj_residual(src_b, after_s):
        g0 = ring_acquire(stick=True)
        g1 = ring_acquire()
        for s in range(NSUB):
            tsl = tsl_of(s)
            for gi, g in enumerate((g0, g1)):
                def ev(mm, b, gi=gi, tsl=tsl):
                    m = gi * 4 + mm
                    S.add("dve", (lambda e: e.tensor_tensor(out=xT[:, m, tsl], in0=b, in1=xT[:, m, tsl], op=ALU.add)),
                          reads=[b, xT[:, m, tsl]], writes=[xT[:, m, tsl]])
                wgroup(g, s, src_b, ev)
            after_s(s)
        ring_unstick()

    out_dmas = []
    for st in range(NST):
        for s in range(NSUB):
            tsl = tsl_of(s)
            for c in range(8):
                b = bank()

                def f(e, c=c, b=b, s=s):
                    for j in range(4):
                        ins = e.transpose(b[:, j * 128:(j + 1) * 128], xin_b[:, s, j, c * 128:(c + 1) * 128], ident)
                    return ins
                S.add("pe", f, reads=[xin_b[:, s, :, c * 128:(c + 1) * 128], ident], writes=[b])
                if c % 2 == 0:
                    S.add("act", (lambda e, c=c, b=b, tsl=tsl: e.copy(out=xT[:, c, tsl], in_=b)),
                          reads=[b], writes=[xT[:, c, tsl]])
                else:
                    S.add("dve", (lambda e, c=c, b=b, tsl=tsl: e.tensor_copy(out=xT[:, c, tsl], in_=b)),
                          reads=[b], writes=[xT[:, c, tsl]])
            normA(xT[:, :, tsl], 8, T, sq[:, s])

        g = ring_acquire()
        for s in range(NSUB):
            tsl = tsl_of(s)
            normB(xT[:, :, tsl], xn[:, :, tsl], C_MIX, 8, D, T, sq[:, s])
        for s in range(NSUB):
            tsl = tsl_of(s)

            def ev(mm, b, tsl=tsl):
                S.add("act", (lambda e: e.copy(out=gc_b[:, mm, tsl], in_=b)), reads=[b], writes=[gc_b[:, mm, tsl]])
            wgroup_k((lambda k, mm, g=g: g[:, k, mm * 128:(mm + 1) * 128]), s, xn, ev)

        g_val = ring_acquire(stick=True)

        def val_part(s):
            tsl = tsl_of(s)
            S.add("dve", (lambda e: e.tensor_copy(out=z_b[:, :, 0:2], in_=zc)), reads=[zc], writes=[z_b[:, :, 0:2]])

            def ev(mm, b):
                S.add("dve", (lambda e: e.tensor_tensor(out=z_b[:, mm, 2:514], in0=b, in1=gc_b[:, mm, tsl],
                                                        op=ALU.mult)),
                      reads=[b, gc_b[:, mm, tsl]], writes=[z_b[:, mm, 2:514]])
            wgroup(g_val, s, xn, ev)
            S.add("dve", (lambda e: e.tensor_copy(out=zc, in_=z_b[:, :, 512:514])),
                  reads=[z_b[:, :, 512:514]], writes=[zc])
            for m in range(4):
                S.add("dve", (lambda e, m=m: e.tensor_scalar(out=yb_b[:, m, tsl], in0=z_b[:, m, 0:512],
                                                             scalar1=cvec[:, C_CW + m:C_CW + m + 1], scalar2=None,
                                                             op0=ALU.mult)),
                      reads=[z_b[:, m, 0:512], cvec], writes=[yb_b[:, m, tsl]])
                for jj in (1, 2):
                    S.add("dve", (lambda e, m=m, jj=jj: e.scalar_tensor_tensor(
                        out=yb_b[:, m, tsl], in0=z_b[:, m, jj:jj + 512],
                        scalar=cvec[:, C_CW + jj * 4 + m:C_CW + jj * 4 + m + 1], in1=yb_b[:, m, tsl],
                        op0=ALU.mult, op1=ALU.add)),
                        reads=[z_b[:, m, jj:jj + 512], cvec, yb_b[:, m, tsl]], writes=[yb_b[:, m, tsl]])
        val_part(0)

        g = ring_acquire()
        for s in range(NSUB):
            for j in range(4):
                b = bank()
                S.add("pe", mm_w(b, (lambda k, s=s, j=j: xn[:, k, s * T + j * 128:s * T + (j + 1) * 128]),
                                 (lambda k, g=g: g[:, k, :]), 8),
                      reads=[g, xn[:, :, s * T + j * 128:s * T + (j + 1) * 128]], writes=[b])
                S.add("act", (lambda e, j=j, b=b: e.activation(out=vg_b[:, j, :], in_=b, func=AF.Gelu_apprx_tanh)),
                      reads=[b], writes=[vg_b[:, j, :]])
                S.add("dve", (lambda e, j=j, s=s: e.bn_stats(out=st6[:, s, j, :], in_=vg_b[:, j, :])),
                      reads=[vg_b[:, j, :]], writes=[st6[:, s, j, :]])
                S.add("dve", (lambda e, j=j, s=s: e.bn_aggr(out=mv[:, s, j, :], in_=st6[:, s, j, :])),
                      reads=[st6[:, s, j, :]], writes=[mv[:, s, j, :]])
            S.add("act", (lambda e, s=s: e.activation(out=sdv[:, s, :], in_=mv[:, s, :, 1], func=AF.Sqrt, bias=EPS,
                                                      scale=1.0)),
                  reads=[mv[:, s]], writes=[sdv[:, s, :]])
            S.add("dve", (lambda e, s=s: e.reciprocal(out=sdv[:, s, :], in_=sdv[:, s, :])),
                  reads=[sdv[:, s, :]], writes=[sdv[:, s, :]])
            S.add("dve", (lambda e, s=s: e.scalar_tensor_tensor(out=nmr[:, s, :], in0=mv[:, s, :, 0], scalar=-1.0,
                                                                in1=sdv[:, s, :], op0=ALU.mult, op1=ALU.mult)),
                  reads=[mv[:, s], sdv[:, s, :]], writes=[nmr[:, s, :]])
            for j in range(4):
                S.add("act", (lambda e, s=s, j=j: e.activation(out=vn_b[:, s * 4 + j, :], in_=vg_b[:, j, :],
                                                               func=AF.Identity, bias=nmr[:, s, j:j + 1],
                                                               scale=sdv[:, s, j:j + 1])),
                      reads=[vg_b[:, j, :], nmr[:, s, :], sdv[:, s, :]], writes=[vn_b[:, s * 4 + j, :]])
        val_part(1)
        ring_unstick()

        g = ring_acquire()
        for s in range(NSUB):
            tsl = tsl_of(s)

            def ev(mm, b, tsl=tsl):
                S.add("act", (lambda e: e.activation(out=u_b[:, mm, tsl], in_=b, func=AF.Gelu_apprx_tanh)),
                      reads=[b], writes=[u_b[:, mm, tsl]])
            wgroup(g, s, xn, ev)

        for s in range(NSUB):
            tsl = tsl_of(s)
            for h in range(4):
                b = bank()

                def f(e, s=s, h=h, b=b):
                    for j in range(4):
                        ins = e.matmul(b[:, j * 128:(j + 1) * 128], lhsT=vn_b[:, s * 4 + j, h * 128:(h + 1) * 128],
                                       rhs=wTm[:, h, :], start=True, stop=True)
                    return ins
                S.add("pe", f, reads=[vn_b[:, s * 4:(s + 1) * 4, h * 128:(h + 1) * 128], wTm[:, h, :]], writes=[b])
                tmp = vg_b[:, h, :]
                S.add("dve", (lambda e, h=h, b=b, tmp=tmp: e.scalar_tensor_tensor(
                    out=tmp.rearrange("p (j t) -> p j t", j=4),
                    in0=b.rearrange("p (j t) -> p j t", j=4),
                    scalar=cvec[:, C_GLN + h:C_GLN + h + 1],
                    in1=E[:, h, :].unsqueeze(1).to_broadcast([128, 4, 128]),
                    op0=ALU.mult, op1=ALU.add)),
                    reads=[b, cvec, E[:, h, :]], writes=[tmp])
                S.add("dve", (lambda e, h=h, tsl=tsl, tmp=tmp: e.tensor_tensor(out=u_b[:, h, tsl], in0=tmp,
                                                                                in1=u_b[:, h, tsl], op=ALU.mult)),
                      reads=[tmp, u_b[:, h, tsl]], writes=[u_b[:, h, tsl]])
            normA(u_b[:, :, tsl], 4, T, sq[:, s, 0:4])

        g = ring_acquire()
        for s in range(NSUB):
            tsl = tsl_of(s)
            normB(u_b[:, :, tsl], yn_b[:, 0:4, tsl], C_GA, 4, 512, T, sq[:, s, 0:4])
        for s in range(NSUB):
            tsl = tsl_of(s)

            def ev(mm, b, tsl=tsl):
                S.add("dve", (lambda e: e.tensor_tensor(out=yb_b[:, mm, tsl], in0=b, in1=yb_b[:, mm, tsl],
                                                        op=ALU.mult)),
                      reads=[b, yb_b[:, mm, tsl]], writes=[yb_b[:, mm, tsl]])
            wgroup(g, s, xn, ev)
            normA(yb_b[:, :, tsl], 4, T, sq[:, s, 4:8])

        nb_done = [False, False]

        def after_wout(s):
            normA(xT[:, :, tsl_of(s)], 8, T, sq[:, s])

        g0 = ring_acquire(stick=True)
        g1 = ring_acquire()
        for s in range(NSUB):
            tsl = tsl_of(s)
            normB(yb_b[:, :, tsl], yn_b[:, 4:8, tsl], C_GB, 4, 512, T, sq[:, s, 4:8])
        for s in range(NSUB):
            tsl = tsl_of(s)
            for gi, g in enumerate((g0, g1)):
                def ev(mm, b, gi=gi, tsl=tsl):
                    m = gi * 4 + mm
                    S.add("dve", (lambda e: e.tensor_tensor(out=xT[:, m, tsl], in0=b, in1=xT[:, m, tsl], op=ALU.add)),
                          reads=[b, xT[:, m, tsl]], writes=[xT[:, m, tsl]])
                wgroup(g, s, yn_b, ev)
            after_wout(s)
        ring_unstick()

        for s in range(NSUB):
            tsl = tsl_of(s)
            normB(xT[:, :, tsl], xn[:, :, tsl], C_ATT, 8, D, T, sq[:, s])
        for gi in range(2):
            g = ring_acquire()
            for s in range(NSUB):
                tsl = tsl_of(s)

                def ev(mm, b, gi=gi, tsl=tsl):
                    m = gi * 4 + mm
                    S.add("act", (lambda e: e.activation(out=q_b[:, m, tsl], in_=b, func=AF.Copy, scale=0.0625)),
                          reads=[b], writes=[q_b[:, m, tsl]])
                if gi == 0:
                    wgroup_k((lambda k, mm, g=g: g[:, k, mm * 128:(mm + 1) * 128]), s, xn, ev)
                else:
                    wgroup(g, s, xn, ev)
        for s in range(NSUB):
            tsl = tsl_of(s)
            for h in range(4):
                for mc in range(2):
                    b = bank()

                    def f(e, h=h, mc=mc, b=b, tsl=tsl):
                        for half in range(2):
                            ins = e.matmul(b, lhsT=KT[:, h * 2 + half, mc * 128:(mc + 1) * 128],
                                           rhs=q_b[:, h * 2 + half, tsl], start=(half == 0), stop=(half == 1))
                        return ins
                    S.add("pe", f, reads=[KT[:, h * 2:h * 2 + 2, :], q_b[:, h * 2:h * 2 + 2, tsl]], writes=[b])
                    S.add("act", (lambda e, h=h, mc=mc, b=b, s=s: e.activation(out=p_b[:, s, h * 2 + mc, :], in_=b,
                                                                               func=AF.Exp)),
                          reads=[b], writes=[p_b[:, s, h * 2 + mc, :]])
        for s in range(NSUB):
            tsl = tsl_of(s)
            for h in range(4):
                b = bank()

                def f(e, h=h, b=b, s=s):
                    for mc in range(2):
                        ins = e.matmul(b, lhsT=ones, rhs=p_b[:, s, h * 2 + mc, :], start=(mc == 0), stop=(mc == 1))
                    return ins
                S.add("pe", f, reads=[ones, p_b[:, s, h * 2:h * 2 + 2, :]], writes=[b])
                S.add("dve", (lambda e, h=h, b=b, s=s: e.reciprocal(out=rden_b[:, s, h, :], in_=b)),
                      reads=[b], writes=[rden_b[:, s, h, :]])
            for c in range(8):
                h = c // 2
                b = bank()

                def f(e, c=c, h=h, b=b, s=s):
                    for mc in range(2):
                        ins = e.matmul(b, lhsT=Vt[:, mc, c * 128:(c + 1) * 128], rhs=p_b[:, s, h * 2 + mc, :],
                                       start=(mc == 0), stop=(mc == 1))
                    return ins
                S.add("pe", f, reads=[Vt[:, :, c * 128:(c + 1) * 128], p_b[:, s, h * 2:h * 2 + 2, :]], writes=[b])
                S.add("dve", (lambda e, c=c, h=h, b=b, tsl=tsl, s=s: e.tensor_tensor(
                    out=o_b[:, c, tsl], in0=b, in1=rden_b[:, s, h, :], op=ALU.mult)),
                    reads=[b, rden_b[:, s, h, :]], writes=[o_b[:, c, tsl]])
        proj_residual(o_b, after_wout)

        sg_ctr = [0]
        for s in range(NSUB):
            tsl = tsl_of(s)
            normB(xT[:, :, tsl], xn[:, :, tsl], C_FFN, 8, D, T, sq[:, s])

        def swiglu(bg, bu, j, tsl):
            sg = sg_b[:, sg_ctr[0] % 2, :]
            sg_ctr[0] += 1
            S.add("act", (lambda e: e.activation(out=sg, in_=bg, func=AF.Silu)), reads=[bg], writes=[sg])
            S.add("dve", (lambda e: e.tensor_tensor(out=h_b[:, j, tsl], in0=bu, in1=sg, op=ALU.mult)),
                  reads=[bu, sg], writes=[h_b[:, j, tsl]])

        for gi in range(11):
            g = ring_acquire()
            for s in range(NSUB):
                tsl = tsl_of(s)
                if gi == 0:
                    got = {}

                    def ev(mm, b, got=got, tsl=tsl, gi=gi):
                        got[mm] = b
                        if mm % 2 == 1:
                            swiglu(got[mm - 1], b, gi * 2 + mm // 2, tsl)
                    wgroup_k((lambda k, mm, g=g: g[:, mm % 2, k, (mm // 2) * 128:(mm // 2 + 1) * 128]), s, xn, ev)
                    continue
                for i in range(2):
                    bg = bank()
                    S.add("pe", mm_w(bg, (lambda k, g=g, i=i: g[:, 0, k, i * 128:(i + 1) * 128]),
                                     (lambda k, tsl=tsl: xn[:, k, tsl]), 8),
                          reads=[g, xn[:, :, tsl]], writes=[bg])
                    bu = bank()
                    S.add("pe", mm_w(bu, (lambda k, g=g, i=i: g[:, 1, k, i * 128:(i + 1) * 128]),
                                     (lambda k, tsl=tsl: xn[:, k, tsl]), 8),
                          reads=[g, xn[:, :, tsl]], writes=[bu])
                    swiglu(bg, bu, gi * 2 + i, tsl)
        for m in range(8):
            g = ring_acquire()
            for s in range(NSUB):
                tsl = tsl_of(s)
                b = bank()
                S.add("pe", mm_w(b, (lambda k, g=g: g[:, k, :]), (lambda k, tsl=tsl: h_b[:, k, tsl]), KF),
                      reads=[g, h_b[:, :, tsl]], writes=[b])
                S.add("dve", (lambda e, m=m, b=b, tsl=tsl: e.tensor_tensor(
                    out=xT[:, m, tsl], in0=b, in1=xT[:, m, tsl], op=ALU.add)),
                    reads=[b, xT[:, m, tsl]], writes=[xT[:, m, tsl]])

        if st + 1 < NST:
            x_load(st + 1, 0)
            x_load(st + 1, 1)

        for s in range(NSUB):
            normA(xT[:, :, tsl_of(s)], 8, T, sq[:, s])
        for s in range(NSUB):
            tsl = tsl_of(s)
            normB(xT[:, :, tsl], outT_b[:, s], C_FIN, 8, D, T, sq[:, s])
        for s in range(NSUB):
            t0 = st * TS + s * T
            tsl = tsl_of(s)
            for j in range(4):
                for half in range(2):
                    b = bank()

                    def f(e, j=j, half=half, b=b, s=s):
                        for cc in range(4):
                            c = half * 4 + cc
                            ins = e.transpose(b[:, cc * 128:(cc + 1) * 128], outT_b[:, s, c, j * 128:(j + 1) * 128],
                                              ident)
                        return ins
                    S.add("pe", f, reads=[outT_b[:, s, half * 4:half * 4 + 4, j * 128:(j + 1) * 128], ident],
                          writes=[b])
                    dsts = yst_b[:, j, half * 512:(half + 1) * 512]
                    if (j + half) % 2 == 0:
                        S.add("act", (lambda e, dsts=dsts, b=b: e.copy(out=dsts, in_=b)), reads=[b], writes=[dsts])
                    else:
                        S.add("dve", (lambda e, dsts=dsts, b=b: e.tensor_copy(out=dsts, in_=b)), reads=[b], writes=[dsts])
            od = S.add("sp", (lambda e, t0=t0: e.dma_start(
                out=y_d[t0:t0 + T, :].rearrange("(j p) d -> p j d", p=128), in_=yst_b)),
                reads=[yst_b], dma_key="yout")
            out_dmas.append(od)

    S.add("sp", None, extra_deps=[out_dmas[-1]])
    assert ring_state["next_acq"] == len(groups), (ring_state, len(groups))

    S.finalize()
    sems = {e: nc.alloc_semaphore("sem_" + e) for e in Sched.CENG}
    dsems = {}
    for key in S.dma_cnt:
        dsems[key] = nc.alloc_semaphore("dsem_" + "_".join(str(k) for k in (key if isinstance(key, tuple) else (key,))))

    with nc.Block() as block:
        @block.tensor
        def _(e):
            S.emit("pe", e, sems, dsems)

        @block.scalar
        def _(e):
            S.emit("act", e, sems, dsems)

        @block.vector
        def _(e):
            S.emit("dve", e, sems, dsems)

        @block.gpsimd
        def _(e):
            S.emit("pool", e, sems, dsems)

        @block.sync
        def _(e):
            S.emit("sp", e, sems, dsems)
    return nc


def _host_layout(inputs):
    f = lambda a: np.ascontiguousarray(np.asarray(a, dtype=np.float32))
    col = lambda v, n: f(v).reshape(n, 128).T
    cw = f(inputs["conv_w"])
    cvec = np.concatenate([
        col(inputs["ln_mix_g"], 8), col(inputs["ln_attn_g"], 8), col(inputs["ln_ffn_g"], 8),
        col(inputs["ln_final_g"], 8), col(inputs["ln_mem_g"], 8),
        col(inputs["grp_norm_a"], 4), col(inputs["grp_norm_b"], 4),
        col(inputs["sgu_ln_g"], 4), col(inputs["sgu_ln_b"], 4),
        col(cw[0], 4), col(cw[1], 4), col(cw[2], 4),
    ], axis=1)
    assert cvec.shape == (128, NCV)
    wspT = f(np.transpose(f(inputs["w_spatial"]), (2, 0, 1)))
    bspb = f(np.broadcast_to(f(inputs["b_spatial"])[None, :, :], (128, 4, 128)))
    shared = {
        "w_in": f(inputs["w_in"]), "w_out": f(inputs["w_out"]), "w_q": f(inputs["w_q"]),
        "w_kv": f(inputs["w_kv"]), "w_o": f(inputs["w_o"]), "w_gate_up": f(inputs["w_gate_up"]),
        "w_down": f(inputs["w_down"]), "cvec": f(cvec), "wspT": wspT, "bspb": bspb,
    }
    x = f(inputs["x"])
    mem = f(inputs["mem"])
    in_maps = []
    for b in range(8):
        d = dict(shared)
        d["x"] = x[b]
        d["mem"] = mem[b]
        in_maps.append(d)
    return in_maps


def kernel(**inputs):
    in_maps = _host_layout(inputs)
    nc = build_program()
    res = run_bass_kernel_spmd(nc, in_maps, core_ids=list(range(8)))
    out = np.stack([np.asarray(r["y"], dtype=np.float32) for r in res.results], axis=0)
    return out
```

```python
import numpy as np
import concourse.bass as bass
import concourse.mybir as mybir
from concourse.bass_utils import run_bass_kernel_spmd

F32 = mybir.dt.float32
BF16 = mybir.dt.bfloat16
U8 = mybir.dt.uint8
AF = mybir.ActivationFunctionType
ALU = mybir.AluOpType

D = 1024
KD = 8
SEQ = 4096
TS = 1024
T = 512
NSUB = TS // T
NST = SEQ // TS
MEM = 256
DFF = 2816
KF = DFF // 128
EPS = 1e-6
NSLOT = 5
SLOT_B = 8192

C_MIX, C_ATT, C_FFN, C_FIN, C_MEM = 0, 8, 16, 24, 32
C_GA, C_GB, C_GLN, C_BLN, C_CW = 40, 44, 48, 52, 56
NCV = 68

_ES = {F32: 4, BF16: 2, U8: 1}


def _esize(dt):
    return _ES[dt]


def _hull(ap):
    es = _esize(ap.dtype)
    pairs = [tuple(p) for p in ap.ap]
    pstride = pairs[0][0]
    off = ap.offset % pstride if pstride else ap.offset
    ext = 1
    for st, cnt in pairs[1:]:
        ext += (cnt - 1) * abs(st)
    return ap.tensor.name, off * es, (off + ext) * es


class Op:
    __slots__ = ("eng", "fn", "idx", "waits", "signal", "sigval", "dma_key", "dma_cnt", "know", "gid")


class Sched:
    CENG = ("pe", "act", "dve", "pool")
    GRAN = 128

    def __init__(self):
        self.ops = []
        self.eng_ops = {e: [] for e in ("pe", "act", "dve", "pool", "sp")}
        self.nidx = {e: 0 for e in self.CENG}
        self.blocks = {}
        self.know = {e: {} for e in ("pe", "act", "dve", "pool", "sp")}
        self.dma_cnt = {}

    def _blocks(self, ap):
        name, a, b = _hull(ap)
        g = 2048 if name.startswith("ps") else self.GRAN
        return [(name, i) for i in range(a // g, (b - 1) // g + 1)]

    def add(self, eng, fn, reads=(), writes=(), dma_key=None, extra_deps=()):
        op = Op()
        op.eng = eng
        op.fn = fn
        op.dma_key = dma_key
        op.signal = False
        op.sigval = None
        op.waits = []
        op.gid = len(self.ops)
        is_dma = dma_key is not None
        if is_dma:
            self.dma_cnt[dma_key] = self.dma_cnt.get(dma_key, 0) + 16
            op.dma_cnt = self.dma_cnt[dma_key]
            op.idx = None
        else:
            op.dma_cnt = None
            if eng in self.CENG and fn is not None:
                op.idx = self.nidx[eng]
                self.nidx[eng] += 1
            else:
                op.idx = None
        deps = {}
        for d in extra_deps:
            deps[d] = True
        for ap in reads:
            for blk in self._blocks(ap):
                ent = self.blocks.get(blk)
                if ent is None:
                    ent = [None, {}, []]
                    self.blocks[blk] = ent
                if ent[0] is not None:
                    deps[ent[0]] = True
        for ap in writes:
            for blk in self._blocks(ap):
                ent = self.blocks.get(blk)
                if ent is None:
                    ent = [None, {}, []]
                    self.blocks[blk] = ent
                if ent[0] is not None and ent[0] not in deps:
                    deps[ent[0]] = False
                for r in ent[1].values():
                    if r not in deps:
                        deps[r] = False
                for r in ent[2]:
                    if r not in deps:
                        deps[r] = False
        deps.pop(op, None)
        kn = self.know[eng]
        for a in sorted(deps, key=lambda o: -o.gid):
            raw = deps[a]
            if a.dma_key is None:
                if a.idx is None:
                    continue
                if a.eng == eng and not is_dma:
                    if eng == "pe":
                        continue
                    if not raw:
                        continue
                key, val = a.eng, a.idx + 1
            else:
                key, val = ("d", a.dma_key), a.dma_cnt
            if kn.get(key, 0) >= val:
                continue
            op.waits.append(a)
            a.signal = True
            for k, v in a.know.items():
                if kn.get(k, 0) < v:
                    kn[k] = v
        op.know = dict(kn)
        if is_dma:
            op.know[("d", dma_key)] = op.dma_cnt
        elif op.idx is not None:
            op.know[eng] = op.idx + 1
        for ap in reads:
            for blk in self._blocks(ap):
                ent = self.blocks[blk]
                if is_dma or op.idx is None:
                    ent[2].append(op)
                else:
                    ent[1][eng] = op
        for ap in writes:
            for blk in self._blocks(ap):
                ent = self.blocks[blk]
                ent[0] = op
                ent[1] = {}
                ent[2] = []
        self.ops.append(op)
        self.eng_ops[eng].append(op)
        return op

    def finalize(self):
        for e in self.CENG:
            n = 0
            for op in self.eng_ops[e]:
                if op.dma_key is None and op.idx is not None and op.signal:
                    n += 1
                    op.sigval = n

    def emit(self, eng, handle, sems, dsems):
        for op in self.eng_ops[eng]:
            for a in op.waits:
                if a.dma_key is None:
                    handle.wait_ge(sems[a.eng], a.sigval)
                else:
                    handle.wait_ge(dsems[a.dma_key], a.dma_cnt)
            if op.fn is None:
                continue
            ins = op.fn(handle)
            if op.dma_key is not None:
                ins.then_inc(dsems[op.dma_key], 16)
            elif op.signal:
                ins.then_inc(sems[op.eng], 1)


def build_program():
    nc = bass.Bass("TRN2", target_bir_lowering=False)
    x_d = nc.dram_tensor("x", [SEQ, D], F32, kind="ExternalInput").ap()
    mem_d = nc.dram_tensor("mem", [MEM, D], F32, kind="ExternalInput").ap()
    w_in_d = nc.dram_tensor("w_in", [D, 2560], F32, kind="ExternalInput").ap()
    w_out_d = nc.dram_tensor("w_out", [D, D], F32, kind="ExternalInput").ap()
    w_q_d = nc.dram_tensor("w_q", [D, D], F32, kind="ExternalInput").ap()
    w_kv_d = nc.dram_tensor("w_kv", [D, 2 * D], F32, kind="ExternalInput").ap()
    w_o_d = nc.dram_tensor("w_o", [D, D], F32, kind="ExternalInput").ap()
    w_gu_d = nc.dram_tensor("w_gate_up", [D, 2 * DFF], F32, kind="ExternalInput").ap()
    w_dn_d = nc.dram_tensor("w_down", [DFF, D], F32, kind="ExternalInput").ap()
    cvec_d = nc.dram_tensor("cvec", [128, NCV], F32, kind="ExternalInput").ap()
    wspT_d = nc.dram_tensor("wspT", [128, 4, 128], F32, kind="ExternalInput").ap()
    bspb_d = nc.dram_tensor("bspb", [128, 4, 128], F32, kind="ExternalInput").ap()
    y_d = nc.dram_tensor("y", [SEQ, D], F32, kind="ExternalOutput").ap()

    top = [0]

    def alloc(nbytes, align=128):
        off = (top[0] + align - 1) // align * align
        top[0] = off + nbytes
        return off

    NSTD = 4
    SCR = 83968
    o_ident = alloc(512)
    o_ones = alloc(256)
    o_cvec = alloc(NCV * 4)
    o_wTm = alloc(1024)
    o_E = alloc(2048)
    o_KT = alloc(4096)
    o_Vt = alloc(4096)
    o_zc = alloc(32)
    o_small = alloc(1024)
    o_sq = alloc(16384)
    o_std = alloc(NSTD * 2048)
    o_xT = alloc(32768)
    o_xn = alloc(16384)
    o_ring = alloc(NSLOT * SLOT_B)
    o_scr = alloc(SCR)
    total = top[0]
    big = nc.alloc_sbuf_tensor("big", [128, total], U8)

    def view(off, dt, *shape):
        n = 1
        for s_ in shape:
            n *= s_
        ap = big[:, off:off + n * _esize(dt)].bitcast(dt)
        if len(shape) == 2:
            ap = ap.rearrange("p (a b) -> p a b", a=shape[0])
        elif len(shape) == 3:
            ap = ap.rearrange("p (a b c) -> p a b c", a=shape[0], b=shape[1])
        return ap

    ident = view(o_ident, F32, 128)
    ones = view(o_ones, BF16, 128)
    cvec = view(o_cvec, F32, NCV)
    wTm = view(o_wTm, BF16, 4, 128)
    E = view(o_E, F32, 4, 128)
    KT = view(o_KT, BF16, 8, 256)
    Vt = view(o_Vt, BF16, 2, 1024)
    zc = view(o_zc, F32, 4, 2)
    st6 = view(o_small, F32, 2, 4, 6)
    mv = view(o_small + 256, F32, 2, 4, 2)
    sdv = view(o_small + 384, F32, 2, 4)
    nmr = view(o_small + 448, F32, 2, 4)
    sq = view(o_sq, BF16, 2, 8, 512)
    stdb = view(o_std, F32, NSTD, 512)
    xT = view(o_xT, F32, 8, TS)
    xn = view(o_xn, BF16, 8, TS)
    u_b = view(o_scr + 0, F32, 4, TS)
    gc_b = view(o_scr + 16384, BF16, 4, TS)
    vn_b = view(o_scr + 24576, BF16, 8, 512)
    yn_b = view(o_scr + 32768, BF16, 8, TS)
    vg_b = view(o_scr + 49152, F32, 4, 512)
    z_b = view(o_scr + 57344, F32, 4, 514)
    yb_b = view(o_scr + 65664, F32, 4, TS)
    q_b = view(o_scr + 0, BF16, 8, TS)
    o_b = view(o_scr + 16384, BF16, 8, TS)
    p_b = view(o_scr + 32768, BF16, 2, 8, 512)
    rden_b = view(o_scr + 49152, F32, 2, 4, 512)
    h_b = view(o_scr + 0, BF16, KF, TS)
    sg_b = view(o_scr + 45056, F32, 2, 512)
    xin_b = view(o_scr + 0, F32, 2, 4, D)
    outT_b = view(o_scr + 32768, F32, 2, 8, 512)
    yst_b = view(o_scr + 65536, F32, 4, D)
    memT = view(o_scr + 32768, F32, 8, MEM)
    memn = view(o_scr + 40960, BF16, 8, MEM)
    wTf = view(o_scr + 49152, F32, 4, 128)
    bspb = view(o_scr + 51200, F32, 4, 128)

    ps = nc.alloc_psum_tensor("ps", [128, 4096], F32)
    bank_ctr = [0]

    def bank():
        b = bank_ctr[0] % 8
        bank_ctr[0] += 1
        return ps[:, b * 512:(b + 1) * 512]

    S = Sched()

    groups = []

    def slot_view(si, dt, *shape):
        return view(o_ring + si * SLOT_B, dt, *shape)

    def a_group(w_d, c0):
        return ("A", [(lambda si: slot_view(si, BF16, 8, 512),
                       w_d[:, c0:c0 + 512].rearrange("(k p) n -> p k n", p=128))])

    def gu_group(i):
        def dv(half):
            return lambda si: view(o_ring + si * SLOT_B + half * 4096, BF16, 8, 256)
        return ("GU", [(dv(0), w_gu_d[:, 256 * i:256 * i + 256].rearrange("(k p) n -> p k n", p=128)),
                       (dv(1), w_gu_d[:, DFF + 256 * i:DFF + 256 * i + 256].rearrange("(k p) n -> p k n", p=128))])

    def dn_group(m):
        def dv(half):
            return lambda si: view(o_ring + si * SLOT_B + half * 11 * 256, BF16, 11, 128)
        return ("DN", [(dv(0), w_dn_d[0:11 * 128, m * 128:(m + 1) * 128].rearrange("(k p) n -> p k n", p=128)),
                       (dv(1), w_dn_d[11 * 128:22 * 128, m * 128:(m + 1) * 128].rearrange("(k p) n -> p k n", p=128))])

    WIN_ORDER = (3, 4, 1, 0, 2)
    for i in range(4):
        groups.append(a_group(w_kv_d, 512 * i))
    for st in range(NST):
        for i in WIN_ORDER:
            groups.append(a_group(w_in_d, 512 * i))
        for i in range(2):
            groups.append(a_group(w_out_d, 512 * i))
        for i in range(2):
            groups.append(a_group(w_q_d, 512 * i))
        for i in range(2):
            groups.append(a_group(w_o_d, 512 * i))
        for i in range(11):
            groups.append(gu_group(i))
        for m in range(8):
            groups.append(dn_group(m))

    ring_state = {"next_dma": 0, "next_acq": 0}

    def ring_issue(gi):
        kind, parts = groups[gi]
        si = gi % NSLOT
        for dvf, src in parts:
            dst = dvf(si)
            S.add("pool", (lambda e, dst=dst, src=src: e.dma_start(out=dst, in_=src)),
                  writes=[dst], dma_key=("ring", si))

    held = []
    sticky = set()

    def ring_pump():
        base = held[0] if held else ring_state["next_acq"]
        want = min(base + NSLOT - 1, len(groups) - 1)
        while ring_state["next_dma"] <= want:
            ring_issue(ring_state["next_dma"])
            ring_state["next_dma"] += 1

    def ring_acquire(stick=False):
        gi = ring_state["next_acq"]
        ring_state["next_acq"] += 1
        held[:] = [h for h in held if h in sticky]
        held.append(gi)
        if stick:
            sticky.add(gi)
        ring_pump()
        kind, _ = groups[gi]
        si = gi % NSLOT
        if kind == "A":
            return slot_view(si, BF16, 8, 512)
        if kind == "GU":
            return slot_view(si, BF16, 2, 8, 256)
        return slot_view(si, BF16, KF, 128)

    def ring_unstick():
        sticky.clear()

    std_ctr = [0]

    def normA(src, nch, Tn, sqv):
        for c in range(nch):
            S.add("act", (lambda e, c=c: e.activation(out=sqv[:, c, 0:Tn], in_=src[:, c, :], func=AF.Square)),
                  reads=[src[:, c, :]], writes=[sqv[:, c, 0:Tn]])

    def normB(src, dst, gbase, nch, N, Tn, sqv):
        b = bank()

        def f(e):
            for c in range(nch):
                ins = e.matmul(b[:, 0:Tn], lhsT=ones, rhs=sqv[:, c, 0:Tn], start=(c == 0), stop=(c == nch - 1))
            return ins
        S.add("pe", f, reads=[ones, sqv[:, 0:nch, 0:Tn]], writes=[b])
        sd = stdb[:, std_ctr[0] % NSTD, 0:Tn]
        std_ctr[0] += 1
        S.add("act", (lambda e: e.activation(out=sd, in_=b[:, 0:Tn], func=AF.Sqrt, bias=EPS, scale=1.0 / N)),
              reads=[b], writes=[sd])
        S.add("dve", (lambda e: e.reciprocal(out=sd, in_=sd)), reads=[sd], writes=[sd])
        for c in range(nch):
            S.add("dve", (lambda e, c=c: e.scalar_tensor_tensor(
                out=dst[:, c, :], in0=src[:, c, :], scalar=cvec[:, gbase + c:gbase + c + 1], in1=sd,
                op0=ALU.mult, op1=ALU.mult)),
                reads=[src[:, c, :], cvec, sd], writes=[dst[:, c, :]])

    def mm_w(b, wfn, rhsfn, nk, ncols=T):
        def f(e):
            for k in range(nk):
                ins = e.matmul(b[:, 0:ncols], lhsT=wfn(k), rhs=rhsfn(k), start=(k == 0), stop=(k == nk - 1))
            return ins
        return f

    def tsl_of(s):
        return slice(s * T, (s + 1) * T)

    S.add("sp", lambda e: e.dma_start(out=cvec, in_=cvec_d), writes=[cvec], dma_key="cvec")
    S.add("sp", lambda e: e.dma_start(out=wTf, in_=wspT_d), writes=[wTf], dma_key="wTf")
    S.add("sp", lambda e: e.dma_start(out=bspb, in_=bspb_d), writes=[bspb], dma_key="bspb")
    S.add("sp", lambda e: e.dma_start(out=yst_b[:, 0:2, :], in_=mem_d.rearrange("(j p) d -> p j d", p=128)),
          writes=[yst_b[:, 0:2, :]], dma_key="memin")

    def x_load(st, s):
        t0 = st * TS + s * T
        S.add("sp", (lambda e: e.dma_start(
            out=xin_b[:, s, :, :], in_=x_d[t0:t0 + T, :].rearrange("(j p) d -> p j d", p=128))),
            writes=[xin_b[:, s, :, :]], dma_key=("xin", s))

    x_load(0, 0)
    x_load(0, 1)
    S.add("pool", lambda e: e.memset(ident, 0.0), writes=[ident])
    S.add("pool", lambda e: e.affine_select(out=ident, in_=ident, pattern=[[-1, 128]], compare_op=ALU.not_equal,
                                            fill=1.0, base=0, channel_multiplier=1),
          reads=[ident], writes=[ident])
    S.add("pool", lambda e: e.memset(ones, 1.0), writes=[ones])
    S.add("pool", lambda e: e.memset(zc, 0.0), writes=[zc])
    for h in range(4):
        S.add("pool", (lambda e, h=h: e.affine_select(out=wTf[:, h, :], in_=wTf[:, h, :], pattern=[[1, 128]],
                                                      compare_op=ALU.is_ge, fill=0.0, base=0, channel_multiplier=-1)),
              reads=[wTf[:, h, :]], writes=[wTf[:, h, :]])
    S.add("dve", lambda e: e.tensor_copy(out=wTm, in_=wTf), reads=[wTf], writes=[wTm])
    b = bank()

    def f_rw(e, b=b):
        for h in range(4):
            ins = e.matmul(b[:, h * 128:(h + 1) * 128], lhsT=ones, rhs=wTm[:, h, :], start=True, stop=True)
        return ins
    S.add("pe", f_rw, reads=[ones, wTm], writes=[b])
    for h in range(4):
        S.add("dve", (lambda e, h=h, b=b: e.scalar_tensor_tensor(
            out=E[:, h, :], in0=b[:, h * 128:(h + 1) * 128], scalar=cvec[:, C_BLN + h:C_BLN + h + 1],
            in1=bspb[:, h, :], op0=ALU.mult, op1=ALU.add)),
            reads=[b, cvec, bspb[:, h, :]], writes=[E[:, h, :]])

    for c in range(8):
        b = bank()

        def f(e, c=c, b=b):
            for j in range(2):
                ins = e.transpose(b[:, j * 128:(j + 1) * 128], yst_b[:, j, c * 128:(c + 1) * 128], ident)
            return ins
        S.add("pe", f, reads=[yst_b[:, 0:2, c * 128:(c + 1) * 128], ident], writes=[b])
        if c % 2 == 0:
            S.add("act", (lambda e, c=c, b=b: e.copy(out=memT[:, c, :], in_=b[:, 0:MEM])),
                  reads=[b], writes=[memT[:, c, :]])
        else:
            S.add("dve", (lambda e, c=c, b=b: e.tensor_copy(out=memT[:, c, :], in_=b[:, 0:MEM])),
                  reads=[b], writes=[memT[:, c, :]])
    sq_mem = view(o_std + 4096, BF16, 8, MEM)
    normA(memT, 8, MEM, sq_mem)

    def xpose(s):
        tsl = tsl_of(s)
        for c in range(8):
            b = bank()

            def f(e, c=c, b=b):
                for j in range(4):
                    ins = e.transpose(b[:, j * 128:(j + 1) * 128], xin_b[:, s, j, c * 128:(c + 1) * 128], ident)
                return ins
            S.add("pe", f, reads=[xin_b[:, s, :, c * 128:(c + 1) * 128], ident], writes=[b])
            if c % 2 == 0:
                S.add("act", (lambda e, c=c, b=b: e.copy(out=xT[:, c, tsl], in_=b)),
                      reads=[b], writes=[xT[:, c, tsl]])
            else:
                S.add("dve", (lambda e, c=c, b=b: e.tensor_copy(out=xT[:, c, tsl], in_=b)),
                      reads=[b], writes=[xT[:, c, tsl]])
        normA(xT[:, :, tsl], 8, T, sq[:, s])

    def norm1B():
        for s in range(NSUB):
            tsl = tsl_of(s)
            normB(xT[:, :, tsl], xn[:, :, tsl], C_MIX, 8, D, T, sq[:, s])

    xpose(0)
    xpose(1)
    normB(memT, memn, C_MEM, 8, D, MEM, sq_mem)
    norm1B()
    for gi in range(2):
        g = ring_acquire()
        for mm in range(4):
            c = gi * 4 + mm
            b = bank()
            S.add("pe", mm_w(b, (lambda k, g=g, mm=mm: g[:, k, mm * 128:(mm + 1) * 128]),
                             (lambda k: memn[:, k, :]), 8, ncols=MEM),
                  reads=[g, memn], writes=[b])
            S.add("act", (lambda e, c=c, b=b: e.copy(out=KT[:, c, :], in_=b[:, 0:MEM])),
                  reads=[b], writes=[KT[:, c, :]])
    for gi in range(2):
        g = ring_acquire()
        for mc in range(2):
            b = bank()
            S.add("pe", mm_w(b, (lambda k, mc=mc: memn[:, k, mc * 128:(mc + 1) * 128]),
                             (lambda k, g=g: g[:, k, :]), 8, ncols=512),
                  reads=[g, memn], writes=[b])
            S.add("act", (lambda e, mc=mc, gi=gi, b=b: e.copy(out=Vt[:, mc, gi * 512:(gi + 1) * 512], in_=b)),
                  reads=[b], writes=[Vt[:, mc, gi * 512:(gi + 1) * 512]])

    def wgroup(g, s, src, evac, nmm=4):
        tsl = tsl_of(s)
        for mm in range(nmm):
            b = bank()
            S.add("pe", mm_w(b, (lambda k, g=g, mm=mm: g[:, k, mm * 128:(mm + 1) * 128]),
                             (lambda k, tsl=tsl: src[:, k, tsl]), 8),
                  reads=[g, src[:, :, tsl]], writes=[b])
            evac(mm, b)

    def wgroup_k(wfn, s, src, evac, nmm=4):
        tsl = tsl_of(s)
        banks = [bank() for _ in range(nmm)]
        for k in range(8):
            def f(e, k=k):
                for mm in range(nmm):
                    ins = e.matmul(banks[mm], lhsT=wfn(k, mm), rhs=src[:, k, tsl], start=(k == 0), stop=(k == 7))
                return ins
            S.add("pe", f, reads=[wfn(k, mm) for mm in range(nmm)] + [src[:, k, tsl]], writes=banks)
        for mm in range(nmm):
            evac(mm, banks[mm])

    def proj_residual(src_b, after_s):
        g0 = ring_acquire(stick=True)
        g1 = ring_acquire()
        for s in range(NSUB):
            tsl = tsl_of(s)
            for gi, g in enumerate((g0, g1)):
                def ev(mm, b, gi=gi, tsl=tsl):
                    m = gi * 4 + mm
                    S.add("dve", (lambda e: e.tensor_tensor(out=xT[:, m, tsl], in0=b, in1=xT[:, m, tsl], op=ALU.add)),
                          reads=[b, xT[:, m, tsl]], writes=[xT[:, m, tsl]])
                wgroup(g, s, src_b, ev)
            after_s(s)
        ring_unstick()

    out_dmas = []
    for st in range(NST):
        if st > 0:
            xpose(0)
            xpose(1)

        g = ring_acquire()
        if st > 0:
            norm1B()
        for s in range(NSUB):
            tsl = tsl_of(s)

            def ev(mm, b, tsl=tsl):
                S.add("act", (lambda e: e.copy(out=gc_b[:, mm, tsl], in_=b)), reads=[b], writes=[gc_b[:, mm, tsl]])
            wgroup_k((lambda k, mm, g=g: g[:, k, mm * 128:(mm + 1) * 128]), s, xn, ev)

        g_val = ring_acquire(stick=True)

        def val_part(s):
            tsl = tsl_of(s)
            S.add("dve", (lambda e: e.tensor_copy(out=z_b[:, :, 0:2], in_=zc)), reads=[zc], writes=[z_b[:, :, 0:2]])

            def ev(mm, b):
                S.add("dve", (lambda e: e.tensor_tensor(out=z_b[:, mm, 2:514], in0=b, in1=gc_b[:, mm, tsl],
                                                        op=ALU.mult)),
                      reads=[b, gc_b[:, mm, tsl]], writes=[z_b[:, mm, 2:514]])
            wgroup(g_val, s, xn, ev)
            S.add("dve", (lambda e: e.tensor_copy(out=zc, in_=z_b[:, :, 512:514])),
                  reads=[z_b[:, :, 512:514]], writes=[zc])
            for m in range(4):
                S.add("dve", (lambda e, m=m: e.tensor_scalar(out=yb_b[:, m, tsl], in0=z_b[:, m, 0:512],
                                                             scalar1=cvec[:, C_CW + m:C_CW + m + 1], scalar2=None,
                                                             op0=ALU.mult)),
                      reads=[z_b[:, m, 0:512], cvec], writes=[yb_b[:, m, tsl]])
                for jj in (1, 2):
                    S.add("dve", (lambda e, m=m, jj=jj: e.scalar_tensor_tensor(
                        out=yb_b[:, m, tsl], in0=z_b[:, m, jj:jj + 512],
                        scalar=cvec[:, C_CW + jj * 4 + m:C_CW + jj * 4 + m + 1], in1=yb_b[:, m, tsl],
                        op0=ALU.mult, op1=ALU.add)),
                        reads=[z_b[:, m, jj:jj + 512], cvec, yb_b[:, m, tsl]], writes=[yb_b[:, m, tsl]])
        val_part(0)

        g = ring_acquire()
        for s in range(NSUB):
            for j in range(4):
                b = bank()
                S.add("pe", mm_w(b, (lambda k, s=s, j=j: xn[:, k, s * T + j * 128:s * T + (j + 1) * 128]),
                                 (lambda k, g=g: g[:, k, :]), 8),
                      reads=[g, xn[:, :, s * T + j * 128:s * T + (j + 1) * 128]], writes=[b])
                S.add("act", (lambda e, j=j, b=b: e.activation(out=vg_b[:, j, :], in_=b, func=AF.Gelu_apprx_tanh)),
                      reads=[b], writes=[vg_b[:, j, :]])
                S.add("dve", (lambda e, j=j, s=s: e.bn_stats(out=st6[:, s, j, :], in_=vg_b[:, j, :])),
                      reads=[vg_b[:, j, :]], writes=[st6[:, s, j, :]])
                S.add("dve", (lambda e, j=j, s=s: e.bn_aggr(out=mv[:, s, j, :], in_=st6[:, s, j, :])),
                      reads=[st6[:, s, j, :]], writes=[mv[:, s, j, :]])
            S.add("act", (lambda e, s=s: e.activation(out=sdv[:, s, :], in_=mv[:, s, :, 1], func=AF.Sqrt, bias=EPS,
                                                      scale=1.0)),
                  reads=[mv[:, s]], writes=[sdv[:, s, :]])
            S.add("dve", (lambda e, s=s: e.reciprocal(out=sdv[:, s, :], in_=sdv[:, s, :])),
                  reads=[sdv[:, s, :]], writes=[sdv[:, s, :]])
            S.add("dve", (lambda e, s=s: e.scalar_tensor_tensor(out=nmr[:, s, :], in0=mv[:, s, :, 0], scalar=-1.0,
                                                                in1=sdv[:, s, :], op0=ALU.mult, op1=ALU.mult)),
                  reads=[mv[:, s], sdv[:, s, :]], writes=[nmr[:, s, :]])
            for j in range(4):
                S.add("act", (lambda e, s=s, j=j: e.activation(out=vn_b[:, s * 4 + j, :], in_=vg_b[:, j, :],
                                                               func=AF.Identity, bias=nmr[:, s, j:j + 1],
                                                               scale=sdv[:, s, j:j + 1])),
                      reads=[vg_b[:, j, :], nmr[:, s, :], sdv[:, s, :]], writes=[vn_b[:, s * 4 + j, :]])
        val_part(1)
        ring_unstick()

        g = ring_acquire()
        for s in range(NSUB):
            tsl = tsl_of(s)

            def ev(mm, b, tsl=tsl):
                S.add("act", (lambda e: e.activation(out=u_b[:, mm, tsl], in_=b, func=AF.Gelu_apprx_tanh)),
                      reads=[b], writes=[u_b[:, mm, tsl]])
            wgroup(g, s, xn, ev)

        def spatial(s):
            tsl = tsl_of(s)
            for h in range(4):
                b = bank()

                def f(e, h=h, b=b):
                    for j in range(4):
                        ins = e.matmul(b[:, j * 128:(j + 1) * 128], lhsT=vn_b[:, s * 4 + j, h * 128:(h + 1) * 128],
                                       rhs=wTm[:, h, :], start=True, stop=True)
                    return ins
                S.add("pe", f, reads=[vn_b[:, s * 4:(s + 1) * 4, h * 128:(h + 1) * 128], wTm[:, h, :]], writes=[b])
                tmp = vg_b[:, h, :]
                S.add("dve", (lambda e, h=h, b=b, tmp=tmp: e.scalar_tensor_tensor(
                    out=tmp.rearrange("p (j t) -> p j t", j=4),
                    in0=b.rearrange("p (j t) -> p j t", j=4),
                    scalar=cvec[:, C_GLN + h:C_GLN + h + 1],
                    in1=E[:, h, :].unsqueeze(1).to_broadcast([128, 4, 128]),
                    op0=ALU.mult, op1=ALU.add)),
                    reads=[b, cvec, E[:, h, :]], writes=[tmp])
                S.add("dve", (lambda e, h=h, tmp=tmp: e.tensor_tensor(out=u_b[:, h, tsl], in0=tmp,
                                                                       in1=u_b[:, h, tsl], op=ALU.mult)),
                      reads=[tmp, u_b[:, h, tsl]], writes=[u_b[:, h, tsl]])
            normA(u_b[:, :, tsl], 4, T, sq[:, s, 0:4])

        g_gb = ring_acquire()

        def gate_b(s):
            tsl = tsl_of(s)

            def ev(mm, b):
                S.add("dve", (lambda e: e.tensor_tensor(out=yb_b[:, mm, tsl], in0=b, in1=yb_b[:, mm, tsl],
                                                        op=ALU.mult)),
                      reads=[b, yb_b[:, mm, tsl]], writes=[yb_b[:, mm, tsl]])
            wgroup(g_gb, s, xn, ev)
            normA(yb_b[:, :, tsl], 4, T, sq[:, s, 4:8])

        gate_b(0)
        spatial(0)
        spatial(1)
        gate_b(1)
        for s in range(NSUB):
            tsl = tsl_of(s)
            normB(u_b[:, :, tsl], yn_b[:, 0:4, tsl], C_GA, 4, 512, T, sq[:, s, 0:4])
            normB(yb_b[:, :, tsl], yn_b[:, 4:8, tsl], C_GB, 4, 512, T, sq[:, s, 4:8])

        nb_done = [False, False]

        def after_wout(s):
            normA(xT[:, :, tsl_of(s)], 8, T, sq[:, s])

        g0 = ring_acquire(stick=True)
        g1 = ring_acquire()
        for s in range(NSUB):
            tsl = tsl_of(s)
            for gi, g in enumerate((g0, g1)):
                def ev(mm, b, gi=gi, tsl=tsl):
                    m = gi * 4 + mm
                    S.add("dve", (lambda e: e.tensor_tensor(out=xT[:, m, tsl], in0=b, in1=xT[:, m, tsl], op=ALU.add)),
                          reads=[b, xT[:, m, tsl]], writes=[xT[:, m, tsl]])
                wgroup(g, s, yn_b, ev)
            after_wout(s)
        ring_unstick()

        for s in range(NSUB):
            tsl = tsl_of(s)
            normB(xT[:, :, tsl], xn[:, :, tsl], C_ATT, 8, D, T, sq[:, s])
        for gi in range(2):
            g = ring_acquire()
            for s in range(NSUB):
                tsl = tsl_of(s)

                def ev(mm, b, gi=gi, tsl=tsl):
                    m = gi * 4 + mm
                    S.add("act", (lambda e: e.activation(out=q_b[:, m, tsl], in_=b, func=AF.Copy, scale=0.0625)),
                          reads=[b], writes=[q_b[:, m, tsl]])
                if gi == 0:
                    wgroup_k((lambda k, mm, g=g: g[:, k, mm * 128:(mm + 1) * 128]), s, xn, ev)
                else:
                    wgroup(g, s, xn, ev)
        for s in range(NSUB):
            tsl = tsl_of(s)
            for h in range(4):
                for mc in range(2):
                    b = bank()

                    def f(e, h=h, mc=mc, b=b, tsl=tsl):
                        for half in range(2):
                            ins = e.matmul(b, lhsT=KT[:, h * 2 + half, mc * 128:(mc + 1) * 128],
                                           rhs=q_b[:, h * 2 + half, tsl], start=(half == 0), stop=(half == 1))
                        return ins
                    S.add("pe", f, reads=[KT[:, h * 2:h * 2 + 2, :], q_b[:, h * 2:h * 2 + 2, tsl]], writes=[b])
                    S.add("act", (lambda e, h=h, mc=mc, b=b, s=s: e.activation(out=p_b[:, s, h * 2 + mc, :], in_=b,
                                                                               func=AF.Exp)),
                          reads=[b], writes=[p_b[:, s, h * 2 + mc, :]])
        for s in range(NSUB):
            tsl = tsl_of(s)
            for h in range(4):
                b = bank()

                def f(e, h=h, b=b, s=s):
                    for mc in range(2):
                        ins = e.matmul(b, lhsT=ones, rhs=p_b[:, s, h * 2 + mc, :], start=(mc == 0), stop=(mc == 1))
                    return ins
                S.add("pe", f, reads=[ones, p_b[:, s, h * 2:h * 2 + 2, :]], writes=[b])
                S.add("dve", (lambda e, h=h, b=b, s=s: e.reciprocal(out=rden_b[:, s, h, :], in_=b)),
                      reads=[b], writes=[rden_b[:, s, h, :]])
            for c in range(8):
                h = c // 2
                b = bank()

                def f(e, c=c, h=h, b=b, s=s):
                    for mc in range(2):
                        ins = e.matmul(b, lhsT=Vt[:, mc, c * 128:(c + 1) * 128], rhs=p_b[:, s, h * 2 + mc, :],
                                       start=(mc == 0), stop=(mc == 1))
                    return ins
                S.add("pe", f, reads=[Vt[:, :, c * 128:(c + 1) * 128], p_b[:, s, h * 2:h * 2 + 2, :]], writes=[b])
                S.add("dve", (lambda e, c=c, h=h, b=b, tsl=tsl, s=s: e.tensor_tensor(
                    out=o_b[:, c, tsl], in0=b, in1=rden_b[:, s, h, :], op=ALU.mult)),
                    reads=[b, rden_b[:, s, h, :]], writes=[o_b[:, c, tsl]])
        proj_residual(o_b, after_wout)

        sg_ctr = [0]
        for s in range(NSUB):
            tsl = tsl_of(s)
            normB(xT[:, :, tsl], xn[:, :, tsl], C_FFN, 8, D, T, sq[:, s])

        def swiglu(bg, bu, j, tsl):
            sg = sg_b[:, sg_ctr[0] % 2, :]
            sg_ctr[0] += 1
            S.add("act", (lambda e: e.activation(out=sg, in_=bg, func=AF.Silu)), reads=[bg], writes=[sg])
            S.add("dve", (lambda e: e.tensor_tensor(out=h_b[:, j, tsl], in0=bu, in1=sg, op=ALU.mult)),
                  reads=[bu, sg], writes=[h_b[:, j, tsl]])

        for gi in range(11):
            g = ring_acquire()
            for s in range(NSUB):
                tsl = tsl_of(s)
                if gi == 0:
                    got = {}

                    def ev(mm, b, got=got, tsl=tsl, gi=gi):
                        got[mm] = b
                        if mm % 2 == 1:
                            swiglu(got[mm - 1], b, gi * 2 + mm // 2, tsl)
                    wgroup_k((lambda k, mm, g=g: g[:, mm % 2, k, (mm // 2) * 128:(mm // 2 + 1) * 128]), s, xn, ev)
                    continue
                for i in range(2):
                    bg = bank()
                    S.add("pe", mm_w(bg, (lambda k, g=g, i=i: g[:, 0, k, i * 128:(i + 1) * 128]),
                                     (lambda k, tsl=tsl: xn[:, k, tsl]), 8),
                          reads=[g, xn[:, :, tsl]], writes=[bg])
                    bu = bank()
                    S.add("pe", mm_w(bu, (lambda k, g=g, i=i: g[:, 1, k, i * 128:(i + 1) * 128]),
                                     (lambda k, tsl=tsl: xn[:, k, tsl]), 8),
                          reads=[g, xn[:, :, tsl]], writes=[bu])
                    swiglu(bg, bu, gi * 2 + i, tsl)
        for m in range(8):
            g = ring_acquire()
            for s in range(NSUB):
                tsl = tsl_of(s)
                b = bank()
                S.add("pe", mm_w(b, (lambda k, g=g: g[:, k, :]), (lambda k, tsl=tsl: h_b[:, k, tsl]), KF),
                      reads=[g, h_b[:, :, tsl]], writes=[b])
                S.add("dve", (lambda e, m=m, b=b, tsl=tsl: e.tensor_tensor(
                    out=xT[:, m, tsl], in0=b, in1=xT[:, m, tsl], op=ALU.add)),
                    reads=[b, xT[:, m, tsl]], writes=[xT[:, m, tsl]])

        if st + 1 < NST:
            x_load(st + 1, 0)
            x_load(st + 1, 1)

        for s in range(NSUB):
            normA(xT[:, :, tsl_of(s)], 8, T, sq[:, s])
        for s in range(NSUB):
            tsl = tsl_of(s)
            normB(xT[:, :, tsl], outT_b[:, s], C_FIN, 8, D, T, sq[:, s])
        for s in range(NSUB):
            t0 = st * TS + s * T
            tsl = tsl_of(s)
            for j in range(4):
                for half in range(2):
                    b = bank()

                    def f(e, j=j, half=half, b=b, s=s):
                        for cc in range(4):
                            c = half * 4 + cc
                            ins = e.transpose(b[:, cc * 128:(cc + 1) * 128], outT_b[:, s, c, j * 128:(j + 1) * 128],
                                              ident)
                        return ins
                    S.add("pe", f, reads=[outT_b[:, s, half * 4:half * 4 + 4, j * 128:(j + 1) * 128], ident],
                          writes=[b])
                    dsts = yst_b[:, j, half * 512:(half + 1) * 512]
                    if (j + half) % 2 == 0:
                        S.add("act", (lambda e, dsts=dsts, b=b: e.copy(out=dsts, in_=b)), reads=[b], writes=[dsts])
                    else:
                        S.add("dve", (lambda e, dsts=dsts, b=b: e.tensor_copy(out=dsts, in_=b)), reads=[b], writes=[dsts])
            od = S.add("sp", (lambda e, t0=t0: e.dma_start(
                out=y_d[t0:t0 + T, :].rearrange("(j p) d -> p j d", p=128), in_=yst_b)),
                reads=[yst_b], dma_key="yout")
            out_dmas.append(od)

    S.add("sp", None, extra_deps=[out_dmas[-1]])
    assert ring_state["next_acq"] == len(groups), (ring_state, len(groups))

    S.finalize()
    sems = {e: nc.alloc_semaphore("sem_" + e) for e in Sched.CENG}
    dsems = {}
    for key in S.dma_cnt:
        dsems[key] = nc.alloc_semaphore("dsem_" + "_".join(str(k) for k in (key if isinstance(key, tuple) else (key,))))

    with nc.Block() as block:
        @block.tensor
        def _(e):
            S.emit("pe", e, sems, dsems)

        @block.scalar
        def _(e):
            S.emit("act", e, sems, dsems)

        @block.vector
        def _(e):
            S.emit("dve", e, sems, dsems)

        @block.gpsimd
        def _(e):
            S.emit("pool", e, sems, dsems)

        @block.sync
        def _(e):
            S.emit("sp", e, sems, dsems)
    return nc


def _host_layout(inputs):
    f = lambda a: np.ascontiguousarray(np.asarray(a, dtype=np.float32))
    col = lambda v, n: f(v).reshape(n, 128).T
    cw = f(inputs["conv_w"])
    cvec = np.concatenate([
        col(inputs["ln_mix_g"], 8), col(inputs["ln_attn_g"], 8), col(inputs["ln_ffn_g"], 8),
        col(inputs["ln_final_g"], 8), col(inputs["ln_mem_g"], 8),
        col(inputs["grp_norm_a"], 4), col(inputs["grp_norm_b"], 4),
        col(inputs["sgu_ln_g"], 4), col(inputs["sgu_ln_b"], 4),
        col(cw[0], 4), col(cw[1], 4), col(cw[2], 4),
    ], axis=1)
    assert cvec.shape == (128, NCV)
    wspT = f(np.transpose(f(inputs["w_spatial"]), (2, 0, 1)))
    bspb = f(np.broadcast_to(f(inputs["b_spatial"])[None, :, :], (128, 4, 128)))
    shared = {
        "w_in": f(inputs["w_in"]), "w_out": f(inputs["w_out"]), "w_q": f(inputs["w_q"]),
        "w_kv": f(inputs["w_kv"]), "w_o": f(inputs["w_o"]), "w_gate_up": f(inputs["w_gate_up"]),
        "w_down": f(inputs["w_down"]), "cvec": f(cvec), "wspT": wspT, "bspb": bspb,
    }
    x = f(inputs["x"])
    mem = f(inputs["mem"])
    in_maps = []
    for b in range(8):
        d = dict(shared)
        d["x"] = x[b]
        d["mem"] = mem[b]
        in_maps.append(d)
    return in_maps


def kernel(**inputs):
    in_maps = _host_layout(inputs)
    nc = build_program()
    res = run_bass_kernel_spmd(nc, in_maps, core_ids=list(range(8)))
    out = np.stack([np.asarray(r["y"], dtype=np.float32) for r in res.results], axis=0)
    return out
```

```python
import numpy as np
import concourse.bass as bass
import concourse.mybir as mybir
from concourse.bass_utils import run_bass_kernel_spmd

F32 = mybir.dt.float32
BF16 = mybir.dt.bfloat16
U8 = mybir.dt.uint8
AF = mybir.ActivationFunctionType
ALU = mybir.AluOpType

D = 1024
KD = 8
SEQ = 4096
TS = 1024
T = 512
NSUB = TS // T
NST = SEQ // TS
MEM = 256
DFF = 2816
KF = DFF // 128
EPS = 1e-6
NSLOT = 5
SLOT_B = 8192

C_MIX, C_ATT, C_FFN, C_FIN, C_MEM = 0, 8, 16, 24, 32
C_GA, C_GB, C_GLN, C_BLN, C_CW = 40, 44, 48, 52, 56
NCV = 68

_ES = {F32: 4, BF16: 2, U8: 1}


def _esize(dt):
    return _ES[dt]


def _hull(ap):
    es = _esize(ap.dtype)
    pairs = [tuple(p) for p in ap.ap]
    pstride = pairs[0][0]
    off = ap.offset % pstride if pstride else ap.offset
    ext = 1
    for st, cnt in pairs[1:]:
        ext += (cnt - 1) * abs(st)
    return ap.tensor.name, off * es, (off + ext) * es


class Op:
    __slots__ = ("eng", "fn", "idx", "waits", "signal", "sigval", "dma_key", "dma_cnt", "know", "gid")


class Sched:
    CENG = ("pe", "act", "dve", "pool")
    GRAN = 128

    def __init__(self):
        self.ops = []
        self.eng_ops = {e: [] for e in ("pe", "act", "dve", "pool", "sp")}
        self.nidx = {e: 0 for e in self.CENG}
        self.blocks = {}
        self.know = {e: {} for e in ("pe", "act", "dve", "pool", "sp")}
        self.dma_cnt = {}

    def _blocks(self, ap):
        name, a, b = _hull(ap)
        g = 2048 if name.startswith("ps") else self.GRAN
        return [(name, i) for i in range(a // g, (b - 1) // g + 1)]

    def add(self, eng, fn, reads=(), writes=(), dma_key=None, extra_deps=()):
        op = Op()
        op.eng = eng
        op.fn = fn
        op.dma_key = dma_key
        op.signal = False
        op.sigval = None
        op.waits = []
        op.gid = len(self.ops)
        is_dma = dma_key is not None
        if is_dma:
            self.dma_cnt[dma_key] = self.dma_cnt.get(dma_key, 0) + 16
            op.dma_cnt = self.dma_cnt[dma_key]
            op.idx = None
        else:
            op.dma_cnt = None
            if eng in self.CENG and fn is not None:
                op.idx = self.nidx[eng]
                self.nidx[eng] += 1
            else:
                op.idx = None
        deps = {}
        for d in extra_deps:
            deps[d] = True
        for ap in reads:
            for blk in self._blocks(ap):
                ent = self.blocks.get(blk)
                if ent is None:
                    ent = [None, {}, []]
                    self.blocks[blk] = ent
                if ent[0] is not None:
                    deps[ent[0]] = True
        for ap in writes:
            for blk in self._blocks(ap):
                ent = self.blocks.get(blk)
                if ent is None:
                    ent = [None, {}, []]
                    self.blocks[blk] = ent
                if ent[0] is not None and ent[0] not in deps:
                    deps[ent[0]] = False
                for r in ent[1].values():
                    if r not in deps:
                        deps[r] = False
                for r in ent[2]:
                    if r not in deps:
                        deps[r] = False
        deps.pop(op, None)
        kn = self.know[eng]
        for a in sorted(deps, key=lambda o: -o.gid):
            raw = deps[a]
            if a.dma_key is None:
                if a.idx is None:
                    continue
                if a.eng == eng and not is_dma:
                    if eng == "pe":
                        continue
                key, val = a.eng, a.idx + 1
            else:
                key, val = ("d", a.dma_key), a.dma_cnt
            if kn.get(key, 0) >= val:
                continue
            op.waits.append(a)
            a.signal = True
            for k, v in a.know.items():
                if kn.get(k, 0) < v:
                    kn[k] = v
        op.know = dict(kn)
        if is_dma:
            op.know[("d", dma_key)] = op.dma_cnt
        elif op.idx is not None:
            op.know[eng] = op.idx + 1
        for ap in reads:
            for blk in self._blocks(ap):
                ent = self.blocks[blk]
                if is_dma or op.idx is None:
                    ent[2].append(op)
                else:
                    ent[1][eng] = op
        for ap in writes:
            for blk in self._blocks(ap):
                ent = self.blocks[blk]
                ent[0] = op
                ent[1] = {}
                ent[2] = []
        self.ops.append(op)
        self.eng_ops[eng].append(op)
        return op

    def finalize(self):
        for e in self.CENG:
            n = 0
            for op in self.eng_ops[e]:
                if op.dma_key is None and op.idx is not None and op.signal:
                    n += 1
                    op.sigval = n

    def emit(self, eng, handle, sems, dsems):
        for op in self.eng_ops[eng]:
            for a in op.waits:
                if a.dma_key is None:
                    handle.wait_ge(sems[a.eng], a.sigval)
                else:
                    handle.wait_ge(dsems[a.dma_key], a.dma_cnt)
            if op.fn is None:
                continue
            ins = op.fn(handle)
            if op.dma_key is not None:
                ins.then_inc(dsems[op.dma_key], 16)
            elif op.signal:
                ins.then_inc(sems[op.eng], 1)


def build_program():
    nc = bass.Bass("TRN2", target_bir_lowering=False)
    x_d = nc.dram_tensor("x", [SEQ, D], F32, kind="ExternalInput").ap()
    mem_d = nc.dram_tensor("mem", [MEM, D], F32, kind="ExternalInput").ap()
    w_in_d = nc.dram_tensor("w_in", [D, 2560], F32, kind="ExternalInput").ap()
    w_out_d = nc.dram_tensor("w_out", [D, D], F32, kind="ExternalInput").ap()
    w_q_d = nc.dram_tensor("w_q", [D, D], F32, kind="ExternalInput").ap()
    w_kv_d = nc.dram_tensor("w_kv", [D, 2 * D], F32, kind="ExternalInput").ap()
    w_o_d = nc.dram_tensor("w_o", [D, D], F32, kind="ExternalInput").ap()
    w_gu_d = nc.dram_tensor("w_gate_up", [D, 2 * DFF], F32, kind="ExternalInput").ap()
    w_dn_d = nc.dram_tensor("w_down", [DFF, D], F32, kind="ExternalInput").ap()
    cvec_d = nc.dram_tensor("cvec", [128, NCV], F32, kind="ExternalInput").ap()
    wspT_d = nc.dram_tensor("wspT", [128, 4, 128], F32, kind="ExternalInput").ap()
    bspb_d = nc.dram_tensor("bspb", [128, 4, 128], F32, kind="ExternalInput").ap()
    y_d = nc.dram_tensor("y", [SEQ, D], F32, kind="ExternalOutput").ap()

    top = [0]

    def alloc(nbytes, align=128):
        off = (top[0] + align - 1) // align * align
        top[0] = off + nbytes
        return off

    NSTD = 4
    SCR = 83968
    o_ident = alloc(512)
    o_ones = alloc(256)
    o_cvec = alloc(NCV * 4)
    o_wTm = alloc(1024)
    o_E = alloc(2048)
    o_KT = alloc(4096)
    o_Vt = alloc(4096)
    o_zc = alloc(32)
    o_small = alloc(1024)
    o_sq = alloc(16384)
    o_std = alloc(NSTD * 2048)
    o_xT = alloc(32768)
    o_xn = alloc(16384)
    o_ring = alloc(NSLOT * SLOT_B)
    o_scr = alloc(SCR)
    total = top[0]
    big = nc.alloc_sbuf_tensor("big", [128, total], U8)

    def view(off, dt, *shape):
        n = 1
        for s_ in shape:
            n *= s_
        ap = big[:, off:off + n * _esize(dt)].bitcast(dt)
        if len(shape) == 2:
            ap = ap.rearrange("p (a b) -> p a b", a=shape[0])
        elif len(shape) == 3:
            ap = ap.rearrange("p (a b c) -> p a b c", a=shape[0], b=shape[1])
        return ap

    ident = view(o_ident, F32, 128)
    ones = view(o_ones, BF16, 128)
    cvec = view(o_cvec, F32, NCV)
    wTm = view(o_wTm, BF16, 4, 128)
    E = view(o_E, F32, 4, 128)
    KT = view(o_KT, BF16, 8, 256)
    Vt = view(o_Vt, BF16, 2, 1024)
    zc = view(o_zc, F32, 4, 2)
    st6 = view(o_small, F32, 2, 4, 6)
    mv = view(o_small + 256, F32, 2, 4, 2)
    sdv = view(o_small + 384, F32, 2, 4)
    nmr = view(o_small + 448, F32, 2, 4)
    sq = view(o_sq, BF16, 2, 8, 512)
    stdb = view(o_std, F32, NSTD, 512)
    xT = view(o_xT, F32, 8, TS)
    xn = view(o_xn, BF16, 8, TS)
    u_b = view(o_scr + 0, F32, 4, TS)
    gc_b = view(o_scr + 16384, BF16, 4, TS)
    vn_b = view(o_scr + 24576, BF16, 8, 512)
    yn_b = view(o_scr + 32768, BF16, 8, TS)
    vg_b = view(o_scr + 49152, F32, 4, 512)
    z_b = view(o_scr + 57344, F32, 4, 514)
    yb_b = view(o_scr + 65664, F32, 4, TS)
    q_b = view(o_scr + 0, BF16, 8, TS)
    o_b = view(o_scr + 16384, BF16, 8, TS)
    p_b = view(o_scr + 32768, BF16, 2, 8, 512)
    rden_b = view(o_scr + 49152, F32, 2, 4, 512)
    h_b = view(o_scr + 0, BF16, KF, TS)
    sg_b = view(o_scr + 45056, F32, 2, 512)
    xin_b = view(o_scr + 0, F32, 2, 4, D)
    outT_b = view(o_scr + 32768, F32, 2, 8, 512)
    yst_b = view(o_scr + 65536, F32, 4, D)
    memT = view(o_scr + 32768, F32, 8, MEM)
    memn = view(o_scr + 40960, BF16, 8, MEM)
    wTf = view(o_scr + 49152, F32, 4, 128)
    bspb = view(o_scr + 51200, F32, 4, 128)

    ps = nc.alloc_psum_tensor("ps", [128, 4096], F32)
    bank_ctr = [0]

    def bank():
        b = bank_ctr[0] % 8
        bank_ctr[0] += 1
        return ps[:, b * 512:(b + 1) * 512]

    S = Sched()

    groups = []

    def slot_view(si, dt, *shape):
        return view(o_ring + si * SLOT_B, dt, *shape)

    def a_group(w_d, c0):
        return ("A", [(lambda si: slot_view(si, BF16, 8, 512),
                       w_d[:, c0:c0 + 512].rearrange("(k p) n -> p k n", p=128))])

    def gu_group(i):
        def dv(half):
            return lambda si: view(o_ring + si * SLOT_B + half * 4096, BF16, 8, 256)
        return ("GU", [(dv(0), w_gu_d[:, 256 * i:256 * i + 256].rearrange("(k p) n -> p k n", p=128)),
                       (dv(1), w_gu_d[:, DFF + 256 * i:DFF + 256 * i + 256].rearrange("(k p) n -> p k n", p=128))])

    def dn_group(m):
        def dv(half):
            return lambda si: view(o_ring + si * SLOT_B + half * 11 * 256, BF16, 11, 128)
        return ("DN", [(dv(0), w_dn_d[0:11 * 128, m * 128:(m + 1) * 128].rearrange("(k p) n -> p k n", p=128)),
                       (dv(1), w_dn_d[11 * 128:22 * 128, m * 128:(m + 1) * 128].rearrange("(k p) n -> p k n", p=128))])

    WIN_ORDER = (3, 4, 1, 0, 2)
    for i in range(4):
        groups.append(a_group(w_kv_d, 512 * i))
    for st in range(NST):
        for i in WIN_ORDER:
            groups.append(a_group(w_in_d, 512 * i))
        for i in range(2):
            groups.append(a_group(w_out_d, 512 * i))
        for i in range(2):
            groups.append(a_group(w_q_d, 512 * i))
        for i in range(2):
            groups.append(a_group(w_o_d, 512 * i))
        for i in range(11):
            groups.append(gu_group(i))
        for m in range(8):
            groups.append(dn_group(m))

    ring_state = {"next_dma": 0, "next_acq": 0}

    def ring_issue(gi):
        kind, parts = groups[gi]
        si = gi % NSLOT
        for dvf, src in parts:
            dst = dvf(si)
            S.add("pool", (lambda e, dst=dst, src=src: e.dma_start(out=dst, in_=src)),
                  writes=[dst], dma_key=("ring", si))

    held = []
    sticky = set()

    def ring_pump():
        base = held[0] if held else ring_state["next_acq"]
        want = min(base + NSLOT - 1, len(groups) - 1)
        while ring_state["next_dma"] <= want:
            ring_issue(ring_state["next_dma"])
            ring_state["next_dma"] += 1

    def ring_acquire(stick=False):
        gi = ring_state["next_acq"]
        ring_state["next_acq"] += 1
        held[:] = [h for h in held if h in sticky]
        held.append(gi)
        if stick:
            sticky.add(gi)
        ring_pump()
        kind, _ = groups[gi]
        si = gi % NSLOT
        if kind == "A":
            return slot_view(si, BF16, 8, 512)
        if kind == "GU":
            return slot_view(si, BF16, 2, 8, 256)
        return slot_view(si, BF16, KF, 128)

    def ring_unstick():
        sticky.clear()

    std_ctr = [0]

    def normA(src, nch, Tn, sqv):
        for c in range(nch):
            S.add("act", (lambda e, c=c: e.activation(out=sqv[:, c, 0:Tn], in_=src[:, c, :], func=AF.Square)),
                  reads=[src[:, c, :]], writes=[sqv[:, c, 0:Tn]])

    def normB(src, dst, gbase, nch, N, Tn, sqv):
        b = bank()

        def f(e):
            for c in range(nch):
                ins = e.matmul(b[:, 0:Tn], lhsT=ones, rhs=sqv[:, c, 0:Tn], start=(c == 0), stop=(c == nch - 1))
            return ins
        S.add("pe", f, reads=[ones, sqv[:, 0:nch, 0:Tn]], writes=[b])
        sd = stdb[:, std_ctr[0] % NSTD, 0:Tn]
        std_ctr[0] += 1
        S.add("act", (lambda e: e.activation(out=sd, in_=b[:, 0:Tn], func=AF.Ln, bias=EPS, scale=1.0 / N)),
              reads=[b], writes=[sd])
        S.add("act", (lambda e: e.activation(out=sd, in_=sd, func=AF.Exp, scale=-0.5)), reads=[sd], writes=[sd])
        for c in range(nch):
            S.add("dve", (lambda e, c=c: e.scalar_tensor_tensor(
                out=dst[:, c, :], in0=src[:, c, :], scalar=cvec[:, gbase + c:gbase + c + 1], in1=sd,
                op0=ALU.mult, op1=ALU.mult)),
                reads=[src[:, c, :], cvec, sd], writes=[dst[:, c, :]])

    def mm_w(b, wfn, rhsfn, nk, ncols=T):
        def f(e):
            for k in range(nk):
                ins = e.matmul(b[:, 0:ncols], lhsT=wfn(k), rhs=rhsfn(k), start=(k == 0), stop=(k == nk - 1))
            return ins
        return f

    def tsl_of(s):
        return slice(s * T, (s + 1) * T)

    S.add("sp", lambda e: e.dma_start(out=cvec, in_=cvec_d), writes=[cvec], dma_key="cvec")
    S.add("sp", lambda e: e.dma_start(out=wTf, in_=wspT_d), writes=[wTf], dma_key="wTf")
    S.add("sp", lambda e: e.dma_start(out=bspb, in_=bspb_d), writes=[bspb], dma_key="bspb")
    S.add("sp", lambda e: e.dma_start(out=yst_b[:, 0:2, :], in_=mem_d.rearrange("(j p) d -> p j d", p=128)),
          writes=[yst_b[:, 0:2, :]], dma_key="memin")

    def x_load(st, s):
        t0 = st * TS + s * T
        S.add("sp", (lambda e: e.dma_start(
            out=xin_b[:, s, :, :], in_=x_d[t0:t0 + T, :].rearrange("(j p) d -> p j d", p=128))),
            writes=[xin_b[:, s, :, :]], dma_key=("xin", s))

    x_load(0, 0)
    x_load(0, 1)
    S.add("pool", lambda e: e.memset(ident, 0.0), writes=[ident])
    S.add("pool", lambda e: e.affine_select(out=ident, in_=ident, pattern=[[-1, 128]], compare_op=ALU.not_equal,
                                            fill=1.0, base=0, channel_multiplier=1),
          reads=[ident], writes=[ident])
    S.add("pool", lambda e: e.memset(ones, 1.0), writes=[ones])
    S.add("pool", lambda e: e.memset(zc, 0.0), writes=[zc])
    for h in range(4):
        S.add("pool", (lambda e, h=h: e.affine_select(out=wTf[:, h, :], in_=wTf[:, h, :], pattern=[[1, 128]],
                                                      compare_op=ALU.is_ge, fill=0.0, base=0, channel_multiplier=-1)),
              reads=[wTf[:, h, :]], writes=[wTf[:, h, :]])
    S.add("dve", lambda e: e.tensor_copy(out=wTm, in_=wTf), reads=[wTf], writes=[wTm])
    b = bank()

    def f_rw(e, b=b):
        for h in range(4):
            ins = e.matmul(b[:, h * 128:(h + 1) * 128], lhsT=ones, rhs=wTm[:, h, :], start=True, stop=True)
        return ins
    S.add("pe", f_rw, reads=[ones, wTm], writes=[b])
    for h in range(4):
        S.add("dve", (lambda e, h=h, b=b: e.scalar_tensor_tensor(
            out=E[:, h, :], in0=b[:, h * 128:(h + 1) * 128], scalar=cvec[:, C_BLN + h:C_BLN + h + 1],
            in1=bspb[:, h, :], op0=ALU.mult, op1=ALU.add)),
            reads=[b, cvec, bspb[:, h, :]], writes=[E[:, h, :]])

    for c in range(8):
        b = bank()

        def f(e, c=c, b=b):
            for j in range(2):
                ins = e.transpose(b[:, j * 128:(j + 1) * 128], yst_b[:, j, c * 128:(c + 1) * 128], ident)
            return ins
        S.add("pe", f, reads=[yst_b[:, 0:2, c * 128:(c + 1) * 128], ident], writes=[b])
        if c % 2 == 0:
            S.add("act", (lambda e, c=c, b=b: e.copy(out=memT[:, c, :], in_=b[:, 0:MEM])),
                  reads=[b], writes=[memT[:, c, :]])
        else:
            S.add("dve", (lambda e, c=c, b=b: e.tensor_copy(out=memT[:, c, :], in_=b[:, 0:MEM])),
                  reads=[b], writes=[memT[:, c, :]])
    sq_mem = view(o_std + 4096, BF16, 8, MEM)
    normA(memT, 8, MEM, sq_mem)

    def xpose(s):
        tsl = tsl_of(s)
        for c in range(8):
            b = bank()

            def f(e, c=c, b=b):
                for j in range(4):
                    ins = e.transpose(b[:, j * 128:(j + 1) * 128], xin_b[:, s, j, c * 128:(c + 1) * 128], ident)
                return ins
            S.add("pe", f, reads=[xin_b[:, s, :, c * 128:(c + 1) * 128], ident], writes=[b])
            if c % 2 == 0:
                S.add("act", (lambda e, c=c, b=b: e.copy(out=xT[:, c, tsl], in_=b)),
                      reads=[b], writes=[xT[:, c, tsl]])
            else:
                S.add("dve", (lambda e, c=c, b=b: e.tensor_copy(out=xT[:, c, tsl], in_=b)),
                      reads=[b], writes=[xT[:, c, tsl]])
        normA(xT[:, :, tsl], 8, T, sq[:, s])

    def norm1B():
        for s in range(NSUB):
            tsl = tsl_of(s)
            normB(xT[:, :, tsl], xn[:, :, tsl], C_MIX, 8, D, T, sq[:, s])

    xpose(0)
    xpose(1)
    normB(memT, memn, C_MEM, 8, D, MEM, sq_mem)
    norm1B()
    for gi in range(2):
        g = ring_acquire()
        for mm in range(4):
            c = gi * 4 + mm
            b = bank()
            S.add("pe", mm_w(b, (lambda k, g=g, mm=mm: g[:, k, mm * 128:(mm + 1) * 128]),
                             (lambda k: memn[:, k, :]), 8, ncols=MEM),
                  reads=[g, memn], writes=[b])
            S.add("act", (lambda e, c=c, b=b: e.copy(out=KT[:, c, :], in_=b[:, 0:MEM])),
                  reads=[b], writes=[KT[:, c, :]])
    for gi in range(2):
        g = ring_acquire()
        for mc in range(2):
            b = bank()
            S.add("pe", mm_w(b, (lambda k, mc=mc: memn[:, k, mc * 128:(mc + 1) * 128]),
                             (lambda k, g=g: g[:, k, :]), 8, ncols=512),
                  reads=[g, memn], writes=[b])
            S.add("act", (lambda e, mc=mc, gi=gi, b=b: e.copy(out=Vt[:, mc, gi * 512:(gi + 1) * 512], in_=b)),
                  reads=[b], writes=[Vt[:, mc, gi * 512:(gi + 1) * 512]])

    def wgroup(g, s, src, evac, nmm=4):
        tsl = tsl_of(s)
        for mm in range(nmm):
            b = bank()
            S.add("pe", mm_w(b, (lambda k, g=g, mm=mm: g[:, k, mm * 128:(mm + 1) * 128]),
                             (lambda k, tsl=tsl: src[:, k, tsl]), 8),
                  reads=[g, src[:, :, tsl]], writes=[b])
            evac(mm, b)

    def wgroup_k(wfn, s, src, evac, nmm=4):
        tsl = tsl_of(s)
        banks = [bank() for _ in range(nmm)]
        for k in range(8):
            def f(e, k=k):
                for mm in range(nmm):
                    ins = e.matmul(banks[mm], lhsT=wfn(k, mm), rhs=src[:, k, tsl], start=(k == 0), stop=(k == 7))
                return ins
            S.add("pe", f, reads=[wfn(k, mm) for mm in range(nmm)] + [src[:, k, tsl]], writes=banks)
        for mm in range(nmm):
            evac(mm, banks[mm])

    def proj_residual(src_b, after_s):
        g0 = ring_acquire(stick=True)
        g1 = ring_acquire()
        for s in range(NSUB):
            tsl = tsl_of(s)
            for gi, g in enumerate((g0, g1)):
                def ev(mm, b, gi=gi, tsl=tsl):
                    m = gi * 4 + mm
                    S.add("dve", (lambda e: e.tensor_tensor(out=xT[:, m, tsl], in0=b, in1=xT[:, m, tsl], op=ALU.add)),
                          reads=[b, xT[:, m, tsl]], writes=[xT[:, m, tsl]])
                wgroup(g, s, src_b, ev)
            after_s(s)
        ring_unstick()

    out_dmas = []
    for st in range(NST):
        if st > 0:
            xpose(0)
            xpose(1)

        g = ring_acquire()
        if st > 0:
            norm1B()
        for s in range(NSUB):
            tsl = tsl_of(s)

            def ev(mm, b, tsl=tsl):
                S.add("act", (lambda e: e.copy(out=gc_b[:, mm, tsl], in_=b)), reads=[b], writes=[gc_b[:, mm, tsl]])
            wgroup_k((lambda k, mm, g=g: g[:, k, mm * 128:(mm + 1) * 128]), s, xn, ev)

        g_val = ring_acquire(stick=True)

        def val_part(s):
            tsl = tsl_of(s)
            S.add("dve", (lambda e: e.tensor_copy(out=z_b[:, :, 0:2], in_=zc)), reads=[zc], writes=[z_b[:, :, 0:2]])

            def ev(mm, b):
                S.add("dve", (lambda e: e.tensor_tensor(out=z_b[:, mm, 2:514], in0=b, in1=gc_b[:, mm, tsl],
                                                        op=ALU.mult)),
                      reads=[b, gc_b[:, mm, tsl]], writes=[z_b[:, mm, 2:514]])
            wgroup(g_val, s, xn, ev)
            S.add("dve", (lambda e: e.tensor_copy(out=zc, in_=z_b[:, :, 512:514])),
                  reads=[z_b[:, :, 512:514]], writes=[zc])
            for m in range(4):
                S.add("dve", (lambda e, m=m: e.tensor_scalar(out=yb_b[:, m, tsl], in0=z_b[:, m, 0:512],
                                                             scalar1=cvec[:, C_CW + m:C_CW + m + 1], scalar2=None,
                                                             op0=ALU.mult)),
                      reads=[z_b[:, m, 0:512], cvec], writes=[yb_b[:, m, tsl]])
                for jj in (1, 2):
                    S.add("dve", (lambda e, m=m, jj=jj: e.scalar_tensor_tensor(
                        out=yb_b[:, m, tsl], in0=z_b[:, m, jj:jj + 512],
                        scalar=cvec[:, C_CW + jj * 4 + m:C_CW + jj * 4 + m + 1], in1=yb_b[:, m, tsl],
                        op0=ALU.mult, op1=ALU.add)),
                        reads=[z_b[:, m, jj:jj + 512], cvec, yb_b[:, m, tsl]], writes=[yb_b[:, m, tsl]])
        val_part(0)

        g = ring_acquire()
        for s in range(NSUB):
            for j in range(4):
                b = bank()
                S.add("pe", mm_w(b, (lambda k, s=s, j=j: xn[:, k, s * T + j * 128:s * T + (j + 1) * 128]),
                                 (lambda k, g=g: g[:, k, :]), 8),
                      reads=[g, xn[:, :, s * T + j * 128:s * T + (j + 1) * 128]], writes=[b])
                S.add("act", (lambda e, j=j, b=b: e.activation(out=vg_b[:, j, :], in_=b, func=AF.Gelu_apprx_tanh)),
                      reads=[b], writes=[vg_b[:, j, :]])
                S.add("dve", (lambda e, j=j, s=s: e.bn_stats(out=st6[:, s, j, :], in_=vg_b[:, j, :])),
                      reads=[vg_b[:, j, :]], writes=[st6[:, s, j, :]])
                S.add("dve", (lambda e, j=j, s=s: e.bn_aggr(out=mv[:, s, j, :], in_=st6[:, s, j, :])),
                      reads=[st6[:, s, j, :]], writes=[mv[:, s, j, :]])
            S.add("act", (lambda e, s=s: e.activation(out=sdv[:, s, :], in_=mv[:, s, :, 1], func=AF.Ln, bias=EPS,
                                                      scale=1.0)),
                  reads=[mv[:, s]], writes=[sdv[:, s, :]])
            S.add("act", (lambda e, s=s: e.activation(out=sdv[:, s, :], in_=sdv[:, s, :], func=AF.Exp, scale=-0.5)),
                  reads=[sdv[:, s, :]], writes=[sdv[:, s, :]])
            S.add("dve", (lambda e, s=s: e.scalar_tensor_tensor(out=nmr[:, s, :], in0=mv[:, s, :, 0], scalar=-1.0,
                                                                in1=sdv[:, s, :], op0=ALU.mult, op1=ALU.mult)),
                  reads=[mv[:, s], sdv[:, s, :]], writes=[nmr[:, s, :]])
            for j in range(4):
                S.add("act", (lambda e, s=s, j=j: e.activation(out=vn_b[:, s * 4 + j, :], in_=vg_b[:, j, :],
                                                               func=AF.Identity, bias=nmr[:, s, j:j + 1],
                                                               scale=sdv[:, s, j:j + 1])),
                      reads=[vg_b[:, j, :], nmr[:, s, :], sdv[:, s, :]], writes=[vn_b[:, s * 4 + j, :]])
        val_part(1)
        ring_unstick()

        g_u = ring_acquire()

        def u_part(s):
            tsl = tsl_of(s)

            def ev(mm, b):
                S.add("act", (lambda e: e.activation(out=u_b[:, mm, tsl], in_=b, func=AF.Gelu_apprx_tanh)),
                      reads=[b], writes=[u_b[:, mm, tsl]])
            wgroup(g_u, s, xn, ev)

        def spatial(s):
            tsl = tsl_of(s)
            for h in range(4):
                b = bank()

                def f(e, h=h, b=b):
                    for j in range(4):
                        ins = e.matmul(b[:, j * 128:(j + 1) * 128], lhsT=vn_b[:, s * 4 + j, h * 128:(h + 1) * 128],
                                       rhs=wTm[:, h, :], start=True, stop=True)
                    return ins
                S.add("pe", f, reads=[vn_b[:, s * 4:(s + 1) * 4, h * 128:(h + 1) * 128], wTm[:, h, :]], writes=[b])
                tmp = vg_b[:, h, :]
                S.add("dve", (lambda e, h=h, b=b, tmp=tmp: e.scalar_tensor_tensor(
                    out=tmp.rearrange("p (j t) -> p j t", j=4),
                    in0=b.rearrange("p (j t) -> p j t", j=4),
                    scalar=cvec[:, C_GLN + h:C_GLN + h + 1],
                    in1=E[:, h, :].unsqueeze(1).to_broadcast([128, 4, 128]),
                    op0=ALU.mult, op1=ALU.add)),
                    reads=[b, cvec, E[:, h, :]], writes=[tmp])
                S.add("dve", (lambda e, h=h, tmp=tmp: e.tensor_tensor(out=u_b[:, h, tsl], in0=tmp,
                                                                       in1=u_b[:, h, tsl], op=ALU.mult)),
                      reads=[tmp, u_b[:, h, tsl]], writes=[u_b[:, h, tsl]])

        def gate_b(g_gb, s):
            tsl = tsl_of(s)

            def ev(mm, b):
                S.add("dve", (lambda e: e.tensor_tensor(out=yb_b[:, mm, tsl], in0=b, in1=yb_b[:, mm, tsl],
                                                        op=ALU.mult)),
                      reads=[b, yb_b[:, mm, tsl]], writes=[yb_b[:, mm, tsl]])
            wgroup(g_gb, s, xn, ev)

        def nA_a(s):
            normA(u_b[:, :, tsl_of(s)], 4, T, sq[:, s, 0:4])

        def nA_b(s):
            normA(yb_b[:, :, tsl_of(s)], 4, T, sq[:, s, 4:8])

        def nB_a(s):
            normB(u_b[:, :, tsl_of(s)], yn_b[:, 0:4, tsl_of(s)], C_GA, 4, 512, T, sq[:, s, 0:4])

        def nB_b(s):
            normB(yb_b[:, :, tsl_of(s)], yn_b[:, 4:8, tsl_of(s)], C_GB, 4, 512, T, sq[:, s, 4:8])

        def after_wout(s):
            normA(xT[:, :, tsl_of(s)], 8, T, sq[:, s])

        def resid_ev(gi, tsl):
            def ev(mm, b):
                m = gi * 4 + mm
                S.add("dve", (lambda e: e.tensor_tensor(out=xT[:, m, tsl], in0=b, in1=xT[:, m, tsl], op=ALU.add)),
                      reads=[b, xT[:, m, tsl]], writes=[xT[:, m, tsl]])
            return ev

        u_part(0)
        spatial(0)
        u_part(1)
        nA_a(0)
        spatial(1)
        g_gb = ring_acquire()
        gate_b(g_gb, 0)
        nA_a(1)
        nA_b(0)
        nB_a(0)
        nB_a(1)
        gate_b(g_gb, 1)
        nB_b(0)
        nA_b(1)
        g0 = ring_acquire(stick=True)
        g1 = ring_acquire()
        wgroup_k((lambda k, mm, g=g0: g[:, k, mm * 128:(mm + 1) * 128]), 0, yn_b, resid_ev(0, tsl_of(0)))
        wgroup(g1, 0, yn_b, resid_ev(1, tsl_of(0)))
        after_wout(0)
        nB_b(1)
        wgroup(g0, 1, yn_b, resid_ev(0, tsl_of(1)))
        wgroup(g1, 1, yn_b, resid_ev(1, tsl_of(1)))
        after_wout(1)
        ring_unstick()

        for s in range(NSUB):
            tsl = tsl_of(s)
            normB(xT[:, :, tsl], xn[:, :, tsl], C_ATT, 8, D, T, sq[:, s])
        for gi in range(2):
            g = ring_acquire()
            for s in range(NSUB):
                tsl = tsl_of(s)

                def ev(mm, b, gi=gi, tsl=tsl):
                    m = gi * 4 + mm
                    S.add("act", (lambda e: e.activation(out=q_b[:, m, tsl], in_=b, func=AF.Copy, scale=0.0625)),
                          reads=[b], writes=[q_b[:, m, tsl]])
                if gi == 0:
                    wgroup_k((lambda k, mm, g=g: g[:, k, mm * 128:(mm + 1) * 128]), s, xn, ev)
                else:
                    wgroup(g, s, xn, ev)
        for s in range(NSUB):
            tsl = tsl_of(s)
            for h in range(4):
                for mc in range(2):
                    b = bank()

                    def f(e, h=h, mc=mc, b=b, tsl=tsl):
                        for half in range(2):
                            ins = e.matmul(b, lhsT=KT[:, h * 2 + half, mc * 128:(mc + 1) * 128],
                                           rhs=q_b[:, h * 2 + half, tsl], start=(half == 0), stop=(half == 1))
                        return ins
                    S.add("pe", f, reads=[KT[:, h * 2:h * 2 + 2, :], q_b[:, h * 2:h * 2 + 2, tsl]], writes=[b])
                    S.add("act", (lambda e, h=h, mc=mc, b=b, s=s: e.activation(out=p_b[:, s, h * 2 + mc, :], in_=b,
                                                                               func=AF.Exp)),
                          reads=[b], writes=[p_b[:, s, h * 2 + mc, :]])
        for s in range(NSUB):
            tsl = tsl_of(s)
            for h in range(4):
                b = bank()

                def f(e, h=h, b=b, s=s):
                    for mc in range(2):
                        ins = e.matmul(b, lhsT=ones, rhs=p_b[:, s, h * 2 + mc, :], start=(mc == 0), stop=(mc == 1))
                    return ins
                S.add("pe", f, reads=[ones, p_b[:, s, h * 2:h * 2 + 2, :]], writes=[b])
                S.add("act", (lambda e, h=h, b=b, s=s: e.activation(out=rden_b[:, s, h, :], in_=b, func=AF.Ln)),
                      reads=[b], writes=[rden_b[:, s, h, :]])
                S.add("act", (lambda e, h=h, s=s: e.activation(out=rden_b[:, s, h, :], in_=rden_b[:, s, h, :],
                                                               func=AF.Exp, scale=-1.0)),
                      reads=[rden_b[:, s, h, :]], writes=[rden_b[:, s, h, :]])
            for c in range(8):
                h = c // 2
                b = bank()

                def f(e, c=c, h=h, b=b, s=s):
                    for mc in range(2):
                        ins = e.matmul(b, lhsT=Vt[:, mc, c * 128:(c + 1) * 128], rhs=p_b[:, s, h * 2 + mc, :],
                                       start=(mc == 0), stop=(mc == 1))
                    return ins
                S.add("pe", f, reads=[Vt[:, :, c * 128:(c + 1) * 128], p_b[:, s, h * 2:h * 2 + 2, :]], writes=[b])
                S.add("dve", (lambda e, c=c, h=h, b=b, tsl=tsl, s=s: e.tensor_tensor(
                    out=o_b[:, c, tsl], in0=b, in1=rden_b[:, s, h, :], op=ALU.mult)),
                    reads=[b, rden_b[:, s, h, :]], writes=[o_b[:, c, tsl]])
        proj_residual(o_b, after_wout)

        sg_ctr = [0]
        for s in range(NSUB):
            tsl = tsl_of(s)
            normB(xT[:, :, tsl], xn[:, :, tsl], C_FFN, 8, D, T, sq[:, s])

        def swiglu(bg, bu, j, tsl):
            sg = sg_b[:, sg_ctr[0] % 2, :]
            sg_ctr[0] += 1
            S.add("act", (lambda e: e.activation(out=sg, in_=bg, func=AF.Silu)), reads=[bg], writes=[sg])
            S.add("dve", (lambda e: e.tensor_tensor(out=h_b[:, j, tsl], in0=bu, in1=sg, op=ALU.mult)),
                  reads=[bu, sg], writes=[h_b[:, j, tsl]])

        for gi in range(11):
            g = ring_acquire()
            for s in range(NSUB):
                tsl = tsl_of(s)
                if gi == 0:
                    got = {}

                    def ev(mm, b, got=got, tsl=tsl, gi=gi):
                        got[mm] = b
                        if mm % 2 == 1:
                            swiglu(got[mm - 1], b, gi * 2 + mm // 2, tsl)
                    wgroup_k((lambda k, mm, g=g: g[:, mm % 2, k, (mm // 2) * 128:(mm // 2 + 1) * 128]), s, xn, ev)
                    continue
                for i in range(2):
                    bg = bank()
                    S.add("pe", mm_w(bg, (lambda k, g=g, i=i: g[:, 0, k, i * 128:(i + 1) * 128]),
                                     (lambda k, tsl=tsl: xn[:, k, tsl]), 8),
                          reads=[g, xn[:, :, tsl]], writes=[bg])
                    bu = bank()
                    S.add("pe", mm_w(bu, (lambda k, g=g, i=i: g[:, 1, k, i * 128:(i + 1) * 128]),
                                     (lambda k, tsl=tsl: xn[:, k, tsl]), 8),
                          reads=[g, xn[:, :, tsl]], writes=[bu])
                    swiglu(bg, bu, gi * 2 + i, tsl)
        for m in range(8):
            g = ring_acquire()
            for s in range(NSUB):
                tsl = tsl_of(s)
                b = bank()
                S.add("pe", mm_w(b, (lambda k, g=g: g[:, k, :]), (lambda k, tsl=tsl: h_b[:, k, tsl]), KF),
                      reads=[g, h_b[:, :, tsl]], writes=[b])
                S.add("dve", (lambda e, m=m, b=b, tsl=tsl: e.tensor_tensor(
                    out=xT[:, m, tsl], in0=b, in1=xT[:, m, tsl], op=ALU.add)),
                    reads=[b, xT[:, m, tsl]], writes=[xT[:, m, tsl]])

        if st + 1 < NST:
            x_load(st + 1, 0)
            x_load(st + 1, 1)

        for s in range(NSUB):
            normA(xT[:, :, tsl_of(s)], 8, T, sq[:, s])
        for s in range(NSUB):
            tsl = tsl_of(s)
            normB(xT[:, :, tsl], outT_b[:, s], C_FIN, 8, D, T, sq[:, s])
        for s in range(NSUB):
            t0 = st * TS + s * T
            tsl = tsl_of(s)
            for j in range(4):
                for half in range(2):
                    b = bank()

                    def f(e, j=j, half=half, b=b, s=s):
                        for cc in range(4):
                            c = half * 4 + cc
                            ins = e.transpose(b[:, cc * 128:(cc + 1) * 128], outT_b[:, s, c, j * 128:(j + 1) * 128],
                                              ident)
                        return ins
                    S.add("pe", f, reads=[outT_b[:, s, half * 4:half * 4 + 4, j * 128:(j + 1) * 128], ident],
                          writes=[b])
                    dsts = yst_b[:, j, half * 512:(half + 1) * 512]
                    if (j + half) % 2 == 0:
                        S.add("act", (lambda e, dsts=dsts, b=b: e.copy(out=dsts, in_=b)), reads=[b], writes=[dsts])
                    else:
                        S.add("dve", (lambda e, dsts=dsts, b=b: e.tensor_copy(out=dsts, in_=b)), reads=[b], writes=[dsts])
            od = S.add("sp", (lambda e, t0=t0: e.dma_start(
                out=y_d[t0:t0 + T, :].rearrange("(j p) d -> p j d", p=128), in_=yst_b)),
                reads=[yst_b], dma_key="yout")
            out_dmas.append(od)

    S.add("sp", None, extra_deps=[out_dmas[-1]])
    assert ring_state["next_acq"] == len(groups), (ring_state, len(groups))

    S.finalize()
    sems = {e: nc.alloc_semaphore("sem_" + e) for e in Sched.CENG}
    dsems = {}
    for key in S.dma_cnt:
        dsems[key] = nc.alloc_semaphore("dsem_" + "_".join(str(k) for k in (key if isinstance(key, tuple) else (key,))))

    with nc.Block() as block:
        @block.tensor
        def _(e):
            S.emit("pe", e, sems, dsems)

        @block.scalar
        def _(e):
            S.emit("act", e, sems, dsems)

        @block.vector
        def _(e):
            S.emit("dve", e, sems, dsems)

        @block.gpsimd
        def _(e):
            S.emit("pool", e, sems, dsems)

        @block.sync
        def _(e):
            S.emit("sp", e, sems, dsems)
    return nc


def _host_layout(inputs):
    f = lambda a: np.ascontiguousarray(np.asarray(a, dtype=np.float32))
    col = lambda v, n: f(v).reshape(n, 128).T
    cw = f(inputs["conv_w"])
    cvec = np.concatenate([
        col(inputs["ln_mix_g"], 8), col(inputs["ln_attn_g"], 8), col(inputs["ln_ffn_g"], 8),
        col(inputs["ln_final_g"], 8), col(inputs["ln_mem_g"], 8),
        col(inputs["grp_norm_a"], 4), col(inputs["grp_norm_b"], 4),
        col(inputs["sgu_ln_g"], 4), col(inputs["sgu_ln_b"], 4),
        col(cw[0], 4), col(cw[1], 4), col(cw[2], 4),
    ], axis=1)
    assert cvec.shape == (128, NCV)
    wspT = f(np.transpose(f(inputs["w_spatial"]), (2, 0, 1)))
    bspb = f(np.broadcast_to(f(inputs["b_spatial"])[None, :, :], (128, 4, 128)))
    shared = {
        "w_in": f(inputs["w_in"]), "w_out": f(inputs["w_out"]), "w_q": f(inputs["w_q"]),
        "w_kv": f(inputs["w_kv"]), "w_o": f(inputs["w_o"]), "w_gate_up": f(inputs["w_gate_up"]),
        "w_down": f(inputs["w_down"]), "cvec": f(cvec), "wspT": wspT, "bspb": bspb,
    }
    x = f(inputs["x"])
    mem = f(inputs["mem"])
    in_maps = []
    for b in range(8):
        d = dict(shared)
        d["x"] = x[b]
        d["mem"] = mem[b]
        in_maps.append(d)
    return in_maps


def kernel(**inputs):
    in_maps = _host_layout(inputs)
    nc = build_program()
    res = run_bass_kernel_spmd(nc, in_maps, core_ids=list(range(8)))
    out = np.stack([np.asarray(r["y"], dtype=np.float32) for r in res.results], axis=0)
    return out
```

```python
import numpy as np
import concourse.bass as bass
import concourse.mybir as mybir
from concourse.bass_utils import run_bass_kernel_spmd

F32 = mybir.dt.float32
BF16 = mybir.dt.bfloat16
U8 = mybir.dt.uint8
AF = mybir.ActivationFunctionType
ALU = mybir.AluOpType

D = 1024
KD = 8
SEQ = 4096
TS = 1024
T = 512
NSUB = TS // T
NST = SEQ // TS
MEM = 256
DFF = 2816
KF = DFF // 128
EPS = 1e-6
NSLOT = 5
SLOT_B = 8192

C_MIX, C_ATT, C_FFN, C_FIN, C_MEM = 0, 8, 16, 24, 32
C_GA, C_GB, C_GLN, C_BLN, C_CW = 40, 44, 48, 52, 56
NCV = 68

_ES = {F32: 4, BF16: 2, U8: 1}


def _esize(dt):
    return _ES[dt]


def _hull(ap):
    es = _esize(ap.dtype)
    pairs = [tuple(p) for p in ap.ap]
    pstride = pairs[0][0]
    off = ap.offset % pstride if pstride else ap.offset
    ext = 1
    for st, cnt in pairs[1:]:
        ext += (cnt - 1) * abs(st)
    return ap.tensor.name, off * es, (off + ext) * es


class Op:
    __slots__ = ("eng", "fn", "idx", "waits", "signal", "sigval", "dma_key", "dma_cnt", "know", "gid")


class Sched:
    CENG = ("pe", "act", "dve", "pool")
    GRAN = 128

    def __init__(self):
        self.ops = []
        self.eng_ops = {e: [] for e in ("pe", "act", "dve", "pool", "sp")}
        self.nidx = {e: 0 for e in self.CENG}
        self.blocks = {}
        self.know = {e: {} for e in ("pe", "act", "dve", "pool", "sp")}
        self.dma_cnt = {}

    def _blocks(self, ap):
        name, a, b = _hull(ap)
        g = 2048 if name.startswith("ps") else self.GRAN
        return [(name, i) for i in range(a // g, (b - 1) // g + 1)]

    def add(self, eng, fn, reads=(), writes=(), dma_key=None, extra_deps=()):
        op = Op()
        op.eng = eng
        op.fn = fn
        op.dma_key = dma_key
        op.signal = False
        op.sigval = None
        op.waits = []
        op.gid = len(self.ops)
        is_dma = dma_key is not None
        if is_dma:
            self.dma_cnt[dma_key] = self.dma_cnt.get(dma_key, 0) + 16
            op.dma_cnt = self.dma_cnt[dma_key]
            op.idx = None
        else:
            op.dma_cnt = None
            if eng in self.CENG and fn is not None:
                op.idx = self.nidx[eng]
                self.nidx[eng] += 1
            else:
                op.idx = None
        deps = {}
        for d in extra_deps:
            deps[d] = True
        for ap in reads:
            for blk in self._blocks(ap):
                ent = self.blocks.get(blk)
                if ent is None:
                    ent = [None, {}, []]
                    self.blocks[blk] = ent
                if ent[0] is not None:
                    deps[ent[0]] = True
        for ap in writes:
            for blk in self._blocks(ap):
                ent = self.blocks.get(blk)
                if ent is None:
                    ent = [None, {}, []]
                    self.blocks[blk] = ent
                if ent[0] is not None and ent[0] not in deps:
                    deps[ent[0]] = False
                for r in ent[1].values():
                    if r not in deps:
                        deps[r] = False
                for r in ent[2]:
                    if r not in deps:
                        deps[r] = False
        deps.pop(op, None)
        kn = self.know[eng]
        for a in sorted(deps, key=lambda o: -o.gid):
            raw = deps[a]
            if a.dma_key is None:
                if a.idx is None:
                    continue
                if a.eng == eng and not is_dma:
                    if eng == "pe":
                        continue
                key, val = a.eng, a.idx + 1
            else:
                key, val = ("d", a.dma_key), a.dma_cnt
            if kn.get(key, 0) >= val:
                continue
            op.waits.append(a)
            a.signal = True
            for k, v in a.know.items():
                if kn.get(k, 0) < v:
                    kn[k] = v
        op.know = dict(kn)
        if is_dma:
            op.know[("d", dma_key)] = op.dma_cnt
        elif op.idx is not None:
            op.know[eng] = op.idx + 1
        for ap in reads:
            for blk in self._blocks(ap):
                ent = self.blocks[blk]
                if is_dma or op.idx is None:
                    ent[2].append(op)
                else:
                    ent[1][eng] = op
        for ap in writes:
            for blk in self._blocks(ap):
                ent = self.blocks[blk]
                ent[0] = op
                ent[1] = {}
                ent[2] = []
        self.ops.append(op)
        self.eng_ops[eng].append(op)
        return op

    def finalize(self):
        for e in self.CENG:
            n = 0
            for op in self.eng_ops[e]:
                if op.dma_key is None and op.idx is not None and op.signal:
                    n += 1
                    op.sigval = n

    def emit(self, eng, handle, sems, dsems):
        for op in self.eng_ops[eng]:
            for a in op.waits:
                if a.dma_key is None:
                    handle.wait_ge(sems[a.eng], a.sigval)
                else:
                    handle.wait_ge(dsems[a.dma_key], a.dma_cnt)
            if op.fn is None:
                continue
            ins = op.fn(handle)
            if op.dma_key is not None:
                ins.then_inc(dsems[op.dma_key], 16)
            elif op.signal:
                ins.then_inc(sems[op.eng], 1)


def build_program():
    nc = bass.Bass("TRN2", target_bir_lowering=False)
    x_d = nc.dram_tensor("x", [SEQ, D], F32, kind="ExternalInput").ap()
    mem_d = nc.dram_tensor("mem", [MEM, D], F32, kind="ExternalInput").ap()
    w_in_d = nc.dram_tensor("w_in", [D, 2560], F32, kind="ExternalInput").ap()
    w_out_d = nc.dram_tensor("w_out", [D, D], F32, kind="ExternalInput").ap()
    w_q_d = nc.dram_tensor("w_q", [D, D], F32, kind="ExternalInput").ap()
    w_kv_d = nc.dram_tensor("w_kv", [D, 2 * D], F32, kind="ExternalInput").ap()
    w_o_d = nc.dram_tensor("w_o", [D, D], F32, kind="ExternalInput").ap()
    w_gu_d = nc.dram_tensor("w_gate_up", [D, 2 * DFF], F32, kind="ExternalInput").ap()
    w_dn_d = nc.dram_tensor("w_down", [DFF, D], F32, kind="ExternalInput").ap()
    cvec_d = nc.dram_tensor("cvec", [128, NCV], F32, kind="ExternalInput").ap()
    wspT_d = nc.dram_tensor("wspT", [128, 4, 128], F32, kind="ExternalInput").ap()
    bspb_d = nc.dram_tensor("bspb", [128, 4, 128], F32, kind="ExternalInput").ap()
    y_d = nc.dram_tensor("y", [SEQ, D], F32, kind="ExternalOutput").ap()

    top = [0]

    def alloc(nbytes, align=128):
        off = (top[0] + align - 1) // align * align
        top[0] = off + nbytes
        return off

    NSTD = 4
    SCR = 83968
    o_ident = alloc(512)
    o_ones = alloc(256)
    o_cvec = alloc(NCV * 4)
    o_wTm = alloc(1024)
    o_E = alloc(2048)
    o_KT = alloc(4096)
    o_Vt = alloc(4096)
    o_zc = alloc(32)
    o_small = alloc(1024)
    o_sq = alloc(16384)
    o_std = alloc(NSTD * 2048)
    o_xT = alloc(32768)
    o_xn = alloc(16384)
    o_ring = alloc(NSLOT * SLOT_B)
    o_scr = alloc(SCR)
    total = top[0]
    big = nc.alloc_sbuf_tensor("big", [128, total], U8)

    def view(off, dt, *shape):
        n = 1
        for s_ in shape:
            n *= s_
        ap = big[:, off:off + n * _esize(dt)].bitcast(dt)
        if len(shape) == 2:
            ap = ap.rearrange("p (a b) -> p a b", a=shape[0])
        elif len(shape) == 3:
            ap = ap.rearrange("p (a b c) -> p a b c", a=shape[0], b=shape[1])
        return ap

    ident = view(o_ident, F32, 128)
    ones = view(o_ones, BF16, 128)
    cvec = view(o_cvec, F32, NCV)
    wTm = view(o_wTm, BF16, 4, 128)
    E = view(o_E, F32, 4, 128)
    KT = view(o_KT, BF16, 8, 256)
    Vt = view(o_Vt, BF16, 2, 1024)
    zc = view(o_zc, F32, 4, 2)
    st6 = view(o_small, F32, 2, 4, 6)
    mv = view(o_small + 256, F32, 2, 4, 2)
    sdv = view(o_small + 384, F32, 2, 4)
    nmr = view(o_small + 448, F32, 2, 4)
    sq = view(o_sq, BF16, 2, 8, 512)
    stdb = view(o_std, F32, NSTD, 512)
    xT = view(o_xT, F32, 8, TS)
    xn = view(o_xn, BF16, 8, TS)
    u_b = view(o_scr + 0, F32, 4, TS)
    gc_b = view(o_scr + 16384, BF16, 4, TS)
    vn_b = view(o_scr + 24576, BF16, 8, 512)
    yn_b = view(o_scr + 32768, BF16, 8, TS)
    vg_b = view(o_scr + 49152, F32, 4, 512)
    z_b = view(o_scr + 57344, F32, 4, 514)
    yb_b = view(o_scr + 65664, F32, 4, TS)
    q_b = view(o_scr + 0, BF16, 8, TS)
    o_b = view(o_scr + 16384, BF16, 8, TS)
    p_b = view(o_scr + 32768, BF16, 2, 8, 512)
    rden_b = view(o_scr + 49152, F32, 2, 4, 512)
    h_b = view(o_scr + 0, BF16, KF, TS)
    sg_b = view(o_scr + 45056, F32, 2, 512)
    xin_b = view(o_scr + 0, F32, 2, 4, D)
    outT_b = view(o_scr + 32768, F32, 2, 8, 512)
    yst_b = view(o_scr + 65536, F32, 4, D)
    yst2_b = view(o_scr + 32768, F32, 4, D)
    memT = view(o_scr + 32768, F32, 8, MEM)
    memn = view(o_scr + 40960, BF16, 8, MEM)
    wTf = view(o_scr + 49152, F32, 4, 128)
    bspb = view(o_scr + 51200, F32, 4, 128)

    ps = nc.alloc_psum_tensor("ps", [128, 4096], F32)
    bank_ctr = [0]

    def bank():
        b = bank_ctr[0] % 8
        bank_ctr[0] += 1
        return ps[:, b * 512:(b + 1) * 512]

    S = Sched()

    groups = []

    def slot_view(si, dt, *shape):
        return view(o_ring + si * SLOT_B, dt, *shape)

    def a_group(w_d, c0):
        return ("A", [(lambda si: slot_view(si, BF16, 8, 512),
                       w_d[:, c0:c0 + 512].rearrange("(k p) n -> p k n", p=128))])

    def gu_group(i):
        def dv(half):
            return lambda si: view(o_ring + si * SLOT_B + half * 4096, BF16, 8, 256)
        return ("GU", [(dv(0), w_gu_d[:, 256 * i:256 * i + 256].rearrange("(k p) n -> p k n", p=128)),
                       (dv(1), w_gu_d[:, DFF + 256 * i:DFF + 256 * i + 256].rearrange("(k p) n -> p k n", p=128))])

    def dn_group(m):
        def dv(half):
            return lambda si: view(o_ring + si * SLOT_B + half * 11 * 256, BF16, 11, 128)
        return ("DN", [(dv(0), w_dn_d[0:11 * 128, m * 128:(m + 1) * 128].rearrange("(k p) n -> p k n", p=128)),
                       (dv(1), w_dn_d[11 * 128:22 * 128, m * 128:(m + 1) * 128].rearrange("(k p) n -> p k n", p=128))])

    WIN_ORDER = (3, 4, 1, 0, 2)
    for i in range(4):
        groups.append(a_group(w_kv_d, 512 * i))
    for st in range(NST):
        for i in WIN_ORDER:
            groups.append(a_group(w_in_d, 512 * i))
        for i in range(2):
            groups.append(a_group(w_out_d, 512 * i))
        for i in range(2):
            groups.append(a_group(w_q_d, 512 * i))
        for i in range(2):
            groups.append(a_group(w_o_d, 512 * i))
        for i in range(11):
            groups.append(gu_group(i))
        for m in range(8):
            groups.append(dn_group(m))

    ring_state = {"next_dma": 0, "next_acq": 0}

    def ring_issue(gi):
        kind, parts = groups[gi]
        si = gi % NSLOT
        for dvf, src in parts:
            dst = dvf(si)
            S.add("pool", (lambda e, dst=dst, src=src: e.dma_start(out=dst, in_=src)),
                  writes=[dst], dma_key=("ring", si))

    held = []
    sticky = set()

    def ring_pump():
        base = held[0] if held else ring_state["next_acq"]
        want = min(base + NSLOT - 1, len(groups) - 1)
        while ring_state["next_dma"] <= want:
            ring_issue(ring_state["next_dma"])
            ring_state["next_dma"] += 1

    def ring_acquire(stick=False):
        gi = ring_state["next_acq"]
        ring_state["next_acq"] += 1
        held[:] = [h for h in held if h in sticky]
        held.append(gi)
        if stick:
            sticky.add(gi)
        ring_pump()
        kind, _ = groups[gi]
        si = gi % NSLOT
        if kind == "A":
            return slot_view(si, BF16, 8, 512)
        if kind == "GU":
            return slot_view(si, BF16, 2, 8, 256)
        return slot_view(si, BF16, KF, 128)

    def ring_unstick():
        sticky.clear()

    std_ctr = [0]

    def normA(src, nch, Tn, sqv):
        for c in range(nch):
            S.add("act", (lambda e, c=c: e.activation(out=sqv[:, c, 0:Tn], in_=src[:, c, :], func=AF.Square)),
                  reads=[src[:, c, :]], writes=[sqv[:, c, 0:Tn]])

    def normB(src, dst, gbase, nch, N, Tn, sqv):
        b = bank()

        def f(e):
            for c in range(nch):
                ins = e.matmul(b[:, 0:Tn], lhsT=ones, rhs=sqv[:, c, 0:Tn], start=(c == 0), stop=(c == nch - 1))
            return ins
        S.add("pe", f, reads=[ones, sqv[:, 0:nch, 0:Tn]], writes=[b])
        sd = stdb[:, std_ctr[0] % NSTD, 0:Tn]
        std_ctr[0] += 1
        S.add("act", (lambda e: e.activation(out=sd, in_=b[:, 0:Tn], func=AF.Ln, bias=EPS, scale=1.0 / N)),
              reads=[b], writes=[sd])
        S.add("act", (lambda e: e.activation(out=sd, in_=sd, func=AF.Exp, scale=-0.5)), reads=[sd], writes=[sd])
        for c in range(nch):
            S.add("dve", (lambda e, c=c: e.scalar_tensor_tensor(
                out=dst[:, c, :], in0=src[:, c, :], scalar=cvec[:, gbase + c:gbase + c + 1], in1=sd,
                op0=ALU.mult, op1=ALU.mult)),
                reads=[src[:, c, :], cvec, sd], writes=[dst[:, c, :]])

    def mm_w(b, wfn, rhsfn, nk, ncols=T):
        def f(e):
            for k in range(nk):
                ins = e.matmul(b[:, 0:ncols], lhsT=wfn(k), rhs=rhsfn(k), start=(k == 0), stop=(k == nk - 1))
            return ins
        return f

    def tsl_of(s):
        return slice(s * T, (s + 1) * T)

    S.add("sp", lambda e: e.dma_start(out=cvec, in_=cvec_d), writes=[cvec], dma_key="cvec")
    S.add("sp", lambda e: e.dma_start(out=wTf, in_=wspT_d), writes=[wTf], dma_key="wTf")
    S.add("sp", lambda e: e.dma_start(out=bspb, in_=bspb_d), writes=[bspb], dma_key="bspb")
    S.add("sp", lambda e: e.dma_start(out=yst_b[:, 0:2, :], in_=mem_d.rearrange("(j p) d -> p j d", p=128)),
          writes=[yst_b[:, 0:2, :]], dma_key="memin")

    def x_load(st, s):
        t0 = st * TS + s * T
        S.add("sp", (lambda e: e.dma_start(
            out=xin_b[:, s, :, :], in_=x_d[t0:t0 + T, :].rearrange("(j p) d -> p j d", p=128))),
            writes=[xin_b[:, s, :, :]], dma_key=("xin", s))

    x_load(0, 0)
    x_load(0, 1)
    S.add("pool", lambda e: e.memset(ident, 0.0), writes=[ident])
    S.add("pool", lambda e: e.affine_select(out=ident, in_=ident, pattern=[[-1, 128]], compare_op=ALU.not_equal,
                                            fill=1.0, base=0, channel_multiplier=1),
          reads=[ident], writes=[ident])
    S.add("pool", lambda e: e.memset(ones, 1.0), writes=[ones])
    S.add("pool", lambda e: e.memset(zc, 0.0), writes=[zc])
    for h in range(4):
        S.add("pool", (lambda e, h=h: e.affine_select(out=wTf[:, h, :], in_=wTf[:, h, :], pattern=[[1, 128]],
                                                      compare_op=ALU.is_ge, fill=0.0, base=0, channel_multiplier=-1)),
              reads=[wTf[:, h, :]], writes=[wTf[:, h, :]])
    S.add("dve", lambda e: e.tensor_copy(out=wTm, in_=wTf), reads=[wTf], writes=[wTm])
    b = bank()

    def f_rw(e, b=b):
        for h in range(4):
            ins = e.matmul(b[:, h * 128:(h + 1) * 128], lhsT=ones, rhs=wTm[:, h, :], start=True, stop=True)
        return ins
    S.add("pe", f_rw, reads=[ones, wTm], writes=[b])
    for h in range(4):
        S.add("dve", (lambda e, h=h, b=b: e.scalar_tensor_tensor(
            out=E[:, h, :], in0=b[:, h * 128:(h + 1) * 128], scalar=cvec[:, C_BLN + h:C_BLN + h + 1],
            in1=bspb[:, h, :], op0=ALU.mult, op1=ALU.add)),
            reads=[b, cvec, bspb[:, h, :]], writes=[E[:, h, :]])

    for c in range(8):
        b = bank()

        def f(e, c=c, b=b):
            for j in range(2):
                ins = e.transpose(b[:, j * 128:(j + 1) * 128], yst_b[:, j, c * 128:(c + 1) * 128], ident)
            return ins
        S.add("pe", f, reads=[yst_b[:, 0:2, c * 128:(c + 1) * 128], ident], writes=[b])
        if c % 2 == 0:
            S.add("act", (lambda e, c=c, b=b: e.copy(out=memT[:, c, :], in_=b[:, 0:MEM])),
                  reads=[b], writes=[memT[:, c, :]])
        else:
            S.add("dve", (lambda e, c=c, b=b: e.tensor_copy(out=memT[:, c, :], in_=b[:, 0:MEM])),
                  reads=[b], writes=[memT[:, c, :]])
    sq_mem = view(o_std + 4096, BF16, 8, MEM)
    normA(memT, 8, MEM, sq_mem)

    def xpose(s):
        tsl = tsl_of(s)
        for c in range(8):
            b = bank()

            def f(e, c=c, b=b):
                for j in range(4):
                    ins = e.transpose(b[:, j * 128:(j + 1) * 128], xin_b[:, s, j, c * 128:(c + 1) * 128], ident)
                return ins
            S.add("pe", f, reads=[xin_b[:, s, :, c * 128:(c + 1) * 128], ident], writes=[b])
            if c % 2 == 0:
                S.add("act", (lambda e, c=c, b=b: e.copy(out=xT[:, c, tsl], in_=b)),
                      reads=[b], writes=[xT[:, c, tsl]])
            else:
                S.add("dve", (lambda e, c=c, b=b: e.tensor_copy(out=xT[:, c, tsl], in_=b)),
                      reads=[b], writes=[xT[:, c, tsl]])
        normA(xT[:, :, tsl], 8, T, sq[:, s])

    def norm1B():
        for s in range(NSUB):
            tsl = tsl_of(s)
            normB(xT[:, :, tsl], xn[:, :, tsl], C_MIX, 8, D, T, sq[:, s])

    xpose(0)
    xpose(1)
    normB(memT, memn, C_MEM, 8, D, MEM, sq_mem)
    norm1B()
    for gi in range(2):
        g = ring_acquire()
        for mm in range(4):
            c = gi * 4 + mm
            b = bank()
            S.add("pe", mm_w(b, (lambda k, g=g, mm=mm: g[:, k, mm * 128:(mm + 1) * 128]),
                             (lambda k: memn[:, k, :]), 8, ncols=MEM),
                  reads=[g, memn], writes=[b])
            S.add("act", (lambda e, c=c, b=b: e.copy(out=KT[:, c, :], in_=b[:, 0:MEM])),
                  reads=[b], writes=[KT[:, c, :]])
    for gi in range(2):
        g = ring_acquire()
        for mc in range(2):
            b = bank()
            S.add("pe", mm_w(b, (lambda k, mc=mc: memn[:, k, mc * 128:(mc + 1) * 128]),
                             (lambda k, g=g: g[:, k, :]), 8, ncols=512),
                  reads=[g, memn], writes=[b])
            S.add("act", (lambda e, mc=mc, gi=gi, b=b: e.copy(out=Vt[:, mc, gi * 512:(gi + 1) * 512], in_=b)),
                  reads=[b], writes=[Vt[:, mc, gi * 512:(gi + 1) * 512]])

    def wgroup(g, s, src, evac, nmm=4):
        tsl = tsl_of(s)
        for mm in range(nmm):
            b = bank()
            S.add("pe", mm_w(b, (lambda k, g=g, mm=mm: g[:, k, mm * 128:(mm + 1) * 128]),
                             (lambda k, tsl=tsl: src[:, k, tsl]), 8),
                  reads=[g, src[:, :, tsl]], writes=[b])
            evac(mm, b)

    def wgroup_k(wfn, s, src, evac, nmm=4):
        tsl = tsl_of(s)
        banks = [bank() for _ in range(nmm)]
        for k in range(8):
            def f(e, k=k):
                for mm in range(nmm):
                    ins = e.matmul(banks[mm], lhsT=wfn(k, mm), rhs=src[:, k, tsl], start=(k == 0), stop=(k == 7))
                return ins
            S.add("pe", f, reads=[wfn(k, mm) for mm in range(nmm)] + [src[:, k, tsl]], writes=banks)
        for mm in range(nmm):
            evac(mm, banks[mm])

    def proj_residual(src_b, after_s):
        g0 = ring_acquire(stick=True)
        g1 = ring_acquire()
        for s in range(NSUB):
            tsl = tsl_of(s)
            for gi, g in enumerate((g0, g1)):
                def ev(mm, b, gi=gi, tsl=tsl):
                    m = gi * 4 + mm
                    S.add("dve", (lambda e: e.tensor_tensor(out=xT[:, m, tsl], in0=b, in1=xT[:, m, tsl], op=ALU.add)),
                          reads=[b, xT[:, m, tsl]], writes=[xT[:, m, tsl]])
                wgroup(g, s, src_b, ev)
            after_s(s)
        ring_unstick()

    out_dmas = []
    for st in range(NST):
        if st > 0:
            xpose(0)
            xpose(1)

        g = ring_acquire()
        if st > 0:
            norm1B()
        for s in range(NSUB):
            tsl = tsl_of(s)

            def ev(mm, b, tsl=tsl):
                S.add("act", (lambda e: e.copy(out=gc_b[:, mm, tsl], in_=b)), reads=[b], writes=[gc_b[:, mm, tsl]])
            wgroup_k((lambda k, mm, g=g: g[:, k, mm * 128:(mm + 1) * 128]), s, xn, ev)

        g_val = ring_acquire(stick=True)

        def val_part(s):
            tsl = tsl_of(s)
            S.add("dve", (lambda e: e.tensor_copy(out=z_b[:, :, 0:2], in_=zc)), reads=[zc], writes=[z_b[:, :, 0:2]])

            def ev(mm, b):
                S.add("dve", (lambda e: e.tensor_tensor(out=z_b[:, mm, 2:514], in0=b, in1=gc_b[:, mm, tsl],
                                                        op=ALU.mult)),
                      reads=[b, gc_b[:, mm, tsl]], writes=[z_b[:, mm, 2:514]])
            wgroup(g_val, s, xn, ev)
            S.add("dve", (lambda e: e.tensor_copy(out=zc, in_=z_b[:, :, 512:514])),
                  reads=[z_b[:, :, 512:514]], writes=[zc])
            for m in range(4):
                S.add("dve", (lambda e, m=m: e.tensor_scalar(out=yb_b[:, m, tsl], in0=z_b[:, m, 0:512],
                                                             scalar1=cvec[:, C_CW + m:C_CW + m + 1], scalar2=None,
                                                             op0=ALU.mult)),
                      reads=[z_b[:, m, 0:512], cvec], writes=[yb_b[:, m, tsl]])
                for jj in (1, 2):
                    S.add("dve", (lambda e, m=m, jj=jj: e.scalar_tensor_tensor(
                        out=yb_b[:, m, tsl], in0=z_b[:, m, jj:jj + 512],
                        scalar=cvec[:, C_CW + jj * 4 + m:C_CW + jj * 4 + m + 1], in1=yb_b[:, m, tsl],
                        op0=ALU.mult, op1=ALU.add)),
                        reads=[z_b[:, m, jj:jj + 512], cvec, yb_b[:, m, tsl]], writes=[yb_b[:, m, tsl]])
        val_part(0)

        g = ring_acquire()
        for s in range(NSUB):
            for j in range(4):
                b = bank()
                S.add("pe", mm_w(b, (lambda k, s=s, j=j: xn[:, k, s * T + j * 128:s * T + (j + 1) * 128]),
                                 (lambda k, g=g: g[:, k, :]), 8),
                      reads=[g, xn[:, :, s * T + j * 128:s * T + (j + 1) * 128]], writes=[b])
                S.add("act", (lambda e, j=j, b=b: e.activation(out=vg_b[:, j, :], in_=b, func=AF.Gelu_apprx_tanh)),
                      reads=[b], writes=[vg_b[:, j, :]])
                S.add("dve", (lambda e, j=j, s=s: e.bn_stats(out=st6[:, s, j, :], in_=vg_b[:, j, :])),
                      reads=[vg_b[:, j, :]], writes=[st6[:, s, j, :]])
                S.add("dve", (lambda e, j=j, s=s: e.bn_aggr(out=mv[:, s, j, :], in_=st6[:, s, j, :])),
                      reads=[st6[:, s, j, :]], writes=[mv[:, s, j, :]])
            S.add("act", (lambda e, s=s: e.activation(out=sdv[:, s, :], in_=mv[:, s, :, 1], func=AF.Ln, bias=EPS,
                                                      scale=1.0)),
                  reads=[mv[:, s]], writes=[sdv[:, s, :]])
            S.add("act", (lambda e, s=s: e.activation(out=sdv[:, s, :], in_=sdv[:, s, :], func=AF.Exp, scale=-0.5)),
                  reads=[sdv[:, s, :]], writes=[sdv[:, s, :]])
            S.add("dve", (lambda e, s=s: e.scalar_tensor_tensor(out=nmr[:, s, :], in0=mv[:, s, :, 0], scalar=-1.0,
                                                                in1=sdv[:, s, :], op0=ALU.mult, op1=ALU.mult)),
                  reads=[mv[:, s], sdv[:, s, :]], writes=[nmr[:, s, :]])
            for j in range(4):
                S.add("act", (lambda e, s=s, j=j: e.activation(out=vn_b[:, s * 4 + j, :], in_=vg_b[:, j, :],
                                                               func=AF.Identity, bias=nmr[:, s, j:j + 1],
                                                               scale=sdv[:, s, j:j + 1])),
                      reads=[vg_b[:, j, :], nmr[:, s, :], sdv[:, s, :]], writes=[vn_b[:, s * 4 + j, :]])
        val_part(1)
        ring_unstick()

        g_u = ring_acquire()

        def u_part(s):
            tsl = tsl_of(s)

            def ev(mm, b):
                S.add("act", (lambda e: e.activation(out=u_b[:, mm, tsl], in_=b, func=AF.Gelu_apprx_tanh)),
                      reads=[b], writes=[u_b[:, mm, tsl]])
            wgroup(g_u, s, xn, ev)

        def spatial(s):
            tsl = tsl_of(s)
            for h in range(4):
                b = bank()

                def f(e, h=h, b=b):
                    for j in range(4):
                        ins = e.matmul(b[:, j * 128:(j + 1) * 128], lhsT=vn_b[:, s * 4 + j, h * 128:(h + 1) * 128],
                                       rhs=wTm[:, h, :], start=True, stop=True)
                    return ins
                S.add("pe", f, reads=[vn_b[:, s * 4:(s + 1) * 4, h * 128:(h + 1) * 128], wTm[:, h, :]], writes=[b])
                tmp = vg_b[:, h, :]
                S.add("dve", (lambda e, h=h, b=b, tmp=tmp: e.scalar_tensor_tensor(
                    out=tmp.rearrange("p (j t) -> p j t", j=4),
                    in0=b.rearrange("p (j t) -> p j t", j=4),
                    scalar=cvec[:, C_GLN + h:C_GLN + h + 1],
                    in1=E[:, h, :].unsqueeze(1).to_broadcast([128, 4, 128]),
                    op0=ALU.mult, op1=ALU.add)),
                    reads=[b, cvec, E[:, h, :]], writes=[tmp])
                S.add("dve", (lambda e, h=h, tmp=tmp: e.tensor_tensor(out=u_b[:, h, tsl], in0=tmp,
                                                                       in1=u_b[:, h, tsl], op=ALU.mult)),
                      reads=[tmp, u_b[:, h, tsl]], writes=[u_b[:, h, tsl]])

        def gate_b(g_gb, s):
            tsl = tsl_of(s)

            def ev(mm, b):
                S.add("dve", (lambda e: e.tensor_tensor(out=yb_b[:, mm, tsl], in0=b, in1=yb_b[:, mm, tsl],
                                                        op=ALU.mult)),
                      reads=[b, yb_b[:, mm, tsl]], writes=[yb_b[:, mm, tsl]])
            wgroup(g_gb, s, xn, ev)

        def nA_a(s):
            normA(u_b[:, :, tsl_of(s)], 4, T, sq[:, s, 0:4])

        def nA_b(s):
            normA(yb_b[:, :, tsl_of(s)], 4, T, sq[:, s, 4:8])

        def nB_a(s):
            normB(u_b[:, :, tsl_of(s)], yn_b[:, 0:4, tsl_of(s)], C_GA, 4, 512, T, sq[:, s, 0:4])

        def nB_b(s):
            normB(yb_b[:, :, tsl_of(s)], yn_b[:, 4:8, tsl_of(s)], C_GB, 4, 512, T, sq[:, s, 4:8])

        def after_wout(s):
            normA(xT[:, :, tsl_of(s)], 8, T, sq[:, s])

        def resid_ev(gi, tsl):
            def ev(mm, b):
                m = gi * 4 + mm
                S.add("dve", (lambda e: e.tensor_tensor(out=xT[:, m, tsl], in0=b, in1=xT[:, m, tsl], op=ALU.add)),
                      reads=[b, xT[:, m, tsl]], writes=[xT[:, m, tsl]])
            return ev

        u_part(0)
        spatial(0)
        u_part(1)
        nA_a(0)
        spatial(1)
        g_gb = ring_acquire()
        gate_b(g_gb, 0)
        nA_a(1)
        nA_b(0)
        nB_a(0)
        nB_a(1)
        gate_b(g_gb, 1)
        nB_b(0)
        nA_b(1)
        g0 = ring_acquire(stick=True)
        g1 = ring_acquire()
        wgroup_k((lambda k, mm, g=g0: g[:, k, mm * 128:(mm + 1) * 128]), 0, yn_b, resid_ev(0, tsl_of(0)))
        wgroup(g1, 0, yn_b, resid_ev(1, tsl_of(0)))
        after_wout(0)
        nB_b(1)
        wgroup(g0, 1, yn_b, resid_ev(0, tsl_of(1)))
        wgroup(g1, 1, yn_b, resid_ev(1, tsl_of(1)))
        after_wout(1)
        ring_unstick()

        for s in range(NSUB):
            tsl = tsl_of(s)
            normB(xT[:, :, tsl], xn[:, :, tsl], C_ATT, 8, D, T, sq[:, s])
        for gi in range(2):
            g = ring_acquire()
            for s in range(NSUB):
                tsl = tsl_of(s)

                def ev(mm, b, gi=gi, tsl=tsl):
                    m = gi * 4 + mm
                    S.add("act", (lambda e: e.activation(out=q_b[:, m, tsl], in_=b, func=AF.Copy, scale=0.0625)),
                          reads=[b], writes=[q_b[:, m, tsl]])
                if gi == 0:
                    wgroup_k((lambda k, mm, g=g: g[:, k, mm * 128:(mm + 1) * 128]), s, xn, ev)
                else:
                    wgroup(g, s, xn, ev)
        for s in range(NSUB):
            tsl = tsl_of(s)
            for h in range(4):
                for mc in range(2):
                    b = bank()

                    def f(e, h=h, mc=mc, b=b, tsl=tsl):
                        for half in range(2):
                            ins = e.matmul(b, lhsT=KT[:, h * 2 + half, mc * 128:(mc + 1) * 128],
                                           rhs=q_b[:, h * 2 + half, tsl], start=(half == 0), stop=(half == 1))
                        return ins
                    S.add("pe", f, reads=[KT[:, h * 2:h * 2 + 2, :], q_b[:, h * 2:h * 2 + 2, tsl]], writes=[b])
                    S.add("act", (lambda e, h=h, mc=mc, b=b, s=s: e.activation(out=p_b[:, s, h * 2 + mc, :], in_=b,
                                                                               func=AF.Exp)),
                          reads=[b], writes=[p_b[:, s, h * 2 + mc, :]])
        for s in range(NSUB):
            tsl = tsl_of(s)
            for h in range(4):
                b = bank()

                def f(e, h=h, b=b, s=s):
                    for mc in range(2):
                        ins = e.matmul(b, lhsT=ones, rhs=p_b[:, s, h * 2 + mc, :], start=(mc == 0), stop=(mc == 1))
                    return ins
                S.add("pe", f, reads=[ones, p_b[:, s, h * 2:h * 2 + 2, :]], writes=[b])
                S.add("act", (lambda e, h=h, b=b, s=s: e.activation(out=rden_b[:, s, h, :], in_=b, func=AF.Ln)),
                      reads=[b], writes=[rden_b[:, s, h, :]])
                S.add("act", (lambda e, h=h, s=s: e.activation(out=rden_b[:, s, h, :], in_=rden_b[:, s, h, :],
                                                               func=AF.Exp, scale=-1.0)),
                      reads=[rden_b[:, s, h, :]], writes=[rden_b[:, s, h, :]])
            for c in range(8):
                h = c // 2
                b = bank()

                def f(e, c=c, h=h, b=b, s=s):
                    for mc in range(2):
                        ins = e.matmul(b, lhsT=Vt[:, mc, c * 128:(c + 1) * 128], rhs=p_b[:, s, h * 2 + mc, :],
                                       start=(mc == 0), stop=(mc == 1))
                    return ins
                S.add("pe", f, reads=[Vt[:, :, c * 128:(c + 1) * 128], p_b[:, s, h * 2:h * 2 + 2, :]], writes=[b])
                S.add("dve", (lambda e, c=c, h=h, b=b, tsl=tsl, s=s: e.tensor_tensor(
                    out=o_b[:, c, tsl], in0=b, in1=rden_b[:, s, h, :], op=ALU.mult)),
                    reads=[b, rden_b[:, s, h, :]], writes=[o_b[:, c, tsl]])
        proj_residual(o_b, after_wout)

        sg_ctr = [0]
        for s in range(NSUB):
            tsl = tsl_of(s)
            normB(xT[:, :, tsl], xn[:, :, tsl], C_FFN, 8, D, T, sq[:, s])

        def swiglu(bg, bu, j, tsl):
            sg = sg_b[:, sg_ctr[0] % 2, :]
            sg_ctr[0] += 1
            S.add("act", (lambda e: e.activation(out=sg, in_=bg, func=AF.Silu)), reads=[bg], writes=[sg])
            S.add("dve", (lambda e: e.tensor_tensor(out=h_b[:, j, tsl], in0=bu, in1=sg, op=ALU.mult)),
                  reads=[bu, sg], writes=[h_b[:, j, tsl]])

        for gi in range(11):
            g = ring_acquire()
            for s in range(NSUB):
                tsl = tsl_of(s)
                if gi == 0:
                    got = {}

                    def ev(mm, b, got=got, tsl=tsl, gi=gi):
                        got[mm] = b
                        if mm % 2 == 1:
                            swiglu(got[mm - 1], b, gi * 2 + mm // 2, tsl)
                    wgroup_k((lambda k, mm, g=g: g[:, mm % 2, k, (mm // 2) * 128:(mm // 2 + 1) * 128]), s, xn, ev)
                    continue
                for i in range(2):
                    bg = bank()
                    S.add("pe", mm_w(bg, (lambda k, g=g, i=i: g[:, 0, k, i * 128:(i + 1) * 128]),
                                     (lambda k, tsl=tsl: xn[:, k, tsl]), 8),
                          reads=[g, xn[:, :, tsl]], writes=[bg])
                    bu = bank()
                    S.add("pe", mm_w(bu, (lambda k, g=g, i=i: g[:, 1, k, i * 128:(i + 1) * 128]),
                                     (lambda k, tsl=tsl: xn[:, k, tsl]), 8),
                          reads=[g, xn[:, :, tsl]], writes=[bu])
                    swiglu(bg, bu, gi * 2 + i, tsl)
        for m in range(8):
            g = ring_acquire()
            for s in range(NSUB):
                tsl = tsl_of(s)
                b = bank()
                S.add("pe", mm_w(b, (lambda k, g=g: g[:, k, :]), (lambda k, tsl=tsl: h_b[:, k, tsl]), KF),
                      reads=[g, h_b[:, :, tsl]], writes=[b])
                S.add("dve", (lambda e, m=m, b=b, tsl=tsl: e.tensor_tensor(
                    out=xT[:, m, tsl], in0=b, in1=xT[:, m, tsl], op=ALU.add)),
                    reads=[b, xT[:, m, tsl]], writes=[xT[:, m, tsl]])

        if st + 1 < NST:
            x_load(st + 1, 0)
            x_load(st + 1, 1)

        for s in range(NSUB):
            normA(xT[:, :, tsl_of(s)], 8, T, sq[:, s])
        for s in range(NSUB):
            tsl = tsl_of(s)
            normB(xT[:, :, tsl], outT_b[:, s], C_FIN, 8, D, T, sq[:, s])
        last_out = []
        for s in range(NSUB):
            t0 = st * TS + s * T
            ydst = yst_b if s == 0 else yst2_b
            for j in range(4):
                for half in range(2):
                    b = bank()

                    def f(e, j=j, half=half, b=b, s=s):
                        for cc in range(4):
                            c = half * 4 + cc
                            ins = e.transpose(b[:, cc * 128:(cc + 1) * 128], outT_b[:, s, c, j * 128:(j + 1) * 128],
                                              ident)
                        return ins
                    S.add("pe", f, reads=[outT_b[:, s, half * 4:half * 4 + 4, j * 128:(j + 1) * 128], ident],
                          writes=[b])
                    dsts = ydst[:, j, half * 512:(half + 1) * 512]
                    if (j + half) % 2 == 0:
                        S.add("act", (lambda e, dsts=dsts, b=b: e.copy(out=dsts, in_=b)), reads=[b], writes=[dsts])
                    else:
                        S.add("dve", (lambda e, dsts=dsts, b=b: e.tensor_copy(out=dsts, in_=b)), reads=[b], writes=[dsts])
                od = S.add("sp", (lambda e, t0=t0, j=j, ydst=ydst: e.dma_start(
                    out=y_d[t0 + j * 128:t0 + (j + 1) * 128, :], in_=ydst[:, j, :])),
                    reads=[ydst[:, j, :]], dma_key=("yout", s, j))
                last_out.append(od)

    S.add("sp", None, extra_deps=last_out)
    assert ring_state["next_acq"] == len(groups), (ring_state, len(groups))

    S.finalize()
    sems = {e: nc.alloc_semaphore("sem_" + e) for e in Sched.CENG}
    dsems = {}
    for key in S.dma_cnt:
        dsems[key] = nc.alloc_semaphore("dsem_" + "_".join(str(k) for k in (key if isinstance(key, tuple) else (key,))))

    with nc.Block() as block:
        @block.tensor
        def _(e):
            S.emit("pe", e, sems, dsems)

        @block.scalar
        def _(e):
            S.emit("act", e, sems, dsems)

        @block.vector
        def _(e):
            S.emit("dve", e, sems, dsems)

        @block.gpsimd
        def _(e):
            S.emit("pool", e, sems, dsems)

        @block.sync
        def _(e):
            S.emit("sp", e, sems, dsems)
    return nc


def _host_layout(inputs):
    f = lambda a: np.ascontiguousarray(np.asarray(a, dtype=np.float32))
    col = lambda v, n: f(v).reshape(n, 128).T
    cw = f(inputs["conv_w"])
    cvec = np.concatenate([
        col(inputs["ln_mix_g"], 8), col(inputs["ln_attn_g"], 8), col(inputs["ln_ffn_g"], 8),
        col(inputs["ln_final_g"], 8), col(inputs["ln_mem_g"], 8),
        col(inputs["grp_norm_a"], 4), col(inputs["grp_norm_b"], 4),
        col(inputs["sgu_ln_g"], 4), col(inputs["sgu_ln_b"], 4),
        col(cw[0], 4), col(cw[1], 4), col(cw[2], 4),
    ], axis=1)
    assert cvec.shape == (128, NCV)
    wspT = f(np.transpose(f(inputs["w_spatial"]), (2, 0, 1)))
    bspb = f(np.broadcast_to(f(inputs["b_spatial"])[None, :, :], (128, 4, 128)))
    shared = {
        "w_in": f(inputs["w_in"]), "w_out": f(inputs["w_out"]), "w_q": f(inputs["w_q"]),
        "w_kv": f(inputs["w_kv"]), "w_o": f(inputs["w_o"]), "w_gate_up": f(inputs["w_gate_up"]),
        "w_down": f(inputs["w_down"]), "cvec": f(cvec), "wspT": wspT, "bspb": bspb,
    }
    x = f(inputs["x"])
    mem = f(inputs["mem"])
    in_maps = []
    for b in range(8):
        d = dict(shared)
        d["x"] = x[b]
        d["mem"] = mem[b]
        in_maps.append(d)
    return in_maps


def kernel(**inputs):
    in_maps = _host_layout(inputs)
    nc = build_program()
    res = run_bass_kernel_spmd(nc, in_maps, core_ids=list(range(8)))
    out = np.stack([np.asarray(r["y"], dtype=np.float32) for r in res.results], axis=0)
    return out
```

```python
import numpy as np
import concourse.bass as bass
import concourse.mybir as mybir
from concourse.bass_utils import run_bass_kernel_spmd

F32 = mybir.dt.float32
BF16 = mybir.dt.bfloat16
U8 = mybir.dt.uint8
AF = mybir.ActivationFunctionType
ALU = mybir.AluOpType

D = 1024
KD = 8
SEQ = 4096
TS = 1024
T = 512
NSUB = TS // T
NST = SEQ // TS
MEM = 256
DFF = 2816
KF = DFF // 128
EPS = 1e-6
NSLOT = 5
SLOT_B = 8192

C_MIX, C_ATT, C_FFN, C_FIN, C_MEM = 0, 8, 16, 24, 32
C_GA, C_GB, C_GLN, C_BLN, C_CW = 40, 44, 48, 52, 56
NCV = 68

_ES = {F32: 4, BF16: 2, U8: 1}


def _esize(dt):
    return _ES[dt]


def _hull(ap):
    es = _esize(ap.dtype)
    pairs = [tuple(p) for p in ap.ap]
    pstride = pairs[0][0]
    off = ap.offset % pstride if pstride else ap.offset
    ext = 1
    for st, cnt in pairs[1:]:
        ext += (cnt - 1) * abs(st)
    return ap.tensor.name, off * es, (off + ext) * es


class Op:
    __slots__ = ("eng", "fn", "idx", "waits", "signal", "sigval", "dma_key", "dma_cnt", "know", "gid")


class Sched:
    CENG = ("pe", "act", "dve", "pool")
    GRAN = 128

    def __init__(self):
        self.ops = []
        self.eng_ops = {e: [] for e in ("pe", "act", "dve", "pool", "sp")}
        self.nidx = {e: 0 for e in self.CENG}
        self.blocks = {}
        self.know = {e: {} for e in ("pe", "act", "dve", "pool", "sp")}
        self.dma_cnt = {}

    def _blocks(self, ap):
        name, a, b = _hull(ap)
        g = 2048 if name.startswith("ps") else self.GRAN
        return [(name, i) for i in range(a // g, (b - 1) // g + 1)]

    def add(self, eng, fn, reads=(), writes=(), dma_key=None, extra_deps=()):
        op = Op()
        op.eng = eng
        op.fn = fn
        op.dma_key = dma_key
        op.signal = False
        op.sigval = None
        op.waits = []
        op.gid = len(self.ops)
        is_dma = dma_key is not None
        if is_dma:
            self.dma_cnt[dma_key] = self.dma_cnt.get(dma_key, 0) + 16
            op.dma_cnt = self.dma_cnt[dma_key]
            op.idx = None
        else:
            op.dma_cnt = None
            if eng in self.CENG and fn is not None:
                op.idx = self.nidx[eng]
                self.nidx[eng] += 1
            else:
                op.idx = None
        deps = {}
        for d in extra_deps:
            deps[d] = True
        for ap in reads:
            for blk in self._blocks(ap):
                ent = self.blocks.get(blk)
                if ent is None:
                    ent = [None, {}, []]
                    self.blocks[blk] = ent
                if ent[0] is not None:
                    deps[ent[0]] = True
        for ap in writes:
            for blk in self._blocks(ap):
                ent = self.blocks.get(blk)
                if ent is None:
                    ent = [None, {}, []]
                    self.blocks[blk] = ent
                if ent[0] is not None and ent[0] not in deps:
                    deps[ent[0]] = False
                for r in ent[1].values():
                    if r not in deps:
                        deps[r] = False
                for r in ent[2]:
                    if r not in deps:
                        deps[r] = False
        deps.pop(op, None)
        kn = self.know[eng]
        for a in sorted(deps, key=lambda o: -o.gid):
            raw = deps[a]
            if a.dma_key is None:
                if a.idx is None:
                    continue
                if a.eng == eng and not is_dma:
                    if eng == "pe":
                        continue
                key, val = a.eng, a.idx + 1
            else:
                key, val = ("d", a.dma_key), a.dma_cnt
            if kn.get(key, 0) >= val:
                continue
            op.waits.append(a)
            a.signal = True
            for k, v in a.know.items():
                if kn.get(k, 0) < v:
                    kn[k] = v
        op.know = dict(kn)
        if is_dma:
            op.know[("d", dma_key)] = op.dma_cnt
        elif op.idx is not None:
            op.know[eng] = op.idx + 1
        for ap in reads:
            for blk in self._blocks(ap):
                ent = self.blocks[blk]
                if is_dma or op.idx is None:
                    ent[2].append(op)
                else:
                    ent[1][eng] = op
        for ap in writes:
            for blk in self._blocks(ap):
                ent = self.blocks[blk]
                ent[0] = op
                ent[1] = {}
                ent[2] = []
        self.ops.append(op)
        self.eng_ops[eng].append(op)
        return op

    def finalize(self):
        for e in self.CENG:
            n = 0
            for op in self.eng_ops[e]:
                if op.dma_key is None and op.idx is not None and op.signal:
                    n += 1
                    op.sigval = n

    def emit(self, eng, handle, sems, dsems):
        for op in self.eng_ops[eng]:
            for a in op.waits:
                if a.dma_key is None:
                    handle.wait_ge(sems[a.eng], a.sigval)
                else:
                    handle.wait_ge(dsems[a.dma_key], a.dma_cnt)
            if op.fn is None:
                continue
            ins = op.fn(handle)
            if op.dma_key is not None:
                ins.then_inc(dsems[op.dma_key], 16)
            elif op.signal:
                ins.then_inc(sems[op.eng], 1)


def build_program():
    nc = bass.Bass("TRN2", target_bir_lowering=False)
    x_d = nc.dram_tensor("x", [SEQ, D], F32, kind="ExternalInput").ap()
    mem_d = nc.dram_tensor("mem", [MEM, D], F32, kind="ExternalInput").ap()
    w_in_d = nc.dram_tensor("w_in", [D, 2560], F32, kind="ExternalInput").ap()
    w_out_d = nc.dram_tensor("w_out", [D, D], F32, kind="ExternalInput").ap()
    w_q_d = nc.dram_tensor("w_q", [D, D], F32, kind="ExternalInput").ap()
    w_kv_d = nc.dram_tensor("w_kv", [D, 2 * D], F32, kind="ExternalInput").ap()
    w_o_d = nc.dram_tensor("w_o", [D, D], F32, kind="ExternalInput").ap()
    w_gu_d = nc.dram_tensor("w_gate_up", [D, 2 * DFF], F32, kind="ExternalInput").ap()
    w_dn_d = nc.dram_tensor("w_down", [DFF, D], F32, kind="ExternalInput").ap()
    cvec_d = nc.dram_tensor("cvec", [128, NCV], F32, kind="ExternalInput").ap()
    wspT_d = nc.dram_tensor("wspT", [128, 4, 128], F32, kind="ExternalInput").ap()
    bspb_d = nc.dram_tensor("bspb", [128, 4, 128], F32, kind="ExternalInput").ap()
    y_d = nc.dram_tensor("y", [SEQ, D], F32, kind="ExternalOutput").ap()

    top = [0]

    def alloc(nbytes, align=128):
        off = (top[0] + align - 1) // align * align
        top[0] = off + nbytes
        return off

    NSTD = 4
    SCR = 83968
    o_ident = alloc(512)
    o_ones = alloc(256)
    o_cvec = alloc(NCV * 4)
    o_wTm = alloc(1024)
    o_E = alloc(2048)
    o_KT = alloc(4096)
    o_Vt = alloc(4096)
    o_zc = alloc(32)
    o_small = alloc(1024)
    o_sq = alloc(16384)
    o_std = alloc(NSTD * 2048)
    o_xT = alloc(32768)
    o_xn = alloc(16384)
    o_ring = alloc(NSLOT * SLOT_B)
    o_scr = alloc(SCR)
    total = top[0]
    big = nc.alloc_sbuf_tensor("big", [128, total], U8)

    def view(off, dt, *shape):
        n = 1
        for s_ in shape:
            n *= s_
        ap = big[:, off:off + n * _esize(dt)].bitcast(dt)
        if len(shape) == 2:
            ap = ap.rearrange("p (a b) -> p a b", a=shape[0])
        elif len(shape) == 3:
            ap = ap.rearrange("p (a b c) -> p a b c", a=shape[0], b=shape[1])
        return ap

    ident = view(o_ident, F32, 128)
    ones = view(o_ones, BF16, 128)
    cvec = view(o_cvec, F32, NCV)
    wTm = view(o_wTm, BF16, 4, 128)
    E = view(o_E, F32, 4, 128)
    KT = view(o_KT, BF16, 8, 256)
    Vt = view(o_Vt, BF16, 2, 1024)
    zc = view(o_zc, F32, 4, 2)
    st6 = view(o_small, F32, 2, 4, 6)
    mv = view(o_small + 256, F32, 2, 4, 2)
    sdv = view(o_small + 384, F32, 2, 4)
    nmr = view(o_small + 448, F32, 2, 4)
    sq = view(o_sq, BF16, 2, 8, 512)
    stdb = view(o_std, F32, NSTD, 512)
    xT = view(o_xT, F32, 8, TS)
    xn = view(o_xn, BF16, 8, TS)
    u_b = view(o_scr + 0, F32, 4, TS)
    gc_b = view(o_scr + 16384, BF16, 4, TS)
    vn_b = view(o_scr + 24576, BF16, 8, 512)
    yn_b = view(o_scr + 32768, BF16, 8, TS)
    vg_b = view(o_scr + 49152, F32, 4, 512)
    z_b = view(o_scr + 57344, F32, 4, 514)
    yb_b = view(o_scr + 65664, F32, 4, TS)
    q_b = view(o_scr + 0, BF16, 8, TS)
    o_b = view(o_scr + 16384, BF16, 8, TS)
    p_b = view(o_scr + 32768, BF16, 2, 8, 512)
    rden_b = view(o_scr + 49152, F32, 2, 4, 512)
    h_b = view(o_scr + 0, BF16, KF, TS)
    sg_b = view(o_scr + 45056, F32, 2, 512)
    xin_b = view(o_scr + 49152, F32, 2, 4, D)
    outT_b = view(o_scr + 0, F32, 2, 8, 512)
    yst_b = view(o_scr + 32768, F32, 4, D)
    yst2_b = view(o_scr + 0, F32, 4, D)
    memin = view(o_scr + 0, F32, 2, D)
    memT = view(o_scr + 32768, F32, 8, MEM)
    memn = view(o_scr + 40960, BF16, 8, MEM)
    wTf = view(o_scr + 45056, F32, 4, 128)
    bspb = view(o_scr + 47104, F32, 4, 128)

    ps = nc.alloc_psum_tensor("ps", [128, 4096], F32)
    bank_ctr = [0]

    def bank():
        b = bank_ctr[0] % 8
        bank_ctr[0] += 1
        return ps[:, b * 512:(b + 1) * 512]

    S = Sched()

    groups = []

    def slot_view(si, dt, *shape):
        return view(o_ring + si * SLOT_B, dt, *shape)

    def a_group(w_d, c0):
        return ("A", [(lambda si: slot_view(si, BF16, 8, 512),
                       w_d[:, c0:c0 + 512].rearrange("(k p) n -> p k n", p=128))])

    def gu_group(i):
        def dv(half):
            return lambda si: view(o_ring + si * SLOT_B + half * 4096, BF16, 8, 256)
        return ("GU", [(dv(0), w_gu_d[:, 256 * i:256 * i + 256].rearrange("(k p) n -> p k n", p=128)),
                       (dv(1), w_gu_d[:, DFF + 256 * i:DFF + 256 * i + 256].rearrange("(k p) n -> p k n", p=128))])

    def dn_group(m):
        def dv(half):
            return lambda si: view(o_ring + si * SLOT_B + half * 11 * 256, BF16, 11, 128)
        return ("DN", [(dv(0), w_dn_d[0:11 * 128, m * 128:(m + 1) * 128].rearrange("(k p) n -> p k n", p=128)),
                       (dv(1), w_dn_d[11 * 128:22 * 128, m * 128:(m + 1) * 128].rearrange("(k p) n -> p k n", p=128))])

    WIN_ORDER = (3, 4, 1, 0, 2)
    for i in range(4):
        groups.append(a_group(w_kv_d, 512 * i))
    for st in range(NST):
        for i in WIN_ORDER:
            groups.append(a_group(w_in_d, 512 * i))
        for i in range(2):
            groups.append(a_group(w_out_d, 512 * i))
        for i in range(2):
            groups.append(a_group(w_q_d, 512 * i))
        for i in range(2):
            groups.append(a_group(w_o_d, 512 * i))
        for i in range(11):
            groups.append(gu_group(i))
        for m in range(8):
            groups.append(dn_group(m))

    ring_state = {"next_dma": 0, "next_acq": 0}

    def ring_issue(gi):
        kind, parts = groups[gi]
        si = gi % NSLOT
        for dvf, src in parts:
            dst = dvf(si)
            S.add("pool", (lambda e, dst=dst, src=src: e.dma_start(out=dst, in_=src)),
                  writes=[dst], dma_key=("ring", si))

    held = []
    sticky = set()

    def ring_pump():
        base = held[0] if held else ring_state["next_acq"]
        want = min(base + NSLOT - 1, len(groups) - 1)
        while ring_state["next_dma"] <= want:
            ring_issue(ring_state["next_dma"])
            ring_state["next_dma"] += 1

    def ring_acquire(stick=False):
        gi = ring_state["next_acq"]
        ring_state["next_acq"] += 1
        held[:] = [h for h in held if h in sticky]
        held.append(gi)
        if stick:
            sticky.add(gi)
        ring_pump()
        kind, _ = groups[gi]
        si = gi % NSLOT
        if kind == "A":
            return slot_view(si, BF16, 8, 512)
        if kind == "GU":
            return slot_view(si, BF16, 2, 8, 256)
        return slot_view(si, BF16, KF, 128)

    def ring_unstick():
        sticky.clear()

    std_ctr = [0]

    def normA(src, nch, Tn, sqv):
        for c in range(nch):
            S.add("act", (lambda e, c=c: e.activation(out=sqv[:, c, 0:Tn], in_=src[:, c, :], func=AF.Square)),
                  reads=[src[:, c, :]], writes=[sqv[:, c, 0:Tn]])

    def normB(src, dst, gbase, nch, N, Tn, sqv):
        b = bank()

        def f(e):
            for c in range(nch):
                ins = e.matmul(b[:, 0:Tn], lhsT=ones, rhs=sqv[:, c, 0:Tn], start=(c == 0), stop=(c == nch - 1))
            return ins
        S.add("pe", f, reads=[ones, sqv[:, 0:nch, 0:Tn]], writes=[b])
        sd = stdb[:, std_ctr[0] % NSTD, 0:Tn]
        std_ctr[0] += 1
        S.add("act", (lambda e: e.activation(out=sd, in_=b[:, 0:Tn], func=AF.Ln, bias=EPS, scale=1.0 / N)),
              reads=[b], writes=[sd])
        S.add("act", (lambda e: e.activation(out=sd, in_=sd, func=AF.Exp, scale=-0.5)), reads=[sd], writes=[sd])
        for c in range(nch):
            S.add("dve", (lambda e, c=c: e.scalar_tensor_tensor(
                out=dst[:, c, :], in0=src[:, c, :], scalar=cvec[:, gbase + c:gbase + c + 1], in1=sd,
                op0=ALU.mult, op1=ALU.mult)),
                reads=[src[:, c, :], cvec, sd], writes=[dst[:, c, :]])

    def mm_w(b, wfn, rhsfn, nk, ncols=T):
        def f(e):
            for k in range(nk):
                ins = e.matmul(b[:, 0:ncols], lhsT=wfn(k), rhs=rhsfn(k), start=(k == 0), stop=(k == nk - 1))
            return ins
        return f

    def tsl_of(s):
        return slice(s * T, (s + 1) * T)

    S.add("sp", lambda e: e.dma_start(out=cvec, in_=cvec_d), writes=[cvec], dma_key="cvec")
    S.add("sp", lambda e: e.dma_start(out=wTf, in_=wspT_d), writes=[wTf], dma_key="wTf")
    S.add("sp", lambda e: e.dma_start(out=bspb, in_=bspb_d), writes=[bspb], dma_key="bspb")
    S.add("sp", lambda e: e.dma_start(out=memin, in_=mem_d.rearrange("(j p) d -> p j d", p=128)),
          writes=[memin], dma_key="memin")

    def x_load(st, s):
        t0 = st * TS + s * T
        S.add("sp", (lambda e: e.dma_start(
            out=xin_b[:, s, :, :], in_=x_d[t0:t0 + T, :].rearrange("(j p) d -> p j d", p=128))),
            writes=[xin_b[:, s, :, :]], dma_key=("xin", s))

    x_load(0, 0)
    x_load(0, 1)
    S.add("pool", lambda e: e.memset(ident, 0.0), writes=[ident])
    S.add("pool", lambda e: e.affine_select(out=ident, in_=ident, pattern=[[-1, 128]], compare_op=ALU.not_equal,
                                            fill=1.0, base=0, channel_multiplier=1),
          reads=[ident], writes=[ident])
    S.add("pool", lambda e: e.memset(ones, 1.0), writes=[ones])
    S.add("pool", lambda e: e.memset(zc, 0.0), writes=[zc])
    for h in range(4):
        S.add("pool", (lambda e, h=h: e.affine_select(out=wTf[:, h, :], in_=wTf[:, h, :], pattern=[[1, 128]],
                                                      compare_op=ALU.is_ge, fill=0.0, base=0, channel_multiplier=-1)),
              reads=[wTf[:, h, :]], writes=[wTf[:, h, :]])
    S.add("dve", lambda e: e.tensor_copy(out=wTm, in_=wTf), reads=[wTf], writes=[wTm])
    b = bank()

    def f_rw(e, b=b):
        for h in range(4):
            ins = e.matmul(b[:, h * 128:(h + 1) * 128], lhsT=ones, rhs=wTm[:, h, :], start=True, stop=True)
        return ins
    S.add("pe", f_rw, reads=[ones, wTm], writes=[b])
    for h in range(4):
        S.add("dve", (lambda e, h=h, b=b: e.scalar_tensor_tensor(
            out=E[:, h, :], in0=b[:, h * 128:(h + 1) * 128], scalar=cvec[:, C_BLN + h:C_BLN + h + 1],
            in1=bspb[:, h, :], op0=ALU.mult, op1=ALU.add)),
            reads=[b, cvec, bspb[:, h, :]], writes=[E[:, h, :]])

    for c in range(8):
        b = bank()

        def f(e, c=c, b=b):
            for j in range(2):
                ins = e.transpose(b[:, j * 128:(j + 1) * 128], memin[:, j, c * 128:(c + 1) * 128], ident)
            return ins
        S.add("pe", f, reads=[memin[:, :, c * 128:(c + 1) * 128], ident], writes=[b])
        if c % 2 == 0:
            S.add("act", (lambda e, c=c, b=b: e.copy(out=memT[:, c, :], in_=b[:, 0:MEM])),
                  reads=[b], writes=[memT[:, c, :]])
        else:
            S.add("dve", (lambda e, c=c, b=b: e.tensor_copy(out=memT[:, c, :], in_=b[:, 0:MEM])),
                  reads=[b], writes=[memT[:, c, :]])
    sq_mem = view(o_std + 4096, BF16, 8, MEM)
    normA(memT, 8, MEM, sq_mem)

    def xpose(s):
        tsl = tsl_of(s)
        for c in range(8):
            b = bank()

            def f(e, c=c, b=b):
                for j in range(4):
                    ins = e.transpose(b[:, j * 128:(j + 1) * 128], xin_b[:, s, j, c * 128:(c + 1) * 128], ident)
                return ins
            S.add("pe", f, reads=[xin_b[:, s, :, c * 128:(c + 1) * 128], ident], writes=[b])
            if c % 2 == 0:
                S.add("act", (lambda e, c=c, b=b: e.copy(out=xT[:, c, tsl], in_=b)),
                      reads=[b], writes=[xT[:, c, tsl]])
            else:
                S.add("dve", (lambda e, c=c, b=b: e.tensor_copy(out=xT[:, c, tsl], in_=b)),
                      reads=[b], writes=[xT[:, c, tsl]])
        normA(xT[:, :, tsl], 8, T, sq[:, s])

    def norm1B():
        for s in range(NSUB):
            tsl = tsl_of(s)
            normB(xT[:, :, tsl], xn[:, :, tsl], C_MIX, 8, D, T, sq[:, s])

    xpose(0)
    xpose(1)
    normB(memT, memn, C_MEM, 8, D, MEM, sq_mem)
    norm1B()
    for gi in range(2):
        g = ring_acquire()
        for mm in range(4):
            c = gi * 4 + mm
            b = bank()
            S.add("pe", mm_w(b, (lambda k, g=g, mm=mm: g[:, k, mm * 128:(mm + 1) * 128]),
                             (lambda k: memn[:, k, :]), 8, ncols=MEM),
                  reads=[g, memn], writes=[b])
            S.add("act", (lambda e, c=c, b=b: e.copy(out=KT[:, c, :], in_=b[:, 0:MEM])),
                  reads=[b], writes=[KT[:, c, :]])
    for gi in range(2):
        g = ring_acquire()
        for mc in range(2):
            b = bank()
            S.add("pe", mm_w(b, (lambda k, mc=mc: memn[:, k, mc * 128:(mc + 1) * 128]),
                             (lambda k, g=g: g[:, k, :]), 8, ncols=512),
                  reads=[g, memn], writes=[b])
            S.add("act", (lambda e, mc=mc, gi=gi, b=b: e.copy(out=Vt[:, mc, gi * 512:(gi + 1) * 512], in_=b)),
                  reads=[b], writes=[Vt[:, mc, gi * 512:(gi + 1) * 512]])

    def wgroup(g, s, src, evac, nmm=4):
        tsl = tsl_of(s)
        for mm in range(nmm):
            b = bank()
            S.add("pe", mm_w(b, (lambda k, g=g, mm=mm: g[:, k, mm * 128:(mm + 1) * 128]),
                             (lambda k, tsl=tsl: src[:, k, tsl]), 8),
                  reads=[g, src[:, :, tsl]], writes=[b])
            evac(mm, b)

    def wgroup_k(wfn, s, src, evac, nmm=4):
        tsl = tsl_of(s)
        banks = [bank() for _ in range(nmm)]
        for k in range(8):
            def f(e, k=k):
                for mm in range(nmm):
                    ins = e.matmul(banks[mm], lhsT=wfn(k, mm), rhs=src[:, k, tsl], start=(k == 0), stop=(k == 7))
                return ins
            S.add("pe", f, reads=[wfn(k, mm) for mm in range(nmm)] + [src[:, k, tsl]], writes=banks)
        for mm in range(nmm):
            evac(mm, banks[mm])

    def proj_residual(src_b, after_s):
        g0 = ring_acquire(stick=True)
        g1 = ring_acquire()
        for s in range(NSUB):
            tsl = tsl_of(s)
            for gi, g in enumerate((g0, g1)):
                def ev(mm, b, gi=gi, tsl=tsl):
                    m = gi * 4 + mm
                    S.add("dve", (lambda e: e.tensor_tensor(out=xT[:, m, tsl], in0=b, in1=xT[:, m, tsl], op=ALU.add)),
                          reads=[b, xT[:, m, tsl]], writes=[xT[:, m, tsl]])
                wgroup(g, s, src_b, ev)
            after_s(s)
        ring_unstick()

    out_dmas = []
    for st in range(NST):

        g = ring_acquire()
        if st > 0:
            norm1B()
        for s in range(NSUB):
            tsl = tsl_of(s)

            def ev(mm, b, tsl=tsl):
                S.add("act", (lambda e: e.copy(out=gc_b[:, mm, tsl], in_=b)), reads=[b], writes=[gc_b[:, mm, tsl]])
            wgroup_k((lambda k, mm, g=g: g[:, k, mm * 128:(mm + 1) * 128]), s, xn, ev)

        g_val = ring_acquire(stick=True)

        def val_part(s):
            tsl = tsl_of(s)
            S.add("dve", (lambda e: e.tensor_copy(out=z_b[:, :, 0:2], in_=zc)), reads=[zc], writes=[z_b[:, :, 0:2]])

            def ev(mm, b):
                S.add("dve", (lambda e: e.tensor_tensor(out=z_b[:, mm, 2:514], in0=b, in1=gc_b[:, mm, tsl],
                                                        op=ALU.mult)),
                      reads=[b, gc_b[:, mm, tsl]], writes=[z_b[:, mm, 2:514]])
            wgroup(g_val, s, xn, ev)
            S.add("dve", (lambda e: e.tensor_copy(out=zc, in_=z_b[:, :, 512:514])),
                  reads=[z_b[:, :, 512:514]], writes=[zc])

        def val_conv(s):
            tsl = tsl_of(s)
            for m in range(4):
                S.add("dve", (lambda e, m=m: e.tensor_scalar(out=yb_b[:, m, tsl], in0=z_b[:, m, 0:512],
                                                             scalar1=cvec[:, C_CW + m:C_CW + m + 1], scalar2=None,
                                                             op0=ALU.mult)),
                      reads=[z_b[:, m, 0:512], cvec], writes=[yb_b[:, m, tsl]])
                for jj in (1, 2):
                    S.add("dve", (lambda e, m=m, jj=jj: e.scalar_tensor_tensor(
                        out=yb_b[:, m, tsl], in0=z_b[:, m, jj:jj + 512],
                        scalar=cvec[:, C_CW + jj * 4 + m:C_CW + jj * 4 + m + 1], in1=yb_b[:, m, tsl],
                        op0=ALU.mult, op1=ALU.add)),
                        reads=[z_b[:, m, jj:jj + 512], cvec, yb_b[:, m, tsl]], writes=[yb_b[:, m, tsl]])
        val_part(0)

        g = ring_acquire()
        for s in range(NSUB):
            for j in range(4):
                b = bank()
                S.add("pe", mm_w(b, (lambda k, s=s, j=j: xn[:, k, s * T + j * 128:s * T + (j + 1) * 128]),
                                 (lambda k, g=g: g[:, k, :]), 8),
                      reads=[g, xn[:, :, s * T + j * 128:s * T + (j + 1) * 128]], writes=[b])
                S.add("act", (lambda e, j=j, b=b: e.activation(out=vg_b[:, j, :], in_=b, func=AF.Gelu_apprx_tanh)),
                      reads=[b], writes=[vg_b[:, j, :]])
                S.add("dve", (lambda e, j=j, s=s: e.bn_stats(out=st6[:, s, j, :], in_=vg_b[:, j, :])),
                      reads=[vg_b[:, j, :]], writes=[st6[:, s, j, :]])
                S.add("dve", (lambda e, j=j, s=s: e.bn_aggr(out=mv[:, s, j, :], in_=st6[:, s, j, :])),
                      reads=[st6[:, s, j, :]], writes=[mv[:, s, j, :]])
            S.add("act", (lambda e, s=s: e.activation(out=sdv[:, s, :], in_=mv[:, s, :, 1], func=AF.Ln, bias=EPS,
                                                      scale=1.0)),
                  reads=[mv[:, s]], writes=[sdv[:, s, :]])
            S.add("act", (lambda e, s=s: e.activation(out=sdv[:, s, :], in_=sdv[:, s, :], func=AF.Exp, scale=-0.5)),
                  reads=[sdv[:, s, :]], writes=[sdv[:, s, :]])
            S.add("dve", (lambda e, s=s: e.scalar_tensor_tensor(out=nmr[:, s, :], in0=mv[:, s, :, 0], scalar=-1.0,
                                                                in1=sdv[:, s, :], op0=ALU.mult, op1=ALU.mult)),
                  reads=[mv[:, s], sdv[:, s, :]], writes=[nmr[:, s, :]])
            for j in range(4):
                S.add("act", (lambda e, s=s, j=j: e.activation(out=vn_b[:, s * 4 + j, :], in_=vg_b[:, j, :],
                                                               func=AF.Identity, bias=nmr[:, s, j:j + 1],
                                                               scale=sdv[:, s, j:j + 1])),
                      reads=[vg_b[:, j, :], nmr[:, s, :], sdv[:, s, :]], writes=[vn_b[:, s * 4 + j, :]])
            if s == 0:
                val_conv(0)
        val_part(1)
        val_conv(1)
        ring_unstick()

        g_u = ring_acquire()

        def u_part(s):
            tsl = tsl_of(s)

            def ev(mm, b):
                S.add("act", (lambda e: e.activation(out=u_b[:, mm, tsl], in_=b, func=AF.Gelu_apprx_tanh)),
                      reads=[b], writes=[u_b[:, mm, tsl]])
            wgroup(g_u, s, xn, ev)

        def spatial(s):
            tsl = tsl_of(s)
            for h in range(4):
                b = bank()

                def f(e, h=h, b=b):
                    for j in range(4):
                        ins = e.matmul(b[:, j * 128:(j + 1) * 128], lhsT=vn_b[:, s * 4 + j, h * 128:(h + 1) * 128],
                                       rhs=wTm[:, h, :], start=True, stop=True)
                    return ins
                S.add("pe", f, reads=[vn_b[:, s * 4:(s + 1) * 4, h * 128:(h + 1) * 128], wTm[:, h, :]], writes=[b])
                tmp = vg_b[:, h, :]
                S.add("dve", (lambda e, h=h, b=b, tmp=tmp: e.scalar_tensor_tensor(
                    out=tmp.rearrange("p (j t) -> p j t", j=4),
                    in0=b.rearrange("p (j t) -> p j t", j=4),
                    scalar=cvec[:, C_GLN + h:C_GLN + h + 1],
                    in1=E[:, h, :].unsqueeze(1).to_broadcast([128, 4, 128]),
                    op0=ALU.mult, op1=ALU.add)),
                    reads=[b, cvec, E[:, h, :]], writes=[tmp])
                S.add("dve", (lambda e, h=h, tmp=tmp: e.tensor_tensor(out=u_b[:, h, tsl], in0=tmp,
                                                                       in1=u_b[:, h, tsl], op=ALU.mult)),
                      reads=[tmp, u_b[:, h, tsl]], writes=[u_b[:, h, tsl]])

        def gate_b(g_gb, s):
            tsl = tsl_of(s)

            def ev(mm, b):
                S.add("dve", (lambda e: e.tensor_tensor(out=yb_b[:, mm, tsl], in0=b, in1=yb_b[:, mm, tsl],
                                                        op=ALU.mult)),
                      reads=[b, yb_b[:, mm, tsl]], writes=[yb_b[:, mm, tsl]])
            wgroup(g_gb, s, xn, ev)

        def nA_a(s):
            normA(u_b[:, :, tsl_of(s)], 4, T, sq[:, s, 0:4])

        def nA_b(s):
            normA(yb_b[:, :, tsl_of(s)], 4, T, sq[:, s, 4:8])

        def nB_a(s):
            normB(u_b[:, :, tsl_of(s)], yn_b[:, 0:4, tsl_of(s)], C_GA, 4, 512, T, sq[:, s, 0:4])

        def nB_b(s):
            normB(yb_b[:, :, tsl_of(s)], yn_b[:, 4:8, tsl_of(s)], C_GB, 4, 512, T, sq[:, s, 4:8])

        def after_wout(s):
            normA(xT[:, :, tsl_of(s)], 8, T, sq[:, s])

        def resid_ev(gi, tsl):
            def ev(mm, b):
                m = gi * 4 + mm
                S.add("dve", (lambda e: e.tensor_tensor(out=xT[:, m, tsl], in0=b, in1=xT[:, m, tsl], op=ALU.add)),
                      reads=[b, xT[:, m, tsl]], writes=[xT[:, m, tsl]])
            return ev

        u_part(0)
        spatial(0)
        u_part(1)
        nA_a(0)
        spatial(1)
        g_gb = ring_acquire()
        gate_b(g_gb, 0)
        nA_a(1)
        nA_b(0)
        nB_a(0)
        nB_a(1)
        gate_b(g_gb, 1)
        nB_b(0)
        nA_b(1)
        g0 = ring_acquire(stick=True)
        g1 = ring_acquire()
        wgroup_k((lambda k, mm, g=g0: g[:, k, mm * 128:(mm + 1) * 128]), 0, yn_b, resid_ev(0, tsl_of(0)))
        nB_b(1)
        wgroup(g1, 0, yn_b, resid_ev(1, tsl_of(0)))
        after_wout(0)
        wgroup(g0, 1, yn_b, resid_ev(0, tsl_of(1)))
        wgroup(g1, 1, yn_b, resid_ev(1, tsl_of(1)))
        after_wout(1)
        ring_unstick()

        for s in range(NSUB):
            tsl = tsl_of(s)
            normB(xT[:, :, tsl], xn[:, :, tsl], C_ATT, 8, D, T, sq[:, s])
        for gi in range(2):
            g = ring_acquire()
            for s in range(NSUB):
                tsl = tsl_of(s)

                def ev(mm, b, gi=gi, tsl=tsl):
                    m = gi * 4 + mm
                    S.add("act", (lambda e: e.activation(out=q_b[:, m, tsl], in_=b, func=AF.Copy, scale=0.0625)),
                          reads=[b], writes=[q_b[:, m, tsl]])
                if gi == 0:
                    wgroup_k((lambda k, mm, g=g: g[:, k, mm * 128:(mm + 1) * 128]), s, xn, ev)
                else:
                    wgroup(g, s, xn, ev)
        for s in range(NSUB):
            tsl = tsl_of(s)
            for h in range(4):
                for mc in range(2):
                    b = bank()

                    def f(e, h=h, mc=mc, b=b, tsl=tsl):
                        for half in range(2):
                            ins = e.matmul(b, lhsT=KT[:, h * 2 + half, mc * 128:(mc + 1) * 128],
                                           rhs=q_b[:, h * 2 + half, tsl], start=(half == 0), stop=(half == 1))
                        return ins
                    S.add("pe", f, reads=[KT[:, h * 2:h * 2 + 2, :], q_b[:, h * 2:h * 2 + 2, tsl]], writes=[b])
                    S.add("act", (lambda e, h=h, mc=mc, b=b, s=s: e.activation(out=p_b[:, s, h * 2 + mc, :], in_=b,
                                                                               func=AF.Exp)),
                          reads=[b], writes=[p_b[:, s, h * 2 + mc, :]])
        for s in range(NSUB):
            tsl = tsl_of(s)
            for h in range(4):
                b = bank()

                def f(e, h=h, b=b, s=s):
                    for mc in range(2):
                        ins = e.matmul(b, lhsT=ones, rhs=p_b[:, s, h * 2 + mc, :], start=(mc == 0), stop=(mc == 1))
                    return ins
                S.add("pe", f, reads=[ones, p_b[:, s, h * 2:h * 2 + 2, :]], writes=[b])
                S.add("act", (lambda e, h=h, b=b, s=s: e.activation(out=rden_b[:, s, h, :], in_=b, func=AF.Ln)),
                      reads=[b], writes=[rden_b[:, s, h, :]])
                S.add("act", (lambda e, h=h, s=s: e.activation(out=rden_b[:, s, h, :], in_=rden_b[:, s, h, :],
                                                               func=AF.Exp, scale=-1.0)),
                      reads=[rden_b[:, s, h, :]], writes=[rden_b[:, s, h, :]])
            for c in range(8):
                h = c // 2
                b = bank()

                def f(e, c=c, h=h, b=b, s=s):
                    for mc in range(2):
                        ins = e.matmul(b, lhsT=Vt[:, mc, c * 128:(c + 1) * 128], rhs=p_b[:, s, h * 2 + mc, :],
                                       start=(mc == 0), stop=(mc == 1))
                    return ins
                S.add("pe", f, reads=[Vt[:, :, c * 128:(c + 1) * 128], p_b[:, s, h * 2:h * 2 + 2, :]], writes=[b])
                S.add("dve", (lambda e, c=c, h=h, b=b, tsl=tsl, s=s: e.tensor_tensor(
                    out=o_b[:, c, tsl], in0=b, in1=rden_b[:, s, h, :], op=ALU.mult)),
                    reads=[b, rden_b[:, s, h, :]], writes=[o_b[:, c, tsl]])
        proj_residual(o_b, after_wout)
        if st + 1 < NST:
            x_load(st + 1, 0)
            x_load(st + 1, 1)

        sg_ctr = [0]
        for s in range(NSUB):
            tsl = tsl_of(s)
            normB(xT[:, :, tsl], xn[:, :, tsl], C_FFN, 8, D, T, sq[:, s])

        def swiglu(bg, bu, j, tsl):
            sg = sg_b[:, sg_ctr[0] % 2, :]
            sg_ctr[0] += 1
            S.add("act", (lambda e: e.activation(out=sg, in_=bg, func=AF.Silu)), reads=[bg], writes=[sg])
            S.add("dve", (lambda e: e.tensor_tensor(out=h_b[:, j, tsl], in0=bu, in1=sg, op=ALU.mult)),
                  reads=[bu, sg], writes=[h_b[:, j, tsl]])

        for gi in range(11):
            g = ring_acquire()
            for s in range(NSUB):
                tsl = tsl_of(s)
                if gi == 0:
                    got = {}

                    def ev(mm, b, got=got, tsl=tsl, gi=gi):
                        got[mm] = b
                        if mm % 2 == 1:
                            swiglu(got[mm - 1], b, gi * 2 + mm // 2, tsl)
                    wgroup_k((lambda k, mm, g=g: g[:, mm % 2, k, (mm // 2) * 128:(mm // 2 + 1) * 128]), s, xn, ev)
                    continue
                for i in range(2):
                    bg = bank()
                    S.add("pe", mm_w(bg, (lambda k, g=g, i=i: g[:, 0, k, i * 128:(i + 1) * 128]),
                                     (lambda k, tsl=tsl: xn[:, k, tsl]), 8),
                          reads=[g, xn[:, :, tsl]], writes=[bg])
                    bu = bank()
                    S.add("pe", mm_w(bu, (lambda k, g=g, i=i: g[:, 1, k, i * 128:(i + 1) * 128]),
                                     (lambda k, tsl=tsl: xn[:, k, tsl]), 8),
                          reads=[g, xn[:, :, tsl]], writes=[bu])
                    swiglu(bg, bu, gi * 2 + i, tsl)
        for m in range(8):
            g = ring_acquire()
            for s in range(NSUB):
                tsl = tsl_of(s)
                b = bank()
                S.add("pe", mm_w(b, (lambda k, g=g: g[:, k, :]), (lambda k, tsl=tsl: h_b[:, k, tsl]), KF),
                      reads=[g, h_b[:, :, tsl]], writes=[b])
                S.add("dve", (lambda e, m=m, b=b, tsl=tsl: e.tensor_tensor(
                    out=xT[:, m, tsl], in0=b, in1=xT[:, m, tsl], op=ALU.add)),
                    reads=[b, xT[:, m, tsl]], writes=[xT[:, m, tsl]])


        for s in range(NSUB):
            normA(xT[:, :, tsl_of(s)], 8, T, sq[:, s])
        last_out = []
        normB(xT[:, :, tsl_of(0)], outT_b[:, 0], C_FIN, 8, D, T, sq[:, 0])
        if st + 1 < NST:
            xpose(0)
        normB(xT[:, :, tsl_of(1)], outT_b[:, 1], C_FIN, 8, D, T, sq[:, 1])
        for s in range(NSUB):
            t0 = st * TS + s * T
            if s == 1 and st + 1 < NST:
                xpose(1)
            ydst = yst_b if s == 0 else yst2_b
            for j in range(4):
                for half in range(2):
                    b = bank()

                    def f(e, j=j, half=half, b=b, s=s):
                        for cc in range(4):
                            c = half * 4 + cc
                            ins = e.transpose(b[:, cc * 128:(cc + 1) * 128], outT_b[:, s, c, j * 128:(j + 1) * 128],
                                              ident)
                        return ins
                    S.add("pe", f, reads=[outT_b[:, s, half * 4:half * 4 + 4, j * 128:(j + 1) * 128], ident],
                          writes=[b])
                    dsts = ydst[:, j, half * 512:(half + 1) * 512]
                    if (j + half) % 2 == 0:
                        S.add("act", (lambda e, dsts=dsts, b=b: e.copy(out=dsts, in_=b)), reads=[b], writes=[dsts])
                    else:
                        S.add("dve", (lambda e, dsts=dsts, b=b: e.tensor_copy(out=dsts, in_=b)), reads=[b], writes=[dsts])
                od = S.add("sp", (lambda e, t0=t0, j=j, ydst=ydst: e.dma_start(
                    out=y_d[t0 + j * 128:t0 + (j + 1) * 128, :], in_=ydst[:, j, :])),
                    reads=[ydst[:, j, :]], dma_key=("yout", s, j))
                last_out.append(od)

    S.add("sp", None, extra_deps=last_out)
    assert ring_state["next_acq"] == len(groups), (ring_state, len(groups))

    S.finalize()
    sems = {e: nc.alloc_semaphore("sem_" + e) for e in Sched.CENG}
    dsems = {}
    for key in S.dma_cnt:
        dsems[key] = nc.alloc_semaphore("dsem_" + "_".join(str(k) for k in (key if isinstance(key, tuple) else (key,))))

    with nc.Block() as block:
        @block.tensor
        def _(e):
            S.emit("pe", e, sems, dsems)

        @block.scalar
        def _(e):
            S.emit("act", e, sems, dsems)

        @block.vector
        def _(e):
            S.emit("dve", e, sems, dsems)

        @block.gpsimd
        def _(e):
            S.emit("pool", e, sems, dsems)

        @block.sync
        def _(e):
            S.emit("sp", e, sems, dsems)
    return nc


def _host_layout(inputs):
    f = lambda a: np.ascontiguousarray(np.asarray(a, dtype=np.float32))
    col = lambda v, n: f(v).reshape(n, 128).T
    cw = f(inputs["conv_w"])
    cvec = np.concatenate([
        col(inputs["ln_mix_g"], 8), col(inputs["ln_attn_g"], 8), col(inputs["ln_ffn_g"], 8),
        col(inputs["ln_final_g"], 8), col(inputs["ln_mem_g"], 8),
        col(inputs["grp_norm_a"], 4), col(inputs["grp_norm_b"], 4),
        col(inputs["sgu_ln_g"], 4), col(inputs["sgu_ln_b"], 4),
        col(cw[0], 4), col(cw[1], 4), col(cw[2], 4),
    ], axis=1)
    assert cvec.shape == (128, NCV)
    wspT = f(np.transpose(f(inputs["w_spatial"]), (2, 0, 1)))
    bspb = f(np.broadcast_to(f(inputs["b_spatial"])[None, :, :], (128, 4, 128)))
    shared = {
        "w_in": f(inputs["w_in"]), "w_out": f(inputs["w_out"]), "w_q": f(inputs["w_q"]),
        "w_kv": f(inputs["w_kv"]), "w_o": f(inputs["w_o"]), "w_gate_up": f(inputs["w_gate_up"]),
        "w_down": f(inputs["w_down"]), "cvec": f(cvec), "wspT": wspT, "bspb": bspb,
    }
    x = f(inputs["x"])
    mem = f(inputs["mem"])
    in_maps = []
    for b in range(8):
        d = dict(shared)
        d["x"] = x[b]
        d["mem"] = mem[b]
        in_maps.append(d)
    return in_maps


def kernel(**inputs):
    in_maps = _host_layout(inputs)
    nc = build_program()
    res = run_bass_kernel_spmd(nc, in_maps, core_ids=list(range(8)))
    out = np.stack([np.asarray(r["y"], dtype=np.float32) for r in res.results], axis=0)
    return out
```
